# Optimizing a Trainium2 kernel written in Bass

```python
import math
import jax, jax.numpy as jnp
from jax import lax
import numpy as np

D_MODEL = 1024
BATCH = 16
SEQ = 2048
DEPTH = 1
DEC_BATCH = 32
DEC_SEQ = 32
PAST_LEN = 1024

CHUNK = 64
SSM_WIDTH = 512
SSM_GROUP = 16
SSM_GROUPS = SSM_WIDTH // SSM_GROUP
SSM_STATE = 64
DT_MIN = 1e-3
DT_MAX = 1e-1
MLA_HEADS = 8
NOPE_DIM = 64
ROPE_DIM = 32
V_DIM = 64
MLA_WIDTH = MLA_HEADS * V_DIM
Q_LORA = 256
KV_LORA = 128
MIX_WIDTH = SSM_WIDTH + MLA_WIDTH
IN_WIDTH = 2 * SSM_WIDTH + Q_LORA + KV_LORA + ROPE_DIM + MLA_WIDTH
ROPE_THETA = 10000.0
Q_BLOCK = 128
EPS = 1e-6

kernel_name = 'hymba_s5_mla_streaming_step'


def rmsnorm(x, g):
    xf = x.astype(jnp.float32)
    y = xf * lax.rsqrt(jnp.mean(xf * xf, axis=-1, keepdims=True) + EPS)
    return (y * g.astype(jnp.float32)).astype(x.dtype)


def rope(x, pos):
    half = ROPE_DIM // 2
    inv = ROPE_THETA ** (-jnp.arange(half, dtype=jnp.float32) / half)
    ang = pos.astype(jnp.float32)[:, None] * inv[None, :]
    ang = ang.reshape(ang.shape[0], *([1] * (x.ndim - 3)), half)
    cos, sin = jnp.cos(ang), jnp.sin(ang)
    xf = x.astype(jnp.float32)
    x1, x2 = xf[..., :half], xf[..., half:]
    return jnp.concatenate([x1 * cos - x2 * sin, x1 * sin + x2 * cos], axis=-1).astype(x.dtype)


def _ssm_combine(left, right):
    a_l, b_l = left
    a_r, b_r = right
    return a_r * a_l, a_r * b_l + b_r


def s5_branch(u, h0, p):
    b, s, _ = u.shape
    lam = lax.complex(p['ssm_a_re'].astype(jnp.float32), p['ssm_a_im'].astype(jnp.float32))
    dt = jnp.exp(p['ssm_log_dt'].astype(jnp.float32))[:, None]
    a_bar = jnp.exp(lam * dt)
    b_mat = lax.complex(p['ssm_b_re'].astype(jnp.float32), p['ssm_b_im'].astype(jnp.float32))
    b_bar = ((a_bar - 1.0) / lam)[..., None] * b_mat
    c_mat = lax.complex(p['ssm_c_re'].astype(jnp.float32), p['ssm_c_im'].astype(jnp.float32))
    ug = u.astype(jnp.float32).reshape(b, s, SSM_GROUPS, SSM_GROUP)
    bu = jnp.einsum('gpc,bsgc->bsgp', b_bar, ug.astype(jnp.complex64))
    if h0 is not None:
        bu = bu.at[:, 0].add(a_bar * h0)
    a_seq = jnp.broadcast_to(a_bar, bu.shape)
    _, states = lax.associative_scan(_ssm_combine, (a_seq, bu), axis=1)
    y = jnp.einsum('gcp,bsgp->bsgc', c_mat, states).real
    y = y + p['ssm_d'].astype(jnp.float32).reshape(SSM_GROUPS, SSM_GROUP) * ug
    y = y.reshape(b, s, SSM_WIDTH)
    yg = jax.nn.gelu(y)
    out = yg * jax.nn.sigmoid(yg @ p['w_glu'].astype(jnp.float32) + p['b_glu'].astype(jnp.float32))
    return out.astype(u.dtype), states[:, -1]


def mla_expand(ckv, w_ukv, k_nope_norm):
    b, t, _ = ckv.shape
    kv = (ckv @ w_ukv).reshape(b, t, MLA_HEADS, NOPE_DIM + V_DIM)
    return rmsnorm(kv[..., :NOPE_DIM], k_nope_norm), kv[..., NOPE_DIM:]


def mla_attend(q_nope, q_rope, k_nope, k_rope, v, mask):
    s = jnp.einsum('bqhn,bkhn->bhqk', q_nope, k_nope) + jnp.einsum('bqhr,bkr->bhqk', q_rope, k_rope)
    s = s.astype(jnp.float32) * (NOPE_DIM + ROPE_DIM) ** -0.5
    if mask is not None:
        s = jnp.where(mask, s, -jnp.inf)
    pr = jax.nn.softmax(s, axis=-1)
    return jnp.einsum('bhqk,bkhv->bqhv', pr.astype(v.dtype), v)


def prompt_attention(q_nope, q_rope, k_nope, k_rope, v):
    b, s = q_nope.shape[:2]
    nb = s // Q_BLOCK

    def blockify(t):
        return jnp.moveaxis(t.reshape(b, nb, Q_BLOCK, *t.shape[2:]), 1, 0)

    k_chunk = jnp.arange(s) // CHUNK
    q_chunk = (jnp.arange(s) // CHUNK).reshape(nb, Q_BLOCK)

    def one_block(args):
        qn, qr, qc = args
        mask = qc[:, None] >= k_chunk[None, :]
        return mla_attend(qn, qr, k_nope, k_rope, v, mask)

    o = lax.map(one_block, (blockify(q_nope), blockify(q_rope), q_chunk))
    return jnp.moveaxis(o, 0, 1).reshape(b, s, MLA_WIDTH)


def mixer_layer(x, pos, h0, past_ckv, past_krope, p):
    b, s, _ = x.shape
    h = rmsnorm(x, p['norm_in'])
    z = h @ p['w_in']
    cuts = np.cumsum([SSM_WIDTH, SSM_WIDTH, Q_LORA, KV_LORA, ROPE_DIM]).tolist()
    u, g_ssm, c_q, c_kv, k_rope_raw, g_mla = jnp.split(z, cuts, axis=-1)
    y_ssm, h_last = s5_branch(u, h0, p)
    q = (rmsnorm(c_q, p['q_lora_norm']) @ p['w_uq']).reshape(b, s, MLA_HEADS, NOPE_DIM + ROPE_DIM)
    q_nope = rmsnorm(q[..., :NOPE_DIM], p['q_nope_norm'])
    q_rope = rope(rmsnorm(q[..., NOPE_DIM:], p['q_rope_norm']), pos)
    ckv = rmsnorm(c_kv, p['kv_lora_norm'])
    krope = rope(rmsnorm(k_rope_raw, p['k_rope_norm']), pos)
    if past_ckv is None:
        k_nope, v = mla_expand(ckv, p['w_ukv'], p['k_nope_norm'])
        attn = prompt_attention(q_nope, q_rope, k_nope, krope, v)
    else:
        ckv_all = jnp.concatenate([past_ckv, ckv], axis=1)
        krope_all = jnp.concatenate([past_krope, krope], axis=1)
        k_nope, v = mla_expand(ckv_all, p['w_ukv'], p['k_nope_norm'])
        attn = mla_attend(q_nope, q_rope, k_nope, krope_all, v, None).reshape(b, s, MLA_WIDTH)
    mix = jnp.concatenate([rmsnorm(y_ssm, p['out_norm_ssm']) * jax.nn.silu(g_ssm),
                           rmsnorm(attn, p['out_norm_mla']) * jax.nn.silu(g_mla)], axis=-1)
    return x + mix @ p['w_out'], ckv, krope, h_last


def setup_inputs(seed: int = 0) -> dict:
    key = jax.random.key(seed)
    ks = jax.random.split(key, 32)
    f32 = jnp.float32

    def nrm(k, shape, scale):
        return jax.random.normal(k, shape, f32) * scale

    def gain(k, shape):
        return 1.0 + 0.05 * jax.random.normal(k, shape, f32)

    n = jnp.arange(SSM_STATE, dtype=f32)
    return {
        'x_prompt': nrm(ks[0], (BATCH, SEQ, D_MODEL), 1.0),
        'x_sample': nrm(ks[1], (DEC_BATCH, DEC_SEQ, D_MODEL), 1.0),
        'cache_ckv': nrm(ks[2], (DEPTH, DEC_BATCH, PAST_LEN, KV_LORA), 1.0),
        'cache_krope': nrm(ks[3], (DEPTH, DEC_BATCH, PAST_LEN, ROPE_DIM), 1.0),
        'state_ssm_re': nrm(ks[4], (DEPTH, DEC_BATCH, SSM_GROUPS, SSM_STATE), 0.1),
        'state_ssm_im': nrm(ks[5], (DEPTH, DEC_BATCH, SSM_GROUPS, SSM_STATE), 0.1),
        'norm_in': gain(ks[6], (DEPTH, D_MODEL)),
        'w_in': nrm(ks[7], (DEPTH, D_MODEL, IN_WIDTH), D_MODEL ** -0.5),
        'ssm_a_re': -0.5 + nrm(ks[8], (DEPTH, SSM_GROUPS, SSM_STATE), 0.01),
        'ssm_a_im': jnp.pi * n + nrm(ks[9], (DEPTH, SSM_GROUPS, SSM_STATE), 0.01),
        'ssm_log_dt': jax.random.uniform(ks[10], (DEPTH, SSM_GROUPS), f32, math.log(DT_MIN), math.log(DT_MAX)),
        'ssm_b_re': nrm(ks[11], (DEPTH, SSM_GROUPS, SSM_STATE, SSM_GROUP), (2 * SSM_GROUP) ** -0.5),
        'ssm_b_im': nrm(ks[12], (DEPTH, SSM_GROUPS, SSM_STATE, SSM_GROUP), (2 * SSM_GROUP) ** -0.5),
        'ssm_c_re': nrm(ks[13], (DEPTH, SSM_GROUPS, SSM_GROUP, SSM_STATE), (2 * SSM_STATE) ** -0.5),
        'ssm_c_im': nrm(ks[14], (DEPTH, SSM_GROUPS, SSM_GROUP, SSM_STATE), (2 * SSM_STATE) ** -0.5),
        'ssm_d': nrm(ks[15], (DEPTH, SSM_WIDTH), 1.0),
        'w_glu': nrm(ks[16], (DEPTH, SSM_WIDTH, SSM_WIDTH), SSM_WIDTH ** -0.5),
        'b_glu': nrm(ks[17], (DEPTH, SSM_WIDTH), 0.01),
        'q_lora_norm': gain(ks[18], (DEPTH, Q_LORA)),
        'kv_lora_norm': gain(ks[19], (DEPTH, KV_LORA)),
        'w_uq': nrm(ks[20], (DEPTH, Q_LORA, MLA_HEADS * (NOPE_DIM + ROPE_DIM)), Q_LORA ** -0.5),
        'w_ukv': nrm(ks[21], (DEPTH, KV_LORA, MLA_HEADS * (NOPE_DIM + V_DIM)), KV_LORA ** -0.5),
        'q_nope_norm': gain(ks[22], (DEPTH, NOPE_DIM)),
        'k_nope_norm': gain(ks[23], (DEPTH, NOPE_DIM)),
        'q_rope_norm': gain(ks[24], (DEPTH, ROPE_DIM)),
        'k_rope_norm': gain(ks[25], (DEPTH, ROPE_DIM)),
        'out_norm_ssm': gain(ks[26], (DEPTH, SSM_WIDTH)),
        'out_norm_mla': gain(ks[27], (DEPTH, MLA_WIDTH)),
        'w_out': nrm(ks[28], (DEPTH, MIX_WIDTH, D_MODEL), MIX_WIDTH ** -0.5),
    }


def reference(x_prompt, x_sample, cache_ckv, cache_krope, state_ssm_re, state_ssm_im,
              norm_in, w_in, ssm_a_re, ssm_a_im, ssm_log_dt, ssm_b_re, ssm_b_im, ssm_c_re, ssm_c_im,
              ssm_d, w_glu, b_glu, q_lora_norm, kv_lora_norm, w_uq, w_ukv,
              q_nope_norm, k_nope_norm, q_rope_norm, k_rope_norm, out_norm_ssm, out_norm_mla, w_out):
    pos_p = jnp.arange(x_prompt.shape[1])
    pos_s = PAST_LEN + jnp.arange(x_sample.shape[1])
    yp, ys = x_prompt, x_sample
    ckv_p, kr_p, re_p, im_p = [], [], [], []
    ckv_s, kr_s, re_s, im_s = [], [], [], []
    for layer in range(DEPTH):
        p = dict(norm_in=norm_in[layer], w_in=w_in[layer], ssm_a_re=ssm_a_re[layer], ssm_a_im=ssm_a_im[layer],
                 ssm_log_dt=ssm_log_dt[layer], ssm_b_re=ssm_b_re[layer], ssm_b_im=ssm_b_im[layer],
                 ssm_c_re=ssm_c_re[layer], ssm_c_im=ssm_c_im[layer], ssm_d=ssm_d[layer], w_glu=w_glu[layer],
                 b_glu=b_glu[layer], q_lora_norm=q_lora_norm[layer], kv_lora_norm=kv_lora_norm[layer],
                 w_uq=w_uq[layer], w_ukv=w_ukv[layer], q_nope_norm=q_nope_norm[layer],
                 k_nope_norm=k_nope_norm[layer], q_rope_norm=q_rope_norm[layer], k_rope_norm=k_rope_norm[layer],
                 out_norm_ssm=out_norm_ssm[layer], out_norm_mla=out_norm_mla[layer], w_out=w_out[layer])
        yp, ckv1, kr1, h1 = mixer_layer(yp, pos_p, None, None, None, p)
        ckv_p.append(ckv1)
        kr_p.append(kr1)
        re_p.append(jnp.real(h1).astype(yp.dtype))
        im_p.append(jnp.imag(h1).astype(yp.dtype))
        h0 = lax.complex(state_ssm_re[layer].astype(jnp.float32), state_ssm_im[layer].astype(jnp.float32))
        ys, ckv2, kr2, h2 = mixer_layer(ys, pos_s, h0, cache_ckv[layer], cache_krope[layer], p)
        ckv_s.append(ckv2)
        kr_s.append(kr2)
        re_s.append(jnp.real(h2).astype(ys.dtype))
        im_s.append(jnp.imag(h2).astype(ys.dtype))
    return (yp, ys,
            jnp.stack(ckv_p), jnp.stack(kr_p), jnp.stack(re_p), jnp.stack(im_p),
            jnp.stack(ckv_s), jnp.stack(kr_s), jnp.stack(re_s), jnp.stack(im_s))
```

```python
import numpy as np
from contextlib import ExitStack
import concourse.bass as bass
import concourse.mybir as mybir
from concourse.bass_utils import run_bass_kernel_spmd

F32 = mybir.dt.float32
BF16 = mybir.dt.bfloat16
AF = mybir.ActivationFunctionType
ALU = mybir.AluOpType

NCORES = 8
D = 1024
SEQ = 2048
NT = 128
L = 8
EPS = 1e-6
PI = float(np.pi)
ENGS = ("pe", "act", "dve", "pool", "sp")


class Buf:
    __slots__ = ("name", "last_w", "readers", "sem", "semcnt")

    def __init__(self, name):
        self.name = name
        self.last_w = None
        self.readers = []
        self.sem = None
        self.semcnt = 0


class Prog:
    def __init__(self, nc, stack):
        self.nc = nc
        self.stack = stack
        self.eng = {"pe": nc.tensor, "act": nc.scalar, "dve": nc.vector, "pool": nc.gpsimd, "sp": nc.sync}
        self.sem = {e: stack.enter_context(nc.semaphore("s_" + e)) for e in ENGS}
        self.cnt = {e: 0 for e in ENGS}
        self.known = {e: {} for e in ENGS}
        self.vc = {e: [] for e in ENGS}
        self.pending = {e: [] for e in ENGS}
        self.out_events = []
        self.nsem = 0
        self.recording = False
        self.cur = None
        self.nodes = []
        self.dma_latest = {}
        self.sched_slack = 0.0
        self.sched_tabpen = 0.0
        self.sched_xlat = 0.0

    def _need(self, e, dep):
        key, n = dep[0], dep[1]
        if n is None:
            raise RuntimeError("dependency on un-incremented instruction")
        if self.known[e].get(key, 0) >= n:
            return
        if isinstance(key, str):
            self.eng[e].wait_ge(self.sem[key], n)
            snap = self.vc[key][n - 1]
            for g, v in snap.items():
                if v > self.known[e].get(g, 0):
                    self.known[e][g] = v
        else:
            self.eng[e].wait_ge(key, n)
        self.known[e][key] = n

    def _deps(self, e, reads, writes):
        for b in reads:
            if b.last_w is not None:
                self._need(e, b.last_w)
        for b in writes:
            if b.last_w is not None and (b.last_w[0] != e or e != "pe"):
                self._need(e, b.last_w)
            for r in b.readers:
                if r[0] != e or e != "pe":
                    self._need(e, r)

    def start_recording(self):
        self.recording = True
        self.nodes = []
        self.cur = None
        self.rw = {}

    def grp(self, e):
        prog = self

        class _G:
            def __enter__(self_):
                prog.cur = prog._new_node(e)
                return self_

            def __exit__(self_, *a):
                nd = prog.cur
                prog.cur = None
                if nd["ops"]:
                    for k in range(len(nd["ops"]) - 1, -1, -1):
                        if nd["ops"][k][0] == "op":
                            nd["ops"][k][5] = True
                            break
                    prog.nodes.append(nd)
                return False
        return _G()

    def _new_node(self, e):
        return {"e": e, "ops": [], "deps": set(), "cost": 0.0, "lat": 0.0}

    def _rec(self, kind, e, payload, reads, writes, inc, cost, lat=None, tab=None):
        if self.cur is not None:
            nd = self.cur
            assert nd["e"] == e, (nd["e"], e)
        else:
            nd = self._new_node(e)
        me = id(nd)
        for b in reads:
            st_ = self.rw.setdefault(id(b), [None, []])
            if st_[0] is not None and st_[0] is not nd:
                nd["deps"].add(st_[0]["i"] if "i" in st_[0] else None)
        for b in writes:
            st_ = self.rw.setdefault(id(b), [None, []])
            if st_[0] is not None and st_[0] is not nd:
                nd["deps"].add(st_[0].get("i"))
            for r in st_[1]:
                if r is not nd:
                    nd["deps"].add(r.get("i"))
        nd["ops"].append([kind, payload, list(reads), list(writes), None, inc])
        if tab is not None:
            nd["tab"] = tab
        nd["cost"] += cost
        nd["lat"] = max(nd["lat"], lat if lat is not None else 0.0)
        if self.cur is None:
            if kind == "op":
                nd["ops"][-1][5] = True
            nd["i"] = len(self.nodes)
            self.nodes.append(nd)
        else:
            if "i" not in nd:
                nd["i"] = len(self.nodes)
        for b in reads:
            self.rw[id(b)][1].append(nd)
        for b in writes:
            self.rw[id(b)][0] = nd
            self.rw[id(b)][1] = []

    def schedule_and_emit(self, window=48):
        self.recording = False
        nodes = self.nodes
        n = len(nodes)
        for i, nd in enumerate(nodes):
            assert nd["i"] == i, (nd["i"], i)
            nd["deps"].discard(None)
            nd["deps"].discard(i)
        succ = [[] for _ in range(n)]
        left = [0] * n
        for i, nd in enumerate(nodes):
            left[i] = len(nd["deps"])
            for d in nd["deps"]:
                assert d < i
                succ[d].append(i)
        rt = [0.0] * n
        bl = [0.0] * n
        for i in range(n - 1, -1, -1):
            m_ = 0.0
            for j in succ[i]:
                if bl[j] > m_:
                    m_ = bl[j]
            bl[i] = nodes[i]["cost"] + nodes[i]["lat"] + m_
        SLACK = self.sched_slack
        TABPEN = self.sched_tabpen
        cur_tab = [None]
        queues = {e: [i for i in range(n) if nodes[i]["e"] == e] for e in ENGS}
        qpos = {e: 0 for e in ENGS}
        done = [False] * n
        tfree = {e: 0.0 for e in ENGS}
        order = []
        remaining = n
        while remaining:
            best = None
            for e in ENGS:
                q = queues[e]
                p = qpos[e]
                while p < len(q) and done[q[p]]:
                    p += 1
                qpos[e] = p
                cnt = 0
                k = p
                cands_e = []
                while k < len(q) and cnt < window:
                    i = q[k]
                    k += 1
                    if done[i]:
                        continue
                    cnt += 1
                    if left[i] == 0:
                        stt_ = max(tfree[e], rt[i])
                        pen = 0.0
                        if e == "act" and TABPEN > 0:
                            tb_ = nodes[i].get("tab")
                            if tb_ is not None and tb_ != cur_tab[0]:
                                pen = TABPEN
                        cands_e.append((stt_ + pen, i, pen))
                if cands_e:
                    m0 = min(c_[0] for c_ in cands_e)
                    pick = None
                    for c_ in cands_e:
                        if c_[0] <= m0 + SLACK:
                            if pick is None or bl[c_[1]] > bl[pick[1]] + 1e-9 or (abs(bl[c_[1]] - bl[pick[1]]) <= 1e-9 and c_[1] < pick[1]):
                                pick = c_
                    if best is None or pick[0] < best[0] - 1e-9 or (abs(pick[0] - best[0]) <= 1e-9 and pick[1] < best[1]):
                        best = (pick[0], pick[1], e, pick[2])
            assert best is not None, "scheduler deadlock"
            stt_, i, e, pen = best
            nd = nodes[i]
            done[i] = True
            remaining -= 1
            order.append(i)
            if e == "act" and nd.get("tab") is not None:
                cur_tab[0] = nd["tab"]
            tfree[e] = stt_ + nd["cost"]
            fin = stt_ + nd["cost"] + nd["lat"]
            for j in succ[i]:
                left[j] -= 1
                f_ = fin + (self.sched_xlat if nodes[j]["e"] != e else 0.0)
                if f_ > rt[j]:
                    rt[j] = f_
        self.est_makespan = max(tfree.values())
        for i in order:
            nd = nodes[i]
            for kind, payload, reads, writes, _, inc in nd["ops"]:
                if kind == "op":
                    self._emit_op(nd["e"], payload, reads, writes, inc)
                elif kind == "dma":
                    self._emit_dma(*payload)
                elif kind == "selfwait":
                    self._need(nd["e"], (nd["e"], self.cnt[nd["e"]]))
        self.nodes = []

    def selfwait(self, e):
        if getattr(self, "recording", False):
            self._rec("selfwait", e, None, [], [], False, 0.2)
        else:
            self._need(e, (e, self.cnt[e]))

    def op(self, e, fn, reads=(), writes=(), inc=True, cost=0.3, tab=None):
        if getattr(self, "recording", False):
            self._rec("op", e, fn, reads, writes, inc, cost, tab=tab)
            return None
        return self._emit_op(e, fn, reads, writes, inc)

    def _emit_op(self, e, fn, reads=(), writes=(), inc=True):
        self._deps(e, reads, writes)
        ins = fn(self.eng[e])
        if inc:
            self.cnt[e] += 1
            n = self.cnt[e]
            ins.then_inc(self.sem[e], 1)
            snap = dict(self.known[e])
            snap[e] = n
            self.vc[e].append(snap)
            me = (e, n)
            for rec in self.pending[e]:
                rec[1] = n
            self.pending[e] = []
        else:
            me = [e, None]
            self.pending[e].append(me)
        for b in reads:
            b.readers.append(me)
        for b in writes:
            b.last_w = me
            b.readers = []
        return ins

    def dma(self, out, in_, reads=(), writes=(), is_output=False, q="sp", nbytes=65536, **kw):
        if getattr(self, "recording", False):
            self._rec("dma", q, (out, in_, list(reads), list(writes), is_output, q, kw), reads, writes, False, 0.15, lat=6.0 + nbytes / 60e3)
            return None
        return self._emit_dma(out, in_, reads, writes, is_output, q, kw)

    def _emit_dma(self, out, in_, reads, writes, is_output, q, kw):
        self._deps(q, reads, writes)
        owner = writes[0] if (writes and not is_output) else reads[0]
        if owner.sem is None:
            owner.sem = self.stack.enter_context(self.nc.semaphore("d%d" % self.nsem))
            self.nsem += 1
        owner.semcnt += 16
        ins = self.eng[q].dma_start(out=out, in_=in_, **kw)
        ins.then_inc(owner.sem, 16)
        me = (owner.sem, owner.semcnt)
        self.dma_latest[owner.sem] = owner.semcnt
        for b in reads:
            b.readers.append(me)
        for b in writes:
            b.last_w = me
            b.readers = []
        if is_output:
            self.out_events.append(me)
        return ins

    def barrier(self):
        for e in ENGS:
            for sem_, c_ in self.dma_latest.items():
                self._need(e, (sem_, c_))
        for e in ENGS:
            for f in ENGS:
                if f != e and self.cnt[f] > 0:
                    self._need(e, (f, self.cnt[f]))

    def finish(self):
        for ev in self.out_events:
            self._need("sp", ev)


class T:
    __slots__ = ("t", "b")

    def __init__(self, t, name):
        self.t = t
        self.b = Buf(name)


def build_program(flags):
    nc = bass.Bass("TRN2", target_bir_lowering=False)
    dr = {}

    def din(name, shape, dt=F32):
        dr[name] = nc.dram_tensor(name, list(shape), dt, kind="ExternalInput").ap()
        return dr[name]

    def dout(name, shape, dt=F32):
        dr[name] = nc.dram_tensor(name, list(shape), dt, kind="ExternalOutput").ap()
        return dr[name]

    xp = din("xp", [2 * SEQ, D]); xs = din("xs", [128, D])
    cckv = din("cckv", [4, 1024, 128]); ckr = din("ckr", [4, 1024, 32])
    h0r = din("h0r", [128, 4, 16]); h0i = din("h0i", [128, 4, 16])
    w_in = din("w_in", [128, 8, 1952]); nin = din("nin", [128, 8])
    w_uq = din("w_uq", [128, 2, 1024]); nq = din("nq", [128, 2])
    w_ukv = din("w_ukv", [128, 1024])
    w_glu = din("w_glu", [128, 4, 512]); b_glu = din("b_glu", [128, 4])
    w_out = din("w_out", [128, 8, 1024]); nout = din("nout", [128, 8])
    gcols = din("gcols", [128, 24])
    grows = din("grows", [128, 160])
    cm = din("cm", [128, 5, 128])
    sel = din("sel", [128, 64])
    a1 = din("a1", [128, 3, 512])
    bt1 = din("bt1", [128, 2, 512])
    a2 = din("a2", [128, 3, 16])
    b2 = din("b2", [128, 2, 16, 128])
    c2 = din("c2", [128, 2, 16, 32])
    dbg = dout("dbg", [128, 2048]) if flags.get("dbg") else None

    yp = dout("yp", [2 * SEQ, D]); ys = dout("ys", [128, D])
    ockv_p = dout("ockv_p", [2 * SEQ, 128]); okr_p = dout("okr_p", [2 * SEQ, 32])
    ost_p = dout("ost_p", [128, 2, 2, 16])
    ockv_s = dout("ockv_s", [128, 128]); okr_s = dout("okr_s", [128, 32])
    ost_s = dout("ost_s", [128, 4, 2, 16])

    with ExitStack() as st:
        P = Prog(nc, st)
        uid = [0]

        def sb(shape, dt=F32, name=None, stack=st):
            uid[0] += 1
            nm = (name or "t") + "_%d" % uid[0]
            return T(stack.enter_context(nc.sbuf_tensor(nm, list(shape), dt)), nm)

        def psum(shape, dt=F32, name=None):
            uid[0] += 1
            nm = (name or "ps") + "_%d" % uid[0]
            return T(st.enter_context(nc.psum_tensor(nm, list(shape), dt)), nm)

        psF = [psum([128, 512], F32, "psF") for _ in range(7)]
        psB = [psum([128, 1024], BF16, "psB") for _ in range(1)]
        rr = {"F": 0, "B": 0}

        stage = ["F"]
        rr["Bk"] = 0

        def nextF():
            if stage[0] == "F":
                rr["F"] = (rr["F"] + 1) % 2
                return psF[rr["F"]]
            if stage[0] == "S4":
                rr["S4"] = (rr.get("S4", 0) + 1) % 2
                return psF[(5, 6)[rr["S4"]]]
            rr["Bk"] = (rr["Bk"] + 1) % 2
            return psF[2 + rr["Bk"]]

        rr["O"] = 0

        def nextO():
            return psF[4]

        def nextB():
            return psB[0]

        def fsz(ap):
            n = 1
            for d_ in tuple(ap.shape)[1:]:
                n *= d_
            return n

        def ecost(e, ap, mul=1.0):
            n = fsz(ap)
            if e == "act":
                return 0.22 + n / 1200.0
            if e == "dve":
                return 0.08 + mul * n / 900.0
            if e == "pool":
                return 0.25 + n / 450.0
            return 0.3

        def tt(e, out, in0, in1, op, R, W):
            return P.op(e, lambda g: g.tensor_tensor(out=out, in0=in0, in1=in1, op=op), R, W, cost=ecost(e, out))

        def ts(e, out, in0, s1, s2, op0, op1, R, W):
            if s2 is None:
                return P.op(e, lambda g: g.tensor_scalar(out=out, in0=in0, scalar1=s1, scalar2=None, op0=op0), R, W, cost=ecost(e, out))
            return P.op(e, lambda g: g.tensor_scalar(out=out, in0=in0, scalar1=s1, scalar2=s2, op0=op0, op1=op1), R, W, cost=ecost(e, out))

        def stt(out, in0, scalar, in1, op0, op1, R, W):
            return P.op("dve", lambda g: g.scalar_tensor_tensor(out=out, in0=in0, scalar=scalar, in1=in1, op0=op0, op1=op1), R, W, cost=ecost("dve", out))

        def act(out, in_, func, R, W, **kw):
            tab = "T" if func == AF.Tanh else ("L" if func == AF.Ln else None)
            return P.op("act", lambda g: g.activation(out=out, in_=in_, func=func, **kw), R, W, cost=ecost("act", out), tab=tab)

        def cp(e, out, in_, R, W):
            if e == "act":
                return P.op("act", lambda g: g.copy(out=out, in_=in_), R, W, cost=ecost(e, out))
            return P.op(e, lambda g: g.tensor_copy(out=out, in_=in_), R, W, cost=ecost(e, out))

        def recip(out, in_, R, W):
            return P.op("dve", lambda g: g.reciprocal(out=out, in_=in_), R, W, cost=ecost("dve", out, 6.5))

        def rsqrt_pow(out, in_, R, W, scale=1.0, from_psum=True):
            np_ = int(tuple(out.shape)[0])
            act(out, in_, AF.Ln, list(R) + [epsc.b], W, scale=float(scale), bias=epsc.t[0:np_, 0:1])
            act(out, out, AF.Exp, W, W, scale=-0.5)

        def ppow(out, W):
            shp = [int(d_) for d_ in tuple(out.shape)]
            mh = mhalf.t[0:shp[0], 0:1].to_broadcast(shp)
            P.op("pool", lambda g: g.tensor_tensor(out=out, in0=out, in1=mh, op=ALU.pow), list(W) + [mhalf.b], W, cost=ecost("pool", out))

        def mset(ap, val, W, e="pool"):
            return P.op(e, lambda g: g.memset(ap, val), [], W, cost=ecost(e, ap))

        def mm(out, lhsT, rhs, start, stop, R, W, inc=None, **kw):
            if inc is None:
                inc = stop
            ncol = fsz(rhs)
            c_ = 0.035 + max(ncol, 64) * (4.0 if rhs.dtype == F32 else 1.0) / 1600.0
            return P.op("pe", lambda g: g.matmul(out, lhsT=lhsT, rhs=rhs, start=start, stop=stop, **kw), R, W, inc=inc, cost=c_)

        def tr(out, in_, ident, R, W, inc=True):
            return P.op("pe", lambda g: g.transpose(out=out, in_=in_, identity=ident), R, W, inc=inc, cost=0.12)

        ident_b = sb([128, 128], BF16, "ident"); blk64 = sb([128, 128], BF16, "blk64"); blk32 = sb([128, 128], BF16, "blk32")
        ones256 = sb([128, 128], BF16, "o256"); ones512 = sb([128, 128], BF16, "o512")
        ident_f = sb([128, 128], F32, "identf")
        sel_f = sb([128, 64], F32, "sel")
        epsc = sb([128, 1], F32, "eps")
        mhalf = sb([128, 1], F32, "mhalf")
        hbglu = sb([128, 4], F32, "hbglu")
        WinD = nc.dram_tensor("WinD", [128, 14, 1024], BF16).ap(); WinD_b = Buf("WinD")
        WinT = nc.dram_tensor("WinT", [128, 8 * 160], BF16).ap(); WinT_b = Buf("WinT")
        PIECE_COL0 = [0, 128, 256, 384, 512, 640, 768, 896, 1440, 1568, 1696, 1824, 1024, 1152]
        Wuq = sb([128, 2, 1024], BF16, "Wuq")
        Wukv = sb([128, 1024], BF16, "Wukv")
        Wglu = sb([128, 4, 512], BF16, "Wglu")
        Wout = sb([128, 8, 1024], BF16, "Wout")
        bglu = sb([128, 4], F32, "bglu")
        gc = sb([128, 24], F32, "gcols")
        gr = sb([128, 160], F32, "grows")
        Kbd = sb([128, 4, 8, 128], BF16, "Kbd")
        Wb = sb([128, 4, 8, 2, 128], BF16, "Wb")
        Vd = sb([128, 16, 2, 8, 32], BF16, "Vd")
        A8 = sb([128, 2, 16], F32, "A8")
        cosF = sb([128, 2048], BF16, "cosF"); sinF = sb([128, 2048], BF16, "sinF")
        cosT = sb([128, 17, 32], F32, "cosT"); sinT = sb([128, 17, 32], F32, "sinT")

        sA = ExitStack()
        with ExitStack() as s0:
            def sb0(shape, dt=F32, name=None):
                return sb(shape, dt, name, stack=sA)

            cmf = sb0([128, 5, 128], F32, "cmf")
            P.dma(cmf.t[:], cm[:, :, :], [], [cmf.b])
            for i, dst in enumerate((ident_b, blk64, blk32, ones256, ones512)):
                cp("dve", dst.t[:], cmf.t[:, i, :], [cmf.b], [dst.b])
            cp("dve", ident_f.t[:], cmf.t[:, 0, :], [cmf.b], [ident_f.b])
            P.dma(sel_f.t[:], sel[:, :], [], [sel_f.b])
            P.op("pool", lambda g: g.memset(epsc.t[:], EPS), [], [epsc.b])
            P.op("pool", lambda g: g.memset(mhalf.t[:], -0.5), [], [mhalf.b])
            P.dma(bglu.t[:], b_glu[:, :], [], [bglu.b])
            ts("dve", hbglu.t[:], bglu.t[:], 0.5, None, ALU.mult, None, [bglu.b], [hbglu.b])
            P.dma(gc.t[:], gcols[:, :], [], [gc.b])
            P.dma(gr.t[:], grows[:, :], [], [gr.b])
            ts("dve", gc.t[:, 0:6], gc.t[:, 0:6], float(96 ** -0.5), None, ALU.mult, None, [gc.b], [gc.b])
            ts("dve", gc.t[:, 16:18], gc.t[:, 16:18], float(96 ** -0.5), None, ALU.mult, None, [gc.b], [gc.b])

            pre_a1 = sb0([128, 3, 512], F32, "a1"); P.dma(pre_a1.t[:], a1[:, :, :], [], [pre_a1.b])
            pre_bt1 = sb0([128, 2, 512], F32, "bt1"); P.dma(pre_bt1.t[:], bt1[:, :, :], [], [pre_bt1.b])
        scope_holder = [None]

        def sb0(shape, dt=F32, name=None):
            return sb(shape, dt, name, stack=scope_holder[0])

        if True:
            def cgen(aa, F_, npow, tag):
                e = "dve"
                dt_ = sb0([128, F_], F32, tag + "dt"); act(dt_.t[:], aa.t[:, 2, :], AF.Exp, [aa.b], [dt_.b])
                mag = sb0([128, F_], F32, tag + "mag"); th = sb0([128, F_], F32, tag + "th")
                tt(e, mag.t[:], aa.t[:, 0, :], dt_.t[:], ALU.mult, [aa.b, dt_.b], [mag.b])
                act(mag.t[:], mag.t[:], AF.Exp, [mag.b], [mag.b])
                tt(e, th.t[:], aa.t[:, 1, :], dt_.t[:], ALU.mult, [aa.b, dt_.b], [th.b])
                cr = sb0([128, F_], F32, tag + "cr"); ci = sb0([128, F_], F32, tag + "ci")
                t1 = sb0([128, F_], F32, tag + "t1"); t2 = sb0([128, F_], F32, tag + "t2")
                hp = sb0([128, 1], F32, tag + "hp"); P.op("pool", lambda g: g.memset(hp.t[:], PI / 2), [], [hp.b])
                act(ci.t[:], th.t[:], AF.Sin, [th.b], [ci.b], scale=1.0 / 64)
                act(cr.t[:], th.t[:], AF.Sin, [th.b, hp.b], [cr.b], scale=1.0 / 64, bias=hp.t[:, 0:1])
                for _ in range(6):
                    tt(e, t1.t[:], cr.t[:], cr.t[:], ALU.mult, [cr.b], [t1.b])
                    tt(e, t2.t[:], ci.t[:], ci.t[:], ALU.mult, [ci.b], [t2.b])
                    tt(e, ci.t[:], cr.t[:], ci.t[:], ALU.mult, [cr.b, ci.b], [ci.b])
                    ts(e, ci.t[:], ci.t[:], 2.0, None, ALU.mult, None, [ci.b], [ci.b])
                    tt(e, cr.t[:], t1.t[:], t2.t[:], ALU.subtract, [t1.b, t2.b], [cr.b])
                pw = sb0([128, 2, npow + 1, F_], F32, tag + "pw")
                P.op("pool", lambda g: g.memset(pw.t[:, 0, 0, :], 1.0), [], [pw.b])
                P.op("pool", lambda g: g.memset(pw.t[:, 1, 0, :], 0.0), [], [pw.b])
                tt(e, pw.t[:, 0, 1, :], cr.t[:], mag.t[:], ALU.mult, [cr.b, mag.b], [pw.b])
                tt(e, pw.t[:, 1, 1, :], ci.t[:], mag.t[:], ALU.mult, [ci.b, mag.b], [pw.b])
                for m in range(2, npow + 1):
                    cmul(pw.t[:, 0, m, :], pw.t[:, 1, m, :], pw.t[:, 0, m - 1, :], pw.t[:, 1, m - 1, :], pw.t[:, 0, 1, :], pw.t[:, 1, 1, :],
                         [pw.b], [pw.b], t1, t2)
                x_ = sb0([128, F_], F32, tag + "x"); den = sb0([128, F_], F32, tag + "den")
                cf = sb0([128, 2, F_], F32, tag + "cf")
                ts(e, x_.t[:], pw.t[:, 0, 1, :], -1.0, None, ALU.add, None, [pw.b], [x_.b])
                tt(e, den.t[:], aa.t[:, 0, :], aa.t[:, 0, :], ALU.mult, [aa.b], [den.b])
                tt(e, t1.t[:], aa.t[:, 1, :], aa.t[:, 1, :], ALU.mult, [aa.b], [t1.b])
                tt(e, den.t[:], den.t[:], t1.t[:], ALU.add, [den.b, t1.b], [den.b])
                P.op(e, lambda g: g.reciprocal(out=den.t[:], in_=den.t[:]), [den.b], [den.b])
                tt(e, t1.t[:], x_.t[:], aa.t[:, 0, :], ALU.mult, [x_.b, aa.b], [t1.b])
                tt(e, t2.t[:], pw.t[:, 1, 1, :], aa.t[:, 1, :], ALU.mult, [pw.b, aa.b], [t2.b])
                tt(e, t1.t[:], t1.t[:], t2.t[:], ALU.add, [t1.b, t2.b], [t1.b])
                tt(e, cf.t[:, 0, :], t1.t[:], den.t[:], ALU.mult, [t1.b, den.b], [cf.b])
                tt(e, t1.t[:], pw.t[:, 1, 1, :], aa.t[:, 0, :], ALU.mult, [pw.b, aa.b], [t1.b])
                tt(e, t2.t[:], x_.t[:], aa.t[:, 1, :], ALU.mult, [x_.b, aa.b], [t2.b])
                tt(e, t1.t[:], t1.t[:], t2.t[:], ALU.subtract, [t1.b, t2.b], [t1.b])
                tt(e, cf.t[:, 1, :], t1.t[:], den.t[:], ALU.mult, [t1.b, den.b], [cf.b])
                return pw, cf

            def cmul(or_, oi_, ar_, ai_, br_, bi_, R, W, t1, t2, e="dve", neg_im=False):
                sh = tuple(or_.shape)
                a1_ = _view(t1, sh); a2_ = _view(t2, sh)
                tt(e, a1_, ar_, br_, ALU.mult, R, [t1.b])
                tt(e, a2_, ai_, bi_, ALU.mult, R, [t2.b])
                tt(e, or_, a1_, a2_, ALU.subtract, [t1.b, t2.b], W)
                tt(e, a1_, ar_, bi_, ALU.mult, R, [t1.b])
                tt(e, a2_, ai_, br_, ALU.mult, R, [t2.b])
                if neg_im:
                    tt(e, a1_, a1_, a2_, ALU.add, [t1.b, t2.b], [t1.b])
                    ts(e, oi_, a1_, -1.0, None, ALU.mult, None, [t1.b], W)
                else:
                    tt(e, oi_, a1_, a2_, ALU.add, [t1.b, t2.b], W)

            def _view(t, sh):
                n = 1
                for s_ in sh[1:]:
                    n *= s_
                flat = t.t[:, 0:n]
                if len(sh) == 2:
                    return flat
                if len(sh) == 3:
                    return flat.rearrange("p (a b) -> p a b", a=sh[1])
                return flat.rearrange("p (a b c) -> p a b c", a=sh[1], b=sh[2])

        with ExitStack() as s1:
            scope_holder[0] = s1
            a1t = pre_a1
            bt1t = pre_bt1
            pw1, cf1 = cgen(a1t, 512, 7, "g1")
            T1 = sb0([128, 512], F32, "T1"); T2 = sb0([128, 512], F32, "T2")
            bb1 = sb0([128, 2, 512], F32, "bb1")
            cmul(bb1.t[:, 0, :], bb1.t[:, 1, :], cf1.t[:, 0, :], cf1.t[:, 1, :], bt1t.t[:, 0, :], bt1t.t[:, 1, :], [cf1.b, bt1t.b], [bb1.b], T1, T2)
            wtmp = sb0([128, 2, 4, 128], F32, "wtmp")
            for tau in range(8):
                m = 7 - tau
                cmul(wtmp.t[:, 0, :, :].rearrange("p a b -> p (a b)"), wtmp.t[:, 1, :, :].rearrange("p a b -> p (a b)"),
                     pw1.t[:, 0, m, :], pw1.t[:, 1, m, :], bb1.t[:, 0, :], bb1.t[:, 1, :], [pw1.b, bb1.b], [wtmp.b], T1, T2)
                for ri in range(2):
                    cp("act", Wb.t[:, :, tau, ri, :], wtmp.t[:, ri, :, :], [wtmp.b], [Wb.b])

            P.barrier()
        P.barrier()
        sA.close()
        with ExitStack() as s2:
            scope_holder[0] = s2
            a2t = sb0([128, 3, 16], F32, "a2"); P.dma(a2t.t[:], a2[:, :, :], [], [a2t.b])
            b2t = sb0([128, 2, 16, 128], F32, "b2"); P.dma(b2t.t[:], b2[:, :, :, :], [], [b2t.b])
            c2t = sb0([128, 2, 16, 32], F32, "c2"); P.dma(c2t.t[:], c2[:, :, :, :], [], [c2t.b])
            pw2, cf2 = cgen(a2t, 16, 8, "g2")
            ninc = sb0([128, 8], F32, "nin"); nqc = sb0([128, 2], F32, "nq"); noutc = sb0([128, 8], F32, "nout")
            P.dma(ninc.t[:], nin[:, :], [], [ninc.b]); P.dma(nqc.t[:], nq[:, :], [], [nqc.b]); P.dma(noutc.t[:], nout[:, :], [], [noutc.b])
            ts("dve", noutc.t[:, 0:4], noutc.t[:, 0:4], 0.125, None, ALU.mult, None, [noutc.b], [noutc.b])
            ts("dve", noutc.t[:, 4:8], noutc.t[:, 4:8], 0.5, None, ALU.mult, None, [noutc.b], [noutc.b])
            stg = [sb0([128, 2048], F32, "stg") for _ in range(2)]
            si = [0]

            def load_w(dst_ap, src_ap, ncol, gain_ap, e):
                s_ = stg[si[0] % 2]; si[0] += 1
                P.dma(s_.t[:, 0:ncol], src_ap, [], [s_.b])
                if gain_ap is None:
                    cp(e, dst_ap, s_.t[:, 0:ncol], [s_.b], [dstb[0]])
                elif e == "act":
                    act(dst_ap, s_.t[:, 0:ncol], AF.Copy, [s_.b, gainb[0]], [dstb[0]], scale=gain_ap)
                else:
                    ts(e, dst_ap, s_.t[:, 0:ncol], gain_ap, None, ALU.mult, None, [s_.b, gainb[0]], [dstb[0]])

            wst = [sb0([128, 8, 160], F32, "wst") for _ in range(2)]
            wbf = [sb0([128, 8, 160], BF16, "wbf") for _ in range(2)]
            for pi_, c0_ in enumerate(PIECE_COL0 + [1280]):
                w_ = 160 if pi_ == 14 else 128
                a_ = wst[pi_ % 2]; b_ = wbf[pi_ % 2]
                P.dma(a_.t[:, :, 0:w_], w_in[:, :, c0_:c0_ + w_], [], [a_.b])
                for d_ in range(8):
                    act(b_.t[:, d_, 0:w_], a_.t[:, d_, 0:w_], AF.Copy, [a_.b, ninc.b], [b_.b], scale=ninc.t[:, d_:d_ + 1])
                if pi_ < 14:
                    P.dma(WinD[:, pi_, :].rearrange("p (a b) -> p a b", a=8), b_.t[:, :, 0:128], [b_.b], [WinD_b])
                else:
                    P.dma(WinT[:, :].rearrange("p (a b) -> p a b", a=8), b_.t[:, :, 0:160], [b_.b], [WinT_b])
            dstb = [Wuq.b]; gainb = [nqc.b]
            for kt in range(2):
                load_w(Wuq.t[:, kt, :], w_uq[:, kt, :], 1024, nqc.t[:, kt:kt + 1], "act")
            dstb = [Wukv.b]
            load_w(Wukv.t[:, :], w_ukv[:, :], 1024, None, "act")
            dstb = [Wglu.b]
            load_w(Wglu.t[:, :, :].rearrange("p a b -> p (a b)"), w_glu[:, :, :].rearrange("p a b -> p (a b)"), 2048, None, "act")
            dstb = [Wout.b]; gainb = [noutc.b]
            for kt in range(8):
                load_w(Wout.t[:, kt, :], w_out[:, kt, :], 1024, noutc.t[:, kt:kt + 1], "act")
            cp("dve", A8.t[:, 0, :], pw2.t[:, 0, 8, :], [pw2.b], [A8.b])
            cp("dve", A8.t[:, 1, :], pw2.t[:, 1, 8, :], [pw2.b], [A8.b])
            U1 = sb0([128, 2048], F32, "U1"); U2 = sb0([128, 2048], F32, "U2")
            VdF = sb0([128, 16, 2, 9, 32], F32, "VdF")
            for m in range(9):
                cmul(VdF.t[:, :, 0, m, :], VdF.t[:, :, 1, m, :],
                     c2t.t[:, 0, :, :], c2t.t[:, 1, :, :],
                     pw2.t[:, 0, m, :].unsqueeze(2).to_broadcast([128, 16, 32]), pw2.t[:, 1, m, :].unsqueeze(2).to_broadcast([128, 16, 32]),
                     [c2t.b, pw2.b], [VdF.b], U1, U2, neg_im=True)
            for ri in range(2):
                for pr in range(16):
                    cp("act" if pr % 2 else "pool", Vd.t[:, pr, ri, :, :], VdF.t[:, pr, ri, 1:9, :], [VdF.b], [Vd.b])
            bb2 = sb0([128, 2, 16, 128], F32, "bb2")
            cmul(bb2.t[:, 0, :, :], bb2.t[:, 1, :, :],
                 cf2.t[:, 0, :].unsqueeze(2).to_broadcast([128, 16, 128]), cf2.t[:, 1, :].unsqueeze(2).to_broadcast([128, 16, 128]),
                 b2t.t[:, 0, :, :], b2t.t[:, 1, :, :], [cf2.b, b2t.b], [bb2.b], U1, U2)
            for ct in range(4):
                for lg in range(2):
                    kp = nextF()
                    for p4 in range(4):
                        pr = ct * 4 + p4
                        for ri in range(2):
                            mm(kp.t[:, 128 * p4:128 * p4 + 128], bb2.t[:, ri, pr, :], VdF.t[:, pr, ri, 4 * lg:4 * lg + 4, :],
                               ri == 0, ri == 1, [bb2.b, VdF.b], [kp.b])
                    kv4 = kp.t[:, :].rearrange("p (a l c) -> p a l c", a=4, l=4)
                    if lg == 0:
                        stt(Kbd.t[:, ct, 0, :].rearrange("p (a c) -> p a c", a=4), ident_f.t[:].rearrange("p (a c) -> p a c", a=4),
                            gc.t[:, 8 + ct:9 + ct], kv4[:, :, 0, :], ALU.mult, ALU.add, [ident_f.b, gc.b, kp.b], [Kbd.b])
                        for l_ in range(1, 4):
                            cp("dve", Kbd.t[:, ct, l_, :].rearrange("p (a c) -> p a c", a=4), kv4[:, :, l_, :], [kp.b], [Kbd.b])
                    else:
                        for l_ in range(4):
                            cp("dve", Kbd.t[:, ct, 4 + l_, :].rearrange("p (a c) -> p a c", a=4), kv4[:, :, l_, :], [kp.b], [Kbd.b])
            P.barrier()
        P.barrier()
        with ExitStack() as s3:
            scope_holder[0] = s3
            T1 = sb0([128, 1024], F32, "T1"); T2 = sb0([128, 1024], F32, "T2")
            cF = sb0([128, 2048], F32, "cF"); sF = sb0([128, 2048], F32, "sF")
            inv = sb0([128, 1], F32, "inv"); wv = sb0([128, 4], F32, "wv"); hp2 = sb0([128, 1], F32, "hp2")
            P.op("pool", lambda g: g.memset(hp2.t[:], PI / 2), [], [hp2.b])
            act(inv.t[:], gc.t[:, 7:8], AF.Exp, [gc.b], [inv.b], scale=float(-np.log(10000.0) / 16))
            act(wv.t[:, 1:2], inv.t[:], AF.Sin, [inv.b], [wv.b])
            act(wv.t[:, 0:1], inv.t[:], AF.Sin, [inv.b, hp2.b], [wv.b], bias=hp2.t[:, 0:1])
            P.op("pool", lambda g: g.memset(cF.t[:, 0:1], 1.0), [], [cF.b])
            P.op("pool", lambda g: g.memset(sF.t[:, 0:1], 0.0), [], [sF.b])
            for k in range(11):
                n = 1 << k
                ts("dve", T1.t[:, 0:n], sF.t[:, 0:n], wv.t[:, 1:2], None, ALU.mult, None, [sF.b, wv.b], [T1.b])
                ts("dve", T2.t[:, 0:n], cF.t[:, 0:n], wv.t[:, 1:2], None, ALU.mult, None, [cF.b, wv.b], [T2.b])
                stt(cF.t[:, n:2 * n], cF.t[:, 0:n], wv.t[:, 0:1], T1.t[:, 0:n], ALU.mult, ALU.subtract, [cF.b, wv.b, T1.b], [cF.b])
                stt(sF.t[:, n:2 * n], sF.t[:, 0:n], wv.t[:, 0:1], T2.t[:, 0:n], ALU.mult, ALU.add, [sF.b, wv.b, T2.b], [sF.b])
                tt("dve", wv.t[:, 2:3], wv.t[:, 0:1], wv.t[:, 0:1], ALU.mult, [wv.b], [wv.b])
                tt("dve", wv.t[:, 3:4], wv.t[:, 1:2], wv.t[:, 1:2], ALU.mult, [wv.b], [wv.b])
                tt("dve", wv.t[:, 1:2], wv.t[:, 0:1], wv.t[:, 1:2], ALU.mult, [wv.b], [wv.b])
                ts("dve", wv.t[:, 1:2], wv.t[:, 1:2], 2.0, None, ALU.mult, None, [wv.b], [wv.b])
                tt("dve", wv.t[:, 0:1], wv.t[:, 2:3], wv.t[:, 3:4], ALU.subtract, [wv.b], [wv.b])
            ts("dve", sF.t[:], sF.t[:], gc.t[:, 6:7], None, ALU.mult, None, [sF.b, gc.b], [sF.b])
            cp("act", cosF.t[:], cF.t[:], [cF.b], [cosF.b])
            cp("act", sinF.t[:], sF.t[:], [sF.b], [sinF.b])
            for src, dst in ((cF, cosT), (sF, sinT)):
                for t_ in range(17):
                    pt = nextF()
                    if t_ < 16:
                        in_ap = src.t[:, 128 * t_:128 * t_ + 128]
                        rb_ = src.b
                    else:
                        cp("dve", T1.t[:, 0:128].rearrange("p (a b) -> p a b", a=4), src.t[:, 1024:1056].unsqueeze(1).to_broadcast([128, 4, 32]), [src.b], [T1.b])
                        in_ap = T1.t[:, 0:128]
                        rb_ = T1.b
                    P.op("pe", lambda g: g.transpose(out=pt.t[:, 0:128], in_=in_ap, identity=ident_f.t[:]), [rb_, ident_f.b], [pt.b])
                    cp("dve", dst.t[:, t_, :], pt.t[:, 0:32], [pt.b], [dst.b])
            P.barrier()
        P.barrier()

        KTn = sb([128, 4, SEQ + 0], BF16, "KTn"); KTr = sb([128, SEQ], BF16, "KTr")
        Vc = sb([128, 16, 8, 65], BF16, "Vc")
        KTn_b = [Buf("KTn%d" % j_) for j_ in range(16)]; KTr_b = [Buf("KTr%d" % j_) for j_ in range(16)]; Vc_b = [Buf("Vc%d" % j_) for j_ in range(16)]
        P.op("pool", lambda g: g.memset(Vc.t[:, :, :, 64:65], 1.0), [], Vc_b)
        Hst = sb([128, 2, 16], F32, "Hst")
        xin = [sb([128, D], F32, "xin") for _ in range(2)]
        xslot = [0]
        ectr = [0]
        wctr = [0]

        def tile_pass(N, xsrc, yout, ckv_out, kr_out, pos0, tabidx0, sample, seq_first, seq_last, seqidx):
            NS = N // 128
            NC = N // L
            stage[0] = 'F'
            tidx[0] += 1
            xt = []
            for s_ in range(NS):
                x_ = xin[xslot[0] % 2]; xslot[0] += 1
                P.dma(x_.t[:], xsrc[128 * s_:128 * s_ + 128, :], [], [x_.b])
                xt.append(x_)
            ss = sb_t("ss", [128, 4], F32); junk = sb_t("xn", [128, D], BF16)
            for s_ in range(NS):
                act(junk.t[:], xt[s_].t[:], AF.Square, [xt[s_].b], [junk.b, ss.b], accum_out=ss.t[:, s_:s_ + 1])
            rs = sb_t("rs", [128, 4], F32)
            rsqrt_pow(rs.t[:, 0:NS], ss.t[:, 0:NS], [ss.b], [rs.b], scale=1.0 / D)
            hT = sb_t("hT", [128, 8, NT], BF16)
            xn = sb_t("xn", [128, D], BF16)
            for s_ in range(NS):
                ts("dve", xn.t[:], xt[s_].t[:], rs.t[:, s_:s_ + 1], None, ALU.mult, None, [xt[s_].b, rs.b], [xn.b])
                pb = nextB()
                with P.grp("pe"):
                    for d_ in range(8):
                        tr(pb.t[:, 128 * d_:128 * d_ + 128], xn.t[:, 128 * d_:128 * d_ + 128], ident_b.t[:], [xn.b, ident_b.b], [pb.b], inc=(d_ == 7))
                cp("act", hT.t[:, :, 128 * s_:128 * s_ + 128], pb.t[:, :].rearrange("p (a b) -> p a b", a=8), [pb.b], [hT.b])

            def proj_fm(col0, M):
                ps = nextF()
                wctr[0] += 1
                wp = sb_t("wp%d" % (wctr[0] % 3), [128, 8, 128], BF16)
                P.dma(wp.t[:], WinD[:, PIECE_COL0.index(col0), :].rearrange("p (a b) -> p a b", a=8), [WinD_b], [wp.b], nbytes=262144)
                with P.grp("pe"):
                    for d_ in range(8):
                        mm(ps.t[0:M, 0:N], wp.t[:, d_, 0:M], hT.t[:, d_, 0:N], d_ == 0, d_ == 7, [wp.b, hT.b], [ps.b])
                return ps

            uT = sb_t("uT", [128, 4, NT], BF16)
            ubd = sb_t("ubd", [128, 4, 4, NT], BF16)
            sg = sb_t("sg", [128, 4, NT], BF16)
            sgm = sb_t("sgm", [128, 4, NT], BF16)
            cq = sb_t("cq", [128, 2, NT], BF16)
            for i in range(4):
                ps = proj_fm(128 * i, 128)
                cp("dve", uT.t[:, i, 0:N], ps.t[:, 0:N], [ps.b], [uT.b])
                for k_ in range(4):
                    ts("dve", ubd.t[:, i, k_, 0:N], ps.t[:, 0:N], gc.t[:, 18 + k_:19 + k_], None, ALU.mult, None, [ps.b, gc.b], [ubd.b])
            for i in range(4):
                ps = proj_fm(512 + 128 * i, 128)
                th_ = sb_t("jk", [128, 160], F32)
                act(th_.t[:, 0:N], ps.t[:, 0:N], AF.Tanh, [ps.b], [th_.b], scale=0.5)
                stt(sg.t[:, i, 0:N], th_.t[:, 0:N], 1.0, ps.t[:, 0:N], ALU.add, ALU.mult, [th_.b, ps.b], [sg.b])
            for i in range(4):
                ps = proj_fm(1440 + 128 * i, 128)
                th_ = sb_t("jk", [128, 160], F32)
                act(th_.t[:, 0:N], ps.t[:, 0:N], AF.Tanh, [ps.b], [th_.b], scale=0.5)
                stt(sgm.t[:, i, 0:N], th_.t[:, 0:N], 1.0, ps.t[:, 0:N], ALU.add, ALU.mult, [th_.b, ps.b], [sgm.b])
            for i in range(2):
                ps = proj_fm(1024 + 128 * i, 128)
                cp("dve", cq.t[:, i, 0:N], ps.t[:, 0:N], [ps.b], [cq.b])
            ckvT = sb_t("ckvT", [128, NT], BF16)
            krT = sb_t("krT", [128, NT], BF16)
            for s_ in range(NS):
                ps = nextF()
                wt_ = sb_t("wtm", [128, 8, 160], BF16)
                if s_ == 0:
                    P.dma(wt_.t[:], WinT[:, :].rearrange("p (a b) -> p a b", a=8), [WinT_b], [wt_.b], nbytes=327680)
                with P.grp("pe"):
                    for d_ in range(8):
                        mm(ps.t[:, 0:160], hT.t[:, d_, 128 * s_:128 * s_ + 128], wt_.t[:, d_, :], d_ == 0, d_ == 7, [wt_.b, hT.b], [ps.b])
                st2 = sb_t("st2", [128, 4], F32); jk = sb_t("jk", [128, 160], F32)
                act(jk.t[:, 0:128], ps.t[:, 0:128], AF.Square, [ps.b], [jk.b, st2.b], accum_out=st2.t[:, 0:1])
                act(jk.t[:, 128:160], ps.t[:, 128:160], AF.Square, [ps.b], [jk.b, st2.b], accum_out=st2.t[:, 1:2])
                rsqrt_pow(st2.t[:, 2:3], st2.t[:, 0:1], [st2.b], [st2.b], scale=1.0 / 128)
                rsqrt_pow(st2.t[:, 3:4], st2.t[:, 1:2], [st2.b], [st2.b], scale=1.0 / 32)
                okv = sb_t("okv", [128, 128], F32); okr = sb_t("okr", [128, 32], F32); kn_ = sb_t("kn_", [128, 32], F32)
                stt(okv.t[:], ps.t[:, 0:128], st2.t[:, 2:3], gr.t[:, 0:128], ALU.mult, ALU.mult, [ps.b, st2.b, gr.b], [okv.b])
                stt(kn_.t[:], ps.t[:, 128:160], st2.t[:, 3:4], gr.t[:, 128:160], ALU.mult, ALU.mult, [ps.b, st2.b, gr.b], [kn_.b])
                ti = tabidx0 + s_
                r1 = sb_t("r1", [128, 32], F32); r2 = sb_t("r2", [128, 32], F32)
                tt("dve", r1.t[:], kn_.t[:], cosT.t[:, ti, :], ALU.mult, [kn_.b, cosT.b], [r1.b])
                tt("dve", r2.t[:, 0:16], kn_.t[:, 16:32], sinT.t[:, ti, 0:16], ALU.mult, [kn_.b, sinT.b], [r2.b])
                tt("dve", r2.t[:, 16:32], kn_.t[:, 0:16], sinT.t[:, ti, 16:32], ALU.mult, [kn_.b, sinT.b], [r2.b])
                tt("dve", okr.t[:], r1.t[:], r2.t[:], ALU.add, [r1.b, r2.b], [okr.b])
                P.dma(ckv_out[128 * s_:128 * s_ + 128, :], okv.t[:], [okv.b], [], is_output=True)
                P.dma(kr_out[128 * s_:128 * s_ + 128, :], okr.t[:], [okr.b], [], is_output=True)
                tb = sb_t("tb", [128, 256], BF16)
                cp("dve", tb.t[:, 0:128], okv.t[:], [okv.b], [tb.b])
                cp("dve", tb.t[:, 128:256].rearrange("p (a b) -> p a b", a=4), okr.t[:, :].unsqueeze(1).to_broadcast([128, 4, 32]), [okr.b], [tb.b])
                pb = nextB()
                with P.grp("pe"):
                    tr(pb.t[:, 0:128], tb.t[:, 0:128], ident_b.t[:], [tb.b, ident_b.b], [pb.b], inc=False)
                    tr(pb.t[:, 128:256], tb.t[:, 128:256], ident_b.t[:], [tb.b, ident_b.b], [pb.b])
                cp("act", ckvT.t[:, 128 * s_:128 * s_ + 128], pb.t[:, 0:128], [pb.b], [ckvT.b])
                cp("act", krT.t[:, 128 * s_:128 * s_ + 128], pb.t[:, 128:256], [pb.b], [krT.b])

            if flags.get('upto', 9) < 2:
                return
            stage[0] = 'B'
            Xs = sb_t("Xs", [128, 2, 16, NT // L], F32)
            Hs = sb_t("Hs", [128, 2, 16, NT // L + 4], F32)
            Hb = sb_t("Hb", [128, 2, 16, NT // L], BF16)
            for ri in range(2):
                xps = [nextF(), nextF()]
                for ct in range(4):
                    ps = xps[ct // 2]
                    c0_ = 4 * NC * (ct % 2)
                    with P.grp("pe"):
                        for tau in range(8):
                            mm(ps.t[:, c0_:c0_ + 4 * NC].rearrange("q (a k) -> q a k", a=4), Wb.t[:, ct, tau, ri, :], ubd.t[:, ct, :, tau:N:L],
                               tau == 0, tau == 7, [Wb.b, ubd.b], [ps.b])
                for hf in range(2):
                    cp("dve" if hf else "act", Xs.t[:, ri, 8 * hf:8 * hf + 8, 0:NC], xps[hf].t[:, 0:8 * NC].rearrange("p (a b) -> p a b", a=8), [xps[hf].b], [Xs.b])
            if flags.get('s3', 9) < 2:
                return
            SCAN_E = flags.get('scan_engine', 'pool')
            M1 = sb_t("M1", [128, 2, 16], F32); M2 = sb_t("M2", [128, 2, 16], F32)
            A8r = A8.t[:, 0, :].unsqueeze(1).to_broadcast([128, 2, 16]); A8i = A8.t[:, 1, :].unsqueeze(1).to_broadcast([128, 2, 16])
            segs = [(0, NC)] if not sample else [(4 * s_, 4) for s_ in range(4)]
            for sgi, (k0, nk) in enumerate(segs):
                base = k0 + sgi
                if sample:
                    h0t = sb_t("h0t", [128, 4, 2, 16], F32)
                    if sgi == 0:
                        P.dma(h0t.t[:, :, 0, :], h0r[:, :, :], [], [h0t.b])
                        P.dma(h0t.t[:, :, 1, :], h0i[:, :, :], [], [h0t.b])
                    cp(SCAN_E, Hs.t[:, :, :, base], h0t.t[:, sgi, :, :], [h0t.b], [Hs.b])
                elif seq_first:
                    mset(Hs.t[:, :, :, base], 0.0, [Hs.b], e=SCAN_E)
                else:
                    cp(SCAN_E, Hs.t[:, :, :, base], Hst.t[:, :, :], [Hst.b], [Hs.b])
                for j in range(nk):
                    hp_ = Hs.t[:, :, :, base + j]; hn_ = Hs.t[:, :, :, base + j + 1]
                    tt(SCAN_E, M1.t[:], hp_, A8r, ALU.mult, [Hs.b, A8.b], [M1.b])
                    tt(SCAN_E, M2.t[:], hp_, A8i, ALU.mult, [Hs.b, A8.b], [M2.b])
                    tt(SCAN_E, M1.t[:], M1.t[:], Xs.t[:, :, :, k0 + j], ALU.add, [M1.b, Xs.b], [M1.b])
                    tt(SCAN_E, Hs.t[:, 0, :, base + j + 1], M1.t[:, 0, :], M2.t[:, 1, :], ALU.subtract, [M1.b, M2.b], [Hs.b])
                    tt(SCAN_E, Hs.t[:, 1, :, base + j + 1], M1.t[:, 1, :], M2.t[:, 0, :], ALU.add, [M1.b, M2.b], [Hs.b])
                cp("dve", Hb.t[:, :, :, k0:k0 + nk], Hs.t[:, :, :, base:base + nk], [Hs.b], [Hb.b])
                if sample:
                    hso = sb_t("hso", [128, 4, 2, 16], F32)
                    cp(SCAN_E, hso.t[:, sgi, :, :], Hs.t[:, :, :, base + nk], [Hs.b], [hso.b])
                    if sgi == 3:
                        P.dma(ost_s[:, :, :, :], hso.t[:], [hso.b], [], is_output=True)
                else:
                    cp(SCAN_E, Hst.t[:, :, :], Hs.t[:, :, :, base + nk], [Hs.b], [Hst.b])
                    if seq_last:
                        hpo = sb_t("hpo", [128, 2, 16], F32)
                        cp(SCAN_E, hpo.t[:], Hst.t[:], [Hst.b], [hpo.b])
                        P.dma(ost_p[:, seqidx, :, :], hpo.t[:], [hpo.b], [], is_output=True)
            if flags.get('s3', 9) < 3:
                return
            yg = sb_t("yg", [128, 4, NT], BF16)
            for ct in range(4):
                yp_ = nextF()
                for tau in range(8):
                    o_ = yp_.t[:, NC * tau:NC * tau + NC]
                    with P.grp("pe"):
                        for lag in range(tau + 1):
                            mm(o_, Kbd.t[:, ct, lag, :], uT.t[:, ct, tau - lag:N:L], lag == 0, False, [Kbd.b, uT.b], [yp_.b], inc=False)
                        for p4 in range(4):
                            pr = 4 * ct + p4
                            for ri in range(2):
                                last = (p4 == 3 and ri == 1)
                                kw = {"tile_position": (0, 96)} if p4 == 3 else {}
                                mm(yp_.t[32 * p4:32 * p4 + 32, NC * tau:NC * tau + NC], Vd.t[:, pr, ri, tau, :], Hb.t[:, ri, pr, 0:NC], False, last,
                                   [Vd.b, Hb.b], [yp_.b], inc=last, **kw)
                g1 = sb_t("rs_ssm", [128, NT], F32); g2 = sb_t("sig", [128, NT], BF16)
                act(g1.t[:, 0:N], yp_.t[:, 0:N], AF.Square, [yp_.b], [g1.b])
                ts("dve", g1.t[:, 0:N], g1.t[:, 0:N], 0.044715, 1.0, ALU.mult, ALU.add, [g1.b], [g1.b])
                tt("dve", g1.t[:, 0:N], g1.t[:, 0:N], yp_.t[:, 0:N], ALU.mult, [g1.b, yp_.b], [g1.b])
                act(g2.t[:, 0:N], g1.t[:, 0:N], AF.Tanh, [g1.b], [g2.b], scale=0.7978845608028654)
                stt(yg.t[:, ct, 0:N].rearrange("p (k t) -> p t k", t=L), g2.t[:, 0:N].rearrange("p (t k) -> p t k", t=L), 1.0,
                    yp_.t[:, 0:N].rearrange("p (t k) -> p t k", t=L), ALU.add, ALU.mult, [g2.b, yp_.b], [yg.b])
            if flags.get('s3', 9) < 5:
                return
            ys_ = sb_t("ys_", [128, 4, NT], BF16)
            sq = sb_t("sq", [128, 4, NT], BF16)
            for co in range(4):
                ps = nextF()
                with P.grp("pe"):
                    for ci_ in range(4):
                        mm(ps.t[:, 0:N], Wglu.t[:, ci_, 128 * co:128 * co + 128], yg.t[:, ci_, 0:N], ci_ == 0, ci_ == 3, [Wglu.b, yg.b], [ps.b])
                sig = sb_t("sig", [128, NT], BF16)
                act(sig.t[:, 0:N], ps.t[:, 0:N], AF.Tanh, [ps.b, hbglu.b], [sig.b], bias=hbglu.t[:, co:co + 1], scale=0.25)
                stt(ys_.t[:, co, 0:N], sig.t[:, 0:N], 1.0, yg.t[:, co, 0:N], ALU.add, ALU.mult, [sig.b, yg.b], [ys_.b])
                act(sq.t[:, co, 0:N], ys_.t[:, co, 0:N], AF.Square, [ys_.b], [sq.b])
            def bc_rstd(sqt, ntile, onesT, name, scale=1.0):
                ps = nextF()
                with P.grp("pe"):
                    for i in range(ntile):
                        mm(ps.t[:, 0:N], onesT.t[:], sqt.t[:, i, 0:N], i == 0, i == ntile - 1, [onesT.b, sqt.b], [ps.b])
                r_ = sb_t(name, [128, NT], F32)
                rsqrt_pow(r_.t[:, 0:N], ps.t[:, 0:N], [ps.b], [r_.b], scale=scale)
                return r_
            rs_ssm = bc_rstd(sq, 4, ones512, "rs_ssm", scale=1.0 / 16)
            mix = sb_t("mix", [128, 8, NT], BF16)
            for ct in range(4):
                tt("dve", ys_.t[:, ct, 0:N], ys_.t[:, ct, 0:N], sg.t[:, ct, 0:N], ALU.mult, [ys_.b, sg.b], [ys_.b])
                tt("dve", mix.t[:, ct, 0:N], ys_.t[:, ct, 0:N], rs_ssm.t[:, 0:N], ALU.mult, [ys_.b, rs_ssm.b], [mix.b])

            if flags.get('upto', 9) < 3:
                return
            stage[0] = 'S4'
            sqq = sb_t("sq4", [128, 4, NT], BF16)
            for i in range(2):
                act(sqq.t[:, i, 0:N], cq.t[:, i, 0:N], AF.Square, [cq.b], [sqq.b])
            rq = bc_rstd(sqq, 2, ones256, "rq")
            rq2 = sb_t("rq2", [128, NT], F32)
            tt("dve", rq2.t[:, 0:N], rq.t[:, 0:N], rq.t[:, 0:N], ALU.mult, [rq.b], [rq2.b])
            QTn = sb_t("QTn", [128, 4, 2 * NT], BF16); QTr = sb_t("QTr", [128, 4, 2 * NT], BF16)

            def headnorm(ps, blk, rq_, rq2_, name):
                s2 = sb_t("hn_s2", [128, NT], BF16)
                act(s2.t[:, 0:N], ps.t[:, 0:N], AF.Square, [ps.b], [s2.b])
                p2 = nextF()
                mm(p2.t[:, 0:N], blk.t[:], s2.t[:, 0:N], True, True, [blk.b, s2.b], [p2.b])
                t_ = sb_t("hn_t", [128, NT], F32)
                if rq_ is not None:
                    tt("dve", t_.t[:, 0:N], p2.t[:, 0:N], rq2_.t[:, 0:N], ALU.mult, [p2.b, rq2_.b], [t_.b])
                    rsqrt_pow(t_.t[:, 0:N], t_.t[:, 0:N], [t_.b], [t_.b])
                else:
                    rsqrt_pow(t_.t[:, 0:N], p2.t[:, 0:N], [p2.b], [t_.b])
                if rq_ is not None:
                    tt("dve", t_.t[:, 0:N], t_.t[:, 0:N], rq_.t[:, 0:N], ALU.mult, [t_.b, rq_.b], [t_.b])
                return t_

            def qproj(col0):
                ps = nextF()
                with P.grp("pe"):
                    for kt in range(2):
                        mm(ps.t[:, 0:N], Wuq.t[:, kt, col0:col0 + 128], cq.t[:, kt, 0:N], kt == 0, kt == 1, [Wuq.b, cq.b], [ps.b])
                return ps

            for i in range(4):
                ps = qproj(128 * i)
                t_ = headnorm(ps, blk64, rq, rq2, "qn")
                stt(QTn.t[:, i, 0:N], ps.t[:, 0:N], gc.t[:, 16:17], t_.t[:, 0:N], ALU.mult, ALU.mult, [ps.b, gc.b, t_.b], [QTn.b])
                stt(QTn.t[:, i, NT:NT + N], ps.t[:, 0:N], gc.t[:, 17:18], t_.t[:, 0:N], ALU.mult, ALU.mult, [ps.b, gc.b, t_.b], [QTn.b])
            if sample:
                cos_q = cosF.t[:, 1024:1056].unsqueeze(1).to_broadcast([128, 4, 32]); sin_q = sinF.t[:, 1024:1056].unsqueeze(1).to_broadcast([128, 4, 32])
                vq = lambda ap: ap.rearrange("p (a b) -> p a b", a=4)
            else:
                cos_q = cosF.t[:, pos0:pos0 + N]; sin_q = sinF.t[:, pos0:pos0 + N]
                vq = lambda ap: ap
            for i in range(2):
                ps = qproj(512 + 128 * i)
                t_ = headnorm(ps, blk32, rq, rq2, "qr")
                qa = sb_t("qa", [128, NT], F32); qb = sb_t("qb", [128, NT], F32)
                stt(qa.t[:, 0:N], ps.t[:, 0:N], gc.t[:, 4:5], t_.t[:, 0:N], ALU.mult, ALU.mult, [ps.b, gc.b, t_.b], [qa.b])
                ps2 = qproj(768 + 128 * i)
                stt(qb.t[:, 0:N], ps2.t[:, 0:N], gc.t[:, 5:6], t_.t[:, 0:N], ALU.mult, ALU.mult, [ps2.b, gc.b, t_.b], [qb.b])
                tt("dve", vq(qa.t[:, 0:N]), vq(qa.t[:, 0:N]), cos_q, ALU.mult, [qa.b, cosF.b], [qa.b])
                tt("dve", vq(qb.t[:, 0:N]), vq(qb.t[:, 0:N]), sin_q, ALU.mult, [qb.b, sinF.b], [qb.b])
                tt("dve", qa.t[:, 0:N], qa.t[:, 0:N], qb.t[:, 0:N], ALU.add, [qa.b, qb.b], [qa.b])
                for k_ in range(4):
                    h_ = 4 * i + k_
                    ts("dve", QTr.t[:, h_ // 2, (h_ % 2) * NT:(h_ % 2) * NT + N], qa.t[:, 0:N], gc.t[:, 18 + k_:19 + k_], None, ALU.mult, None,
                       [qa.b, gc.b], [QTr.b])

            if sample:
                KTn_new = sb_t("KTnn", [128, 4, 128], BF16); KTr_new = krT
                Vn = sb_t("Vn", [32, 4, 8, 65], BF16)
                kdst = lambda i: KTn_new.t[:, i, 0:N]; kdb = KTn_new.b
            else:
                kdst = lambda i: KTn.t[:, i, pos0:pos0 + N]; kdb = KTn_b[pos0 // 128]
                cp("act", KTr.t[:, pos0:pos0 + N], krT.t[:, 0:N], [krT.b], [KTr_b[pos0 // 128]])
            for i in range(4):
                ps = nextF()
                mm(ps.t[:, 0:N], Wukv.t[:, 128 * i:128 * i + 128], ckvT.t[:, 0:N], True, True, [Wukv.b, ckvT.b], [ps.b])
                t_ = headnorm(ps, blk64, None, None, "kn")
                stt(kdst(i), ps.t[:, 0:N], gc.t[:, 12 + i:13 + i], t_.t[:, 0:N], ALU.mult, ALU.mult, [ps.b, gc.b, t_.b], [kdb])
            if sample:
                mset(Vn.t[:, :, :, 64:65], 1.0, [Vn.b])
                for s_ in range(4):
                    ps = nextF()
                    mm(ps.t[0:32, 0:512], ckvT.t[:, 32 * s_:32 * s_ + 32], Wukv.t[:, 512:1024], True, True, [ckvT.b, Wukv.b], [ps.b])
                    cp("act", Vn.t[:, s_, :, 0:64], ps.t[0:32, 0:512].rearrange("p (h v) -> p h v", h=8), [ps.b], [Vn.b])
            else:
                for s_ in range(NS):
                    ps = nextF()
                    mm(ps.t[:, 0:512], ckvT.t[:, 128 * s_:128 * s_ + 128], Wukv.t[:, 512:1024], True, True, [ckvT.b, Wukv.b], [ps.b])
                    cp("act", Vc.t[:, pos0 // 128 + s_, :, 0:64], ps.t[:, 0:512].rearrange("p (h v) -> p h v", h=8), [ps.b], [Vc_b[pos0 // 128 + s_]])

            attn = sb_t("attn", [128, 4, NT], BF16)
            sqa = sb_t("sq4", [128, 4, NT], BF16)

            def qview(Qt, p, c0, n):
                return Qt.t[:, p, :].rearrange("q (c n) -> q c n", c=2)[:, :, c0:c0 + n]

            def scores_pair(ps3, p, kn_ap, kr_ap, c0, n, R, Wb_):
                with P.grp("pe"):
                    mm(ps3, kn_ap, qview(QTn, p, c0, n), True, False, R + [QTn.b], [Wb_])
                    mm(ps3, kr_ap, qview(QTr, p, c0, n), False, True, R + [QTr.b], [Wb_])

            def finish_pair(ops, p, c0, n, stride):
                w_ = stride + n
                osb = sb_t("osb", [65, 2 * NT], F32)
                cp("act", osb.t[:, 0:w_], ops.t[0:65, 0:w_], [ops.b], [osb.b])
                dps = nextF()
                mm(dps.t[0:64, 0:w_], sel_f.t[0:65, :], osb.t[0:65, 0:w_], True, True, [sel_f.b, osb.b], [dps.b])
                rd = sb_t("rd", [64, 2 * NT], F32)
                recip(rd.t[:, 0:w_], dps.t[0:64, 0:w_], [dps.b], [rd.b])
                for c_ in range(2):
                    tt("dve", attn.t[64 * c_:64 * c_ + 64, p, c0:c0 + n], osb.t[0:64, c_ * stride:c_ * stride + n], rd.t[:, c_ * stride:c_ * stride + n],
                       ALU.mult, [osb.b, rd.b], [attn.b])

            if not sample:
                qb0 = pos0 // 128
                for p in range(4):
                    ops = nextO()
                    nj = qb0 + NS
                    for j in range(nj):
                        lo = max(j, qb0) - qb0
                        nq_ = N - 128 * lo
                        sp_ = nextF()
                        sp3 = sp_.t[:, 0:2 * nq_].rearrange("q (c n) -> q c n", c=2)
                        scores_pair(sp3, p, KTn.t[:, p, 128 * j:128 * j + 128], KTr.t[:, 128 * j:128 * j + 128], 128 * lo, nq_, [KTn_b[j], KTr_b[j]], sp_.b)
                        ectr[0] += 1
                        E = sb_t("E%d" % (ectr[0] % 3), [128, 2 * NT], BF16)
                        E3 = E.t[:, 0:2 * nq_].rearrange("q (c n) -> q c n", c=2)
                        act(E3, sp3, AF.Exp, [sp_.b], [E.b])
                        if j >= qb0:
                            mset(E.t[64:128, 0:2 * nq_].rearrange("q (c n) -> q c n", c=2)[:, :, 0:64], 0.0, [E.b], e="dve")
                        for c_ in range(2):
                            mm(ops.t[0:65, c_ * NT + 128 * lo:c_ * NT + N], Vc.t[:, j, 2 * p + c_, :], E.t[:, c_ * nq_:(c_ + 1) * nq_], j == 0 and c_ == 0,
                               j == nj - 1, [Vc_b[j], E.b], [ops.b], inc=True)
                    finish_pair(ops, p, 0, N, NT)
            else:
                for s_ in range(4):
                    prep_cache(s_)
                    for p in range(4):
                        ops = nextO()
                        for j in range(9):
                            sp_ = nextF()
                            ectr[0] += 1
                            E = sb_t("E%d" % (ectr[0] % 3), [128, 2 * NT], BF16)
                            if j < 8:
                                sp3 = sp_.t[:, 0:64].rearrange("q (c n) -> q c n", c=2)
                                jj = 8 * (s_ % 2) + j
                                scores_pair(sp3, p, KTc[s_].t[:, p, 128 * jj:128 * jj + 128], KRc[s_].t[:, 128 * jj:128 * jj + 128], 32 * s_, 32, [KTn_b[jj], KTr_b[jj]], sp_.b)
                                act(E.t[:, 0:64], sp_.t[:, 0:64], AF.Exp, [sp_.b], [E.b])
                                for c_ in range(2):
                                    mm(ops.t[0:65, 32 * c_:32 * c_ + 32], Vcc[s_].t[:, jj, 2 * p + c_, :], E.t[:, 32 * c_:32 * c_ + 32], j == 0 and c_ == 0, False, [Vc_b[jj], E.b], [ops.b], inc=True)
                            else:
                                sp3 = sp_.t[0:32, 0:64].rearrange("q (c n) -> q c n", c=2)
                                scores_pair(sp3, p, KTn_new.t[:, p, 32 * s_:32 * s_ + 32], KTr_new.t[:, 32 * s_:32 * s_ + 32], 32 * s_, 32, [KTn_new.b, KTr_new.b], sp_.b)
                                act(E.t[0:32, 0:64], sp_.t[0:32, 0:64], AF.Exp, [sp_.b], [E.b])
                                for c_ in range(2):
                                    mm(ops.t[0:65, 32 * c_:32 * c_ + 32], Vn.t[0:32, s_, 2 * p + c_, :], E.t[0:32, 32 * c_:32 * c_ + 32], False, True, [Vn.b, E.b], [ops.b], inc=True)
                        finish_pair(ops, p, 32 * s_, 32, 32)
            for i in range(4):
                act(sqa.t[:, i, 0:N], attn.t[:, i, 0:N], AF.Square, [attn.b], [sqa.b])
            rs_mla = bc_rstd(sqa, 4, ones512, "rs_mla")
            for i in range(4):
                tt("dve", attn.t[:, i, 0:N], attn.t[:, i, 0:N], sgm.t[:, i, 0:N], ALU.mult, [attn.b, sgm.b], [attn.b])
                tt("dve", mix.t[:, 4 + i, 0:N], attn.t[:, i, 0:N], rs_mla.t[:, 0:N], ALU.mult, [attn.b, rs_mla.b], [mix.b])

            if flags.get('upto', 9) < 4:
                return
            stage[0] = 'B'
            for s_ in range(NS):
                for half in range(2):
                    ps = nextF()
                    with P.grp("pe"):
                        for kt in range(8):
                            mm(ps.t[:, 0:512], mix.t[:, kt, 128 * s_:128 * s_ + 128], Wout.t[:, kt, 512 * half:512 * half + 512], kt == 0, kt == 7, [mix.b, Wout.b], [ps.b])
                    yo = sb_t("yo", [128, 512], F32)
                    tt("dve", yo.t[:], ps.t[:, 0:512], xt[s_].t[:, 512 * half:512 * half + 512], ALU.add, [ps.b, xt[s_].b], [yo.b])
                    P.dma(yout[128 * s_:128 * s_ + 128, 512 * half:512 * half + 512], yo.t[:], [yo.b], [], is_output=True)

        pool_tiles = {}

        tidx = [0]
        DOUBLE = set(flags.get("double", ("hT", "uT", "sg", "sgm", "cq", "ckvT", "krT", "st2", "okv", "okr", "kn_", "r1", "r2", "tb", "ss", "rs",
                                          "Xs", "Hs", "Hb", "yg", "ys_", "sig", "rs_ssm", "rq", "rq2", "hn_s2", "hn_t",
                                          "attn", "rs_mla", "M1", "M2")))

        def sb_t(name, shape, dt):
            key = name + ("_%d" % (tidx[0] % 2) if name in DOUBLE else "")
            if key not in pool_tiles:
                pool_tiles[key] = sb(shape, dt, key)
            return pool_tiles[key]

        KTc, KRc, Vcc = [], [], []
        if flags.get("sample", True):
            ktc = KTn; krc = KTr; vcc = Vc
            for s_ in range(4):
                KTc.append(ktc); KRc.append(krc); Vcc.append(vcc)

            def prep_cache(s_):
                o8 = 8 * (s_ % 2)
                cT = sb_t("cT", [128, 1024], BF16)
                for j in range(8):
                    cl = sb_t("cl", [128, 160], F32)
                    P.dma(cl.t[:, 0:128], cckv[s_, 128 * j:128 * j + 128, :], [], [cl.b])
                    P.dma(cl.t[:, 128:160], ckr[s_, 128 * j:128 * j + 128, :], [], [cl.b])
                    tb = sb_t("tb", [128, 256], BF16)
                    cp("dve", tb.t[:, 0:128], cl.t[:, 0:128], [cl.b], [tb.b])
                    cp("dve", tb.t[:, 128:256].rearrange("p (a b) -> p a b", a=4), cl.t[:, 128:160].unsqueeze(1).to_broadcast([128, 4, 32]), [cl.b], [tb.b])
                    pb = nextB()
                    with P.grp("pe"):
                        tr(pb.t[:, 0:128], tb.t[:, 0:128], ident_b.t[:], [tb.b, ident_b.b], [pb.b], inc=False)
                        tr(pb.t[:, 128:256], tb.t[:, 128:256], ident_b.t[:], [tb.b, ident_b.b], [pb.b])
                    cp("act", cT.t[:, 128 * j:128 * j + 128], pb.t[:, 0:128], [pb.b], [cT.b])
                    cp("act", krc.t[:, 128 * (o8 + j):128 * (o8 + j) + 128], pb.t[:, 128:256], [pb.b], [KTr_b[o8 + j]])
                for j in range(8):
                    ps = nextF()
                    mm(ps.t[:, 0:512], cT.t[:, 128 * j:128 * j + 128], Wukv.t[:, 512:1024], True, True, [cT.b, Wukv.b], [ps.b])
                    cp("act", vcc.t[:, o8 + j, :, 0:64], ps.t[:, 0:512].rearrange("p (h v) -> p h v", h=8), [ps.b], [Vc_b[o8 + j]])
                for i in range(4):
                    for hf in range(2):
                        ps = nextF()
                        mm(ps.t[:, 0:512], Wukv.t[:, 128 * i:128 * i + 128], cT.t[:, 512 * hf:512 * hf + 512], True, True, [Wukv.b, cT.b], [ps.b])
                        s2 = sb_t("sq4", [128, 4, NT], BF16)
                        s2v = s2.t[:, :, :].rearrange("p a b -> p (a b)")
                        act(s2v, ps.t[:, 0:512], AF.Square, [ps.b], [s2.b])
                        p2 = nextF()
                        mm(p2.t[:, 0:512], blk64.t[:], s2v, True, True, [blk64.b, s2.b], [p2.b])
                        t_ = sb_t("kt_", [128, 512], F32)
                        rsqrt_pow(t_.t[:], p2.t[:, 0:512], [p2.b], [t_.b])
                        stt(ktc.t[:, i, 128 * o8 + 512 * hf:128 * o8 + 512 * hf + 512], ps.t[:, 0:512], gc.t[:, 12 + i:13 + i], t_.t[:], ALU.mult, ALU.mult, [ps.b, gc.b, t_.b],
                            KTn_b[o8 + 4 * hf:o8 + 4 * hf + 4])

        if flags.get('sched', True):
            P.start_recording()
        if flags.get("sample", True) and flags.get('upto', 9) >= 1:
            tile_pass(128, xs, ys, ockv_s, okr_s, 1024, 16, True, True, True, 0)
        nseq = flags.get("nseq", 2) if flags.get('upto', 9) >= 1 else 0
        ntile = flags.get("ntile", SEQ // NT)
        for sq_ in range(nseq):
            for it in range(ntile):
                r0 = sq_ * SEQ + it * NT
                tile_pass(NT, xp[r0:r0 + NT, :], yp[r0:r0 + NT, :], ockv_p[r0:r0 + NT, :], okr_p[r0:r0 + NT, :],
                          it * NT, (it * NT) // 128, False, it == 0, it == ntile - 1, sq_)
        if flags.get('sched', True):
            P.sched_slack = flags.get('slack', 0.25)
            P.sched_tabpen = flags.get('tabpen', 1.3)
            P.sched_xlat = flags.get('xlat', 0.3)
            P.schedule_and_emit(flags.get('window', 800))
        P.finish()
    return nc


def _host_weights(inp):
    f = lambda a: np.ascontiguousarray(np.asarray(a, dtype=np.float32))
    W = {}
    W["w_in"] = f(inp["w_in"][0].reshape(8, 128, 1952).transpose(1, 0, 2))
    W["nin"] = f(inp["norm_in"][0].reshape(8, 128).T)
    wuq = np.asarray(inp["w_uq"][0]).reshape(256, 8, 96)
    sw = np.concatenate([np.arange(16, 32), np.arange(0, 16)])
    wq = np.concatenate([wuq[:, :, :64].reshape(256, 512), wuq[:, :, 64:].reshape(256, 256), wuq[:, :, 64:][:, :, sw].reshape(256, 256)], axis=1)
    W["w_uq"] = f(wq.reshape(2, 128, 1024).transpose(1, 0, 2))
    W["nq"] = f(inp["q_lora_norm"][0].reshape(2, 128).T)
    wkv = np.asarray(inp["w_ukv"][0]).reshape(128, 8, 128)
    W["w_ukv"] = f(np.concatenate([wkv[:, :, :64].reshape(128, 512), wkv[:, :, 64:].reshape(128, 512)], axis=1))
    W["w_glu"] = f(inp["w_glu"][0].reshape(4, 128, 512).transpose(1, 0, 2))
    W["b_glu"] = f(inp["b_glu"][0].reshape(4, 128).T)
    W["w_out"] = f(inp["w_out"][0].reshape(8, 128, 1024).transpose(1, 0, 2))
    W["nout"] = f(np.concatenate([inp["out_norm_ssm"][0], inp["out_norm_mla"][0]]).reshape(8, 128).T)
    gcols = np.zeros((128, 24), np.float32)
    qn = np.asarray(inp["q_nope_norm"][0]); qr = np.asarray(inp["q_rope_norm"][0]); kn = np.asarray(inp["k_nope_norm"][0])
    for i in range(4):
        gcols[:, i] = np.tile(qn, 2)
        gcols[:, 12 + i] = np.tile(kn, 2)
        gcols[:, 8 + i] = np.asarray(inp["ssm_d"][0])[128 * i:128 * i + 128]
    gcols[:, 4] = np.tile(qr, 4)
    pidx = np.arange(128)
    gcols[:, 16] = np.tile(qn, 2) * (pidx < 64)
    gcols[:, 17] = np.tile(qn, 2) * (pidx >= 64)
    for k_ in range(4):
        gcols[:, 18 + k_] = (pidx // 32 == k_)
    gcols[:, 5] = np.tile(qr[sw], 4)
    gcols[:, 6] = np.tile(np.concatenate([-np.ones(16), np.ones(16)]), 4)
    gcols[:, 7] = np.tile(np.concatenate([np.arange(16), np.arange(16)]), 4)
    W["gcols"] = gcols
    W["grows"] = f(np.tile(np.concatenate([inp["kv_lora_norm"][0], inp["k_rope_norm"][0]])[None, :], (128, 1)))
    cmx = np.zeros((128, 5, 128), np.float32)
    cmx[:, 0] = np.eye(128)
    p = np.arange(128)
    cmx[:, 1] = (p[:, None] // 64 == p[None, :] // 64) / 64.0
    cmx[:, 2] = (p[:, None] // 32 == p[None, :] // 32) / 32.0
    cmx[:, 3] = 1.0 / 256
    cmx[:, 4] = 1.0 / 512
    W["cm"] = cmx
    sel = np.zeros((128, 64), np.float32); sel[64, :] = 1.0
    W["sel"] = sel
    are = np.asarray(inp["ssm_a_re"][0]); aim = np.asarray(inp["ssm_a_im"][0]); ldt = np.asarray(inp["ssm_log_dt"][0])
    bre = np.asarray(inp["ssm_b_re"][0]); bim = np.asarray(inp["ssm_b_im"][0])
    cre = np.asarray(inp["ssm_c_re"][0]); cim = np.asarray(inp["ssm_c_im"][0])
    def l2(a):
        return a.reshape(16, 2, 64).transpose(1, 2, 0).reshape(128, 16)
    W["a2"] = f(np.stack([l2(are), l2(aim), l2(np.tile(ldt[:, None], (1, 64)))], axis=1))
    def l1(a):
        v = a.reshape(4, 4, 2, 64)
        v = v.transpose(1, 0, 2, 3)
        v = np.broadcast_to(v[:, None, None], (4, 2, 16, 4, 2, 64))
        return v.reshape(128, 512)
    W["a1"] = f(np.stack([l1(are), l1(aim), l1(np.tile(ldt[:, None], (1, 64)))], axis=1))
    def lbt(b):
        v = b.reshape(4, 4, 2, 64, 16)
        out = np.zeros((4, 2, 16, 4, 2, 64), np.float32)
        for g2 in range(2):
            out[:, g2, :, :, g2, :] = v[:, :, g2].transpose(1, 3, 0, 2)
        return out.reshape(128, 512)
    W["bt1"] = f(np.stack([lbt(bre), lbt(bim)], axis=1))
    def lb2(b):
        v = b.reshape(16, 2, 64, 16)
        out = np.zeros((2, 64, 16, 4, 2, 16), np.float32)
        for pr in range(16):
            for g2 in range(2):
                out[g2, :, pr, pr % 4, g2, :] = v[pr, g2]
        return out.reshape(128, 16, 128)
    W["b2"] = f(np.stack([lb2(bre), lb2(bim)], axis=1))
    def lc2(c_):
        v = c_.reshape(16, 2, 16, 64)
        out = np.zeros((2, 64, 16, 2, 16), np.float32)
        for g2 in range(2):
            out[g2, :, :, g2, :] = v[:, g2].transpose(2, 0, 1)
        return out.reshape(128, 16, 32)
    W["c2"] = f(np.stack([lc2(cre), lc2(cim)], axis=1))
    return W


FLAGS = {}


def kernel(**inp):
    inp = {k: np.asarray(v) for k, v in inp.items()}
    W = _host_weights(inp)
    nc = build_program(FLAGS)
    in_maps = []
    for c in range(NCORES):
        m = dict(W)
        m["xp"] = np.ascontiguousarray(inp["x_prompt"][2 * c:2 * c + 2].reshape(2 * SEQ, D))
        m["xs"] = np.ascontiguousarray(inp["x_sample"][4 * c:4 * c + 4].reshape(128, D))
        m["cckv"] = np.ascontiguousarray(inp["cache_ckv"][0, 4 * c:4 * c + 4])
        m["ckr"] = np.ascontiguousarray(inp["cache_krope"][0, 4 * c:4 * c + 4])
        for nm, key in (("h0r", "state_ssm_re"), ("h0i", "state_ssm_im")):
            v = inp[key][0, 4 * c:4 * c + 4].reshape(4, 16, 2, 64)
            m[nm] = np.ascontiguousarray(v.transpose(2, 3, 0, 1).reshape(128, 4, 16))
        in_maps.append(m)
    res = run_bass_kernel_spmd(nc, in_maps, core_ids=list(range(NCORES)))
    R = res.results
    yp = np.concatenate([R[c]["yp"].reshape(2, SEQ, D) for c in range(NCORES)], axis=0)
    ysm = np.concatenate([R[c]["ys"].reshape(4, 32, D) for c in range(NCORES)], axis=0)
    ckv_p = np.concatenate([R[c]["ockv_p"].reshape(2, SEQ, 128) for c in range(NCORES)], axis=0)[None]
    kr_p = np.concatenate([R[c]["okr_p"].reshape(2, SEQ, 32) for c in range(NCORES)], axis=0)[None]
    ckv_s = np.concatenate([R[c]["ockv_s"].reshape(4, 32, 128) for c in range(NCORES)], axis=0)[None]
    kr_s = np.concatenate([R[c]["okr_s"].reshape(4, 32, 32) for c in range(NCORES)], axis=0)[None]

    def unst(a, nseq):
        v = a.reshape(2, 64, nseq, 2, 16).transpose(2, 3, 4, 0, 1)
        v = v.reshape(nseq, 2, 32, 64)
        return v[:, 0], v[:, 1]
    sp_ = [unst(R[c]["ost_p"], 2) for c in range(NCORES)]
    ss_ = [unst(R[c]["ost_s"], 4) for c in range(NCORES)]
    re_p = np.concatenate([a for a, _ in sp_], axis=0)[None]; im_p = np.concatenate([b for _, b in sp_], axis=0)[None]
    re_s = np.concatenate([a for a, _ in ss_], axis=0)[None]; im_s = np.concatenate([b for _, b in ss_], axis=0)[None]
    f = lambda a: np.ascontiguousarray(a, dtype=np.float32)
    return (f(yp), f(ysm), f(ckv_p), f(kr_p), f(re_p), f(im_p), f(ckv_s), f(kr_s), f(re_s), f(im_s))
```

```python
import numpy as np
from contextlib import ExitStack
import concourse.bass as bass
import concourse.mybir as mybir
from concourse.bass_utils import run_bass_kernel_spmd

F32 = mybir.dt.float32
BF16 = mybir.dt.bfloat16
AF = mybir.ActivationFunctionType
ALU = mybir.AluOpType

NCORES = 8
D = 1024
SEQ = 2048
NT = 128
L = 8
EPS = 1e-6
PI = float(np.pi)
ENGS = ("pe", "act", "dve", "pool", "sp")


class Buf:
    __slots__ = ("name", "last_w", "readers", "sem", "semcnt")

    def __init__(self, name):
        self.name = name
        self.last_w = None
        self.readers = []
        self.sem = None
        self.semcnt = 0


class Prog:
    def __init__(self, nc, stack):
        self.nc = nc
        self.stack = stack
        self.eng = {"pe": nc.tensor, "act": nc.scalar, "dve": nc.vector, "pool": nc.gpsimd, "sp": nc.sync}
        self.sem = {e: stack.enter_context(nc.semaphore("s_" + e)) for e in ENGS}
        self.cnt = {e: 0 for e in ENGS}
        self.known = {e: {} for e in ENGS}
        self.vc = {e: [] for e in ENGS}
        self.pending = {e: [] for e in ENGS}
        self.out_events = []
        self.nsem = 0
        self.recording = False
        self.cur = None
        self.nodes = []
        self.dma_latest = {}
        self.sched_slack = 0.0
        self.sched_tabpen = 0.0
        self.sched_xlat = 0.0

    def _need(self, e, dep):
        key, n = dep[0], dep[1]
        if n is None:
            raise RuntimeError("dependency on un-incremented instruction")
        if self.known[e].get(key, 0) >= n:
            return
        if isinstance(key, str):
            self.eng[e].wait_ge(self.sem[key], n)
            snap = self.vc[key][n - 1]
            for g, v in snap.items():
                if v > self.known[e].get(g, 0):
                    self.known[e][g] = v
        else:
            self.eng[e].wait_ge(key, n)
        self.known[e][key] = n

    def _deps(self, e, reads, writes):
        for b in reads:
            if b.last_w is not None:
                self._need(e, b.last_w)
        for b in writes:
            if b.last_w is not None and (b.last_w[0] != e or e != "pe"):
                self._need(e, b.last_w)
            for r in b.readers:
                if r[0] != e or e != "pe":
                    self._need(e, r)

    def start_recording(self):
        self.recording = True
        self.nodes = []
        self.cur = None
        self.rw = {}

    def grp(self, e):
        prog = self

        class _G:
            def __enter__(self_):
                prog.cur = prog._new_node(e)
                return self_

            def __exit__(self_, *a):
                nd = prog.cur
                prog.cur = None
                if nd["ops"]:
                    for k in range(len(nd["ops"]) - 1, -1, -1):
                        if nd["ops"][k][0] == "op":
                            nd["ops"][k][5] = True
                            break
                    prog.nodes.append(nd)
                return False
        return _G()

    def _new_node(self, e):
        return {"e": e, "ops": [], "deps": set(), "cost": 0.0, "lat": 0.0}

    def _rec(self, kind, e, payload, reads, writes, inc, cost, lat=None, tab=None):
        if self.cur is not None:
            nd = self.cur
            assert nd["e"] == e, (nd["e"], e)
        else:
            nd = self._new_node(e)
        me = id(nd)
        for b in reads:
            st_ = self.rw.setdefault(id(b), [None, []])
            if st_[0] is not None and st_[0] is not nd:
                nd["deps"].add(st_[0]["i"] if "i" in st_[0] else None)
        for b in writes:
            st_ = self.rw.setdefault(id(b), [None, []])
            if st_[0] is not None and st_[0] is not nd:
                nd["deps"].add(st_[0].get("i"))
            for r in st_[1]:
                if r is not nd:
                    nd["deps"].add(r.get("i"))
        nd["ops"].append([kind, payload, list(reads), list(writes), None, inc])
        if tab is not None:
            nd["tab"] = tab
        nd["cost"] += cost
        nd["lat"] = max(nd["lat"], lat if lat is not None else 0.0)
        if self.cur is None:
            if kind == "op":
                nd["ops"][-1][5] = True
            nd["i"] = len(self.nodes)
            self.nodes.append(nd)
        else:
            if "i" not in nd:
                nd["i"] = len(self.nodes)
        for b in reads:
            self.rw[id(b)][1].append(nd)
        for b in writes:
            self.rw[id(b)][0] = nd
            self.rw[id(b)][1] = []

    def schedule_and_emit(self, window=48):
        self.recording = False
        nodes = self.nodes
        n = len(nodes)
        for i, nd in enumerate(nodes):
            assert nd["i"] == i, (nd["i"], i)
            nd["deps"].discard(None)
            nd["deps"].discard(i)
        succ = [[] for _ in range(n)]
        left = [0] * n
        for i, nd in enumerate(nodes):
            left[i] = len(nd["deps"])
            for d in nd["deps"]:
                assert d < i
                succ[d].append(i)
        rt = [0.0] * n
        bl = [0.0] * n
        for i in range(n - 1, -1, -1):
            m_ = 0.0
            for j in succ[i]:
                if bl[j] > m_:
                    m_ = bl[j]
            bl[i] = nodes[i]["cost"] + nodes[i]["lat"] + m_
        SLACK = self.sched_slack
        TABPEN = self.sched_tabpen
        cur_tab = [None]
        queues = {e: [i for i in range(n) if nodes[i]["e"] == e] for e in ENGS}
        qpos = {e: 0 for e in ENGS}
        done = [False] * n
        tfree = {e: 0.0 for e in ENGS}
        order = []
        remaining = n
        while remaining:
            best = None
            for e in ENGS:
                q = queues[e]
                p = qpos[e]
                while p < len(q) and done[q[p]]:
                    p += 1
                qpos[e] = p
                cnt = 0
                k = p
                cands_e = []
                while k < len(q) and cnt < window:
                    i = q[k]
                    k += 1
                    if done[i]:
                        continue
                    cnt += 1
                    if left[i] == 0:
                        stt_ = max(tfree[e], rt[i])
                        pen = 0.0
                        if e == "act" and TABPEN > 0:
                            tb_ = nodes[i].get("tab")
                            if tb_ is not None and tb_ != cur_tab[0]:
                                pen = TABPEN
                        cands_e.append((stt_ + pen, i, pen))
                if cands_e:
                    m0 = min(c_[0] for c_ in cands_e)
                    pick = None
                    for c_ in cands_e:
                        if c_[0] <= m0 + SLACK:
                            if pick is None or bl[c_[1]] > bl[pick[1]] + 1e-9 or (abs(bl[c_[1]] - bl[pick[1]]) <= 1e-9 and c_[1] < pick[1]):
                                pick = c_
                    if best is None or pick[0] < best[0] - 1e-9 or (abs(pick[0] - best[0]) <= 1e-9 and pick[1] < best[1]):
                        best = (pick[0], pick[1], e, pick[2])
            assert best is not None, "scheduler deadlock"
            stt_, i, e, pen = best
            nd = nodes[i]
            done[i] = True
            remaining -= 1
            order.append(i)
            if e == "act" and nd.get("tab") is not None:
                cur_tab[0] = nd["tab"]
            tfree[e] = stt_ + nd["cost"]
            fin = stt_ + nd["cost"] + nd["lat"]
            for j in succ[i]:
                left[j] -= 1
                f_ = fin + (self.sched_xlat if nodes[j]["e"] != e else 0.0)
                if f_ > rt[j]:
                    rt[j] = f_
        self.est_makespan = max(tfree.values())
        for i in order:
            nd = nodes[i]
            for kind, payload, reads, writes, _, inc in nd["ops"]:
                if kind == "op":
                    self._emit_op(nd["e"], payload, reads, writes, inc)
                elif kind == "dma":
                    self._emit_dma(*payload)
                elif kind == "selfwait":
                    self._need(nd["e"], (nd["e"], self.cnt[nd["e"]]))
        self.nodes = []

    def selfwait(self, e):
        if getattr(self, "recording", False):
            self._rec("selfwait", e, None, [], [], False, 0.2)
        else:
            self._need(e, (e, self.cnt[e]))

    def op(self, e, fn, reads=(), writes=(), inc=True, cost=0.3, tab=None):
        if getattr(self, "recording", False):
            self._rec("op", e, fn, reads, writes, inc, cost, tab=tab)
            return None
        return self._emit_op(e, fn, reads, writes, inc)

    def _emit_op(self, e, fn, reads=(), writes=(), inc=True):
        self._deps(e, reads, writes)
        ins = fn(self.eng[e])
        if inc:
            self.cnt[e] += 1
            n = self.cnt[e]
            ins.then_inc(self.sem[e], 1)
            snap = dict(self.known[e])
            snap[e] = n
            self.vc[e].append(snap)
            me = (e, n)
            for rec in self.pending[e]:
                rec[1] = n
            self.pending[e] = []
        else:
            me = [e, None]
            self.pending[e].append(me)
        for b in reads:
            b.readers.append(me)
        for b in writes:
            b.last_w = me
            b.readers = []
        return ins

    def dma(self, out, in_, reads=(), writes=(), is_output=False, q="sp", nbytes=65536, **kw):
        if getattr(self, "recording", False):
            self._rec("dma", q, (out, in_, list(reads), list(writes), is_output, q, kw), reads, writes, False, 0.15, lat=3.5 + nbytes / 80e3)
            return None
        return self._emit_dma(out, in_, reads, writes, is_output, q, kw)

    def _emit_dma(self, out, in_, reads, writes, is_output, q, kw):
        self._deps(q, reads, writes)
        owner = writes[0] if (writes and not is_output) else reads[0]
        if owner.sem is None:
            owner.sem = self.stack.enter_context(self.nc.semaphore("d%d" % self.nsem))
            self.nsem += 1
        owner.semcnt += 16
        ins = self.eng[q].dma_start(out=out, in_=in_, **kw)
        ins.then_inc(owner.sem, 16)
        me = (owner.sem, owner.semcnt)
        self.dma_latest[owner.sem] = owner.semcnt
        for b in reads:
            b.readers.append(me)
        for b in writes:
            b.last_w = me
            b.readers = []
        if is_output:
            self.out_events.append(me)
        return ins

    def barrier(self):
        for e in ENGS:
            for sem_, c_ in self.dma_latest.items():
                self._need(e, (sem_, c_))
        for e in ENGS:
            for f in ENGS:
                if f != e and self.cnt[f] > 0:
                    self._need(e, (f, self.cnt[f]))

    def finish(self):
        for ev in self.out_events:
            self._need("sp", ev)


class T:
    __slots__ = ("t", "b")

    def __init__(self, t, name):
        self.t = t
        self.b = Buf(name)


def build_program(flags):
    nc = bass.Bass("TRN2", target_bir_lowering=False)
    dr = {}

    def din(name, shape, dt=F32):
        dr[name] = nc.dram_tensor(name, list(shape), dt, kind="ExternalInput").ap()
        return dr[name]

    def dout(name, shape, dt=F32):
        dr[name] = nc.dram_tensor(name, list(shape), dt, kind="ExternalOutput").ap()
        return dr[name]

    xp = din("xp", [2 * SEQ, D]); xs = din("xs", [128, D])
    cckv = din("cckv", [4, 1024, 128]); ckr = din("ckr", [4, 1024, 32])
    h0r = din("h0r", [128, 4, 16]); h0i = din("h0i", [128, 4, 16])
    w_in = din("w_in", [128, 8, 1952]); nin = din("nin", [128, 8])
    w_uq = din("w_uq", [128, 2, 1024]); nq = din("nq", [128, 2])
    w_ukv = din("w_ukv", [128, 1024])
    w_glu = din("w_glu", [128, 4, 512]); b_glu = din("b_glu", [128, 4])
    w_out = din("w_out", [128, 8, 1024]); nout = din("nout", [128, 8])
    gcols = din("gcols", [128, 24])
    grows = din("grows", [128, 160])
    cm = din("cm", [128, 5, 128])
    sel = din("sel", [128, 64])
    a1 = din("a1", [128, 3, 512])
    bt1 = din("bt1", [128, 2, 512])
    a2 = din("a2", [128, 3, 16])
    b2 = din("b2", [128, 2, 16, 128])
    c2 = din("c2", [128, 2, 16, 32])
    dbg = dout("dbg", [128, 2048]) if flags.get("dbg") else None

    yp = dout("yp", [2 * SEQ, D]); ys = dout("ys", [128, D])
    ockv_p = dout("ockv_p", [2 * SEQ, 128]); okr_p = dout("okr_p", [2 * SEQ, 32])
    ost_p = dout("ost_p", [128, 2, 2, 16])
    ockv_s = dout("ockv_s", [128, 128]); okr_s = dout("okr_s", [128, 32])
    ost_s = dout("ost_s", [128, 4, 2, 16])

    with ExitStack() as st:
        P = Prog(nc, st)
        uid = [0]

        def sb(shape, dt=F32, name=None, stack=st):
            uid[0] += 1
            nm = (name or "t") + "_%d" % uid[0]
            return T(stack.enter_context(nc.sbuf_tensor(nm, list(shape), dt)), nm)

        def psum(shape, dt=F32, name=None):
            uid[0] += 1
            nm = (name or "ps") + "_%d" % uid[0]
            return T(st.enter_context(nc.psum_tensor(nm, list(shape), dt)), nm)

        psF = [psum([128, 512], F32, "psF") for _ in range(7)]
        psB = [psum([128, 1024], BF16, "psB") for _ in range(1)]
        rr = {"F": 0, "B": 0}

        stage = ["F"]
        rr["Bk"] = 0

        def nextF():
            if stage[0] == "F":
                rr["F"] = (rr["F"] + 1) % 2
                return psF[rr["F"]]
            if stage[0] == "S4":
                rr["S4"] = (rr.get("S4", 0) + 1) % 2
                return psF[(5, 6)[rr["S4"]]]
            rr["Bk"] = (rr["Bk"] + 1) % 2
            return psF[2 + rr["Bk"]]

        rr["O"] = 0

        def nextO():
            return psF[4]

        def nextB():
            return psB[0]

        def fsz(ap):
            n = 1
            for d_ in tuple(ap.shape)[1:]:
                n *= d_
            return n

        def ecost(e, ap, mul=1.0):
            n = fsz(ap)
            if e == "act":
                return 0.22 + n / 1200.0
            if e == "dve":
                return 0.08 + mul * n / 900.0
            if e == "pool":
                return 0.25 + n / 450.0
            return 0.3

        def tt(e, out, in0, in1, op, R, W):
            return P.op(e, lambda g: g.tensor_tensor(out=out, in0=in0, in1=in1, op=op), R, W, cost=ecost(e, out))

        def ts(e, out, in0, s1, s2, op0, op1, R, W):
            if s2 is None:
                return P.op(e, lambda g: g.tensor_scalar(out=out, in0=in0, scalar1=s1, scalar2=None, op0=op0), R, W, cost=ecost(e, out))
            return P.op(e, lambda g: g.tensor_scalar(out=out, in0=in0, scalar1=s1, scalar2=s2, op0=op0, op1=op1), R, W, cost=ecost(e, out))

        def stt(out, in0, scalar, in1, op0, op1, R, W):
            return P.op("dve", lambda g: g.scalar_tensor_tensor(out=out, in0=in0, scalar=scalar, in1=in1, op0=op0, op1=op1), R, W, cost=ecost("dve", out))

        def act(out, in_, func, R, W, **kw):
            tab = "T" if func == AF.Tanh else ("L" if func == AF.Ln else None)
            return P.op("act", lambda g: g.activation(out=out, in_=in_, func=func, **kw), R, W, cost=ecost("act", out), tab=tab)

        def cp(e, out, in_, R, W):
            if e == "act":
                return P.op("act", lambda g: g.copy(out=out, in_=in_), R, W, cost=ecost(e, out))
            return P.op(e, lambda g: g.tensor_copy(out=out, in_=in_), R, W, cost=ecost(e, out))

        def recip(out, in_, R, W):
            return P.op("dve", lambda g: g.reciprocal(out=out, in_=in_), R, W, cost=ecost("dve", out, 6.5))

        def rsqrt_pow(out, in_, R, W, scale=1.0, from_psum=True):
            np_ = int(tuple(out.shape)[0])
            act(out, in_, AF.Ln, list(R) + [epsc.b], W, scale=float(scale), bias=epsc.t[0:np_, 0:1])
            act(out, out, AF.Exp, W, W, scale=-0.5)

        def ppow(out, W):
            shp = [int(d_) for d_ in tuple(out.shape)]
            mh = mhalf.t[0:shp[0], 0:1].to_broadcast(shp)
            P.op("pool", lambda g: g.tensor_tensor(out=out, in0=out, in1=mh, op=ALU.pow), list(W) + [mhalf.b], W, cost=ecost("pool", out))

        def mset(ap, val, W, e="pool"):
            return P.op(e, lambda g: g.memset(ap, val), [], W, cost=ecost(e, ap))

        def mm(out, lhsT, rhs, start, stop, R, W, inc=None, **kw):
            if inc is None:
                inc = stop
            ncol = fsz(rhs)
            c_ = 0.035 + max(ncol, 64) * (4.0 if rhs.dtype == F32 else 1.0) / 1600.0
            return P.op("pe", lambda g: g.matmul(out, lhsT=lhsT, rhs=rhs, start=start, stop=stop, **kw), R, W, inc=inc, cost=c_)

        def tr(out, in_, ident, R, W, inc=True):
            return P.op("pe", lambda g: g.transpose(out=out, in_=in_, identity=ident), R, W, inc=inc, cost=0.12)

        ident_b = sb([128, 128], BF16, "ident"); blk64 = sb([128, 128], BF16, "blk64"); blk32 = sb([128, 128], BF16, "blk32")
        ones256 = sb([128, 128], BF16, "o256"); ones512 = sb([128, 128], BF16, "o512")
        ident_f = sb([128, 128], F32, "identf")
        sel_f = sb([128, 64], F32, "sel")
        epsc = sb([128, 1], F32, "eps")
        mhalf = sb([128, 1], F32, "mhalf")
        hbglu = sb([128, 4], F32, "hbglu")
        WinD = nc.dram_tensor("WinD", [128, 14, 1024], BF16).ap(); WinD_b = Buf("WinD")
        WinT = nc.dram_tensor("WinT", [128, 8 * 160], BF16).ap(); WinT_b = Buf("WinT")
        PIECE_COL0 = [0, 128, 256, 384, 512, 640, 768, 896, 1440, 1568, 1696, 1824, 1024, 1152]
        Wuq = sb([128, 2, 1024], BF16, "Wuq")
        Wukv = sb([128, 1024], BF16, "Wukv")
        Wglu = sb([128, 4, 512], BF16, "Wglu")
        Wout = sb([128, 8, 1024], BF16, "Wout")
        bglu = sb([128, 4], F32, "bglu")
        gc = sb([128, 24], F32, "gcols")
        gr = sb([128, 160], F32, "grows")
        Kbd = sb([128, 4, 8, 128], BF16, "Kbd")
        Wb = sb([128, 4, 8, 2, 128], BF16, "Wb")
        Vd = sb([128, 16, 2, 8, 32], BF16, "Vd")
        A8 = sb([128, 2, 16], F32, "A8")
        cosF = sb([128, 2048], BF16, "cosF"); sinF = sb([128, 2048], BF16, "sinF")
        cosT = sb([128, 17, 32], F32, "cosT"); sinT = sb([128, 17, 32], F32, "sinT")

        sA = ExitStack()
        with ExitStack() as s0:
            def sb0(shape, dt=F32, name=None):
                return sb(shape, dt, name, stack=sA)

            cmf = sb0([128, 5, 128], F32, "cmf")
            P.dma(cmf.t[:], cm[:, :, :], [], [cmf.b])
            for i, dst in enumerate((ident_b, blk64, blk32, ones256, ones512)):
                cp("dve", dst.t[:], cmf.t[:, i, :], [cmf.b], [dst.b])
            cp("dve", ident_f.t[:], cmf.t[:, 0, :], [cmf.b], [ident_f.b])
            P.dma(sel_f.t[:], sel[:, :], [], [sel_f.b])
            P.op("pool", lambda g: g.memset(epsc.t[:], EPS), [], [epsc.b])
            P.op("pool", lambda g: g.memset(mhalf.t[:], -0.5), [], [mhalf.b])
            P.dma(bglu.t[:], b_glu[:, :], [], [bglu.b])
            ts("dve", hbglu.t[:], bglu.t[:], 0.5, None, ALU.mult, None, [bglu.b], [hbglu.b])
            P.dma(gc.t[:], gcols[:, :], [], [gc.b])
            P.dma(gr.t[:], grows[:, :], [], [gr.b])
            ts("dve", gc.t[:, 0:6], gc.t[:, 0:6], float(96 ** -0.5), None, ALU.mult, None, [gc.b], [gc.b])
            ts("dve", gc.t[:, 16:18], gc.t[:, 16:18], float(96 ** -0.5), None, ALU.mult, None, [gc.b], [gc.b])

            pre_a1 = sb0([128, 3, 512], F32, "a1"); P.dma(pre_a1.t[:], a1[:, :, :], [], [pre_a1.b])
            pre_bt1 = sb0([128, 2, 512], F32, "bt1"); P.dma(pre_bt1.t[:], bt1[:, :, :], [], [pre_bt1.b])
        scope_holder = [None]

        def sb0(shape, dt=F32, name=None):
            return sb(shape, dt, name, stack=scope_holder[0])

        if True:
            def cgen(aa, F_, npow, tag):
                e = "dve"
                dt_ = sb0([128, F_], F32, tag + "dt"); act(dt_.t[:], aa.t[:, 2, :], AF.Exp, [aa.b], [dt_.b])
                mag = sb0([128, F_], F32, tag + "mag"); th = sb0([128, F_], F32, tag + "th")
                tt(e, mag.t[:], aa.t[:, 0, :], dt_.t[:], ALU.mult, [aa.b, dt_.b], [mag.b])
                act(mag.t[:], mag.t[:], AF.Exp, [mag.b], [mag.b])
                tt(e, th.t[:], aa.t[:, 1, :], dt_.t[:], ALU.mult, [aa.b, dt_.b], [th.b])
                cr = sb0([128, F_], F32, tag + "cr"); ci = sb0([128, F_], F32, tag + "ci")
                t1 = sb0([128, F_], F32, tag + "t1"); t2 = sb0([128, F_], F32, tag + "t2")
                hp = sb0([128, 1], F32, tag + "hp"); P.op("pool", lambda g: g.memset(hp.t[:], PI / 2), [], [hp.b])
                act(ci.t[:], th.t[:], AF.Sin, [th.b], [ci.b], scale=1.0 / 64)
                act(cr.t[:], th.t[:], AF.Sin, [th.b, hp.b], [cr.b], scale=1.0 / 64, bias=hp.t[:, 0:1])
                for _ in range(6):
                    tt(e, t1.t[:], cr.t[:], cr.t[:], ALU.mult, [cr.b], [t1.b])
                    tt(e, t2.t[:], ci.t[:], ci.t[:], ALU.mult, [ci.b], [t2.b])
                    tt(e, ci.t[:], cr.t[:], ci.t[:], ALU.mult, [cr.b, ci.b], [ci.b])
                    ts(e, ci.t[:], ci.t[:], 2.0, None, ALU.mult, None, [ci.b], [ci.b])
                    tt(e, cr.t[:], t1.t[:], t2.t[:], ALU.subtract, [t1.b, t2.b], [cr.b])
                pw = sb0([128, 2, npow + 1, F_], F32, tag + "pw")
                P.op("pool", lambda g: g.memset(pw.t[:, 0, 0, :], 1.0), [], [pw.b])
                P.op("pool", lambda g: g.memset(pw.t[:, 1, 0, :], 0.0), [], [pw.b])
                tt(e, pw.t[:, 0, 1, :], cr.t[:], mag.t[:], ALU.mult, [cr.b, mag.b], [pw.b])
                tt(e, pw.t[:, 1, 1, :], ci.t[:], mag.t[:], ALU.mult, [ci.b, mag.b], [pw.b])
                for m in range(2, npow + 1):
                    cmul(pw.t[:, 0, m, :], pw.t[:, 1, m, :], pw.t[:, 0, m - 1, :], pw.t[:, 1, m - 1, :], pw.t[:, 0, 1, :], pw.t[:, 1, 1, :],
                         [pw.b], [pw.b], t1, t2)
                x_ = sb0([128, F_], F32, tag + "x"); den = sb0([128, F_], F32, tag + "den")
                cf = sb0([128, 2, F_], F32, tag + "cf")
                ts(e, x_.t[:], pw.t[:, 0, 1, :], -1.0, None, ALU.add, None, [pw.b], [x_.b])
                tt(e, den.t[:], aa.t[:, 0, :], aa.t[:, 0, :], ALU.mult, [aa.b], [den.b])
                tt(e, t1.t[:], aa.t[:, 1, :], aa.t[:, 1, :], ALU.mult, [aa.b], [t1.b])
                tt(e, den.t[:], den.t[:], t1.t[:], ALU.add, [den.b, t1.b], [den.b])
                P.op(e, lambda g: g.reciprocal(out=den.t[:], in_=den.t[:]), [den.b], [den.b])
                tt(e, t1.t[:], x_.t[:], aa.t[:, 0, :], ALU.mult, [x_.b, aa.b], [t1.b])
                tt(e, t2.t[:], pw.t[:, 1, 1, :], aa.t[:, 1, :], ALU.mult, [pw.b, aa.b], [t2.b])
                tt(e, t1.t[:], t1.t[:], t2.t[:], ALU.add, [t1.b, t2.b], [t1.b])
                tt(e, cf.t[:, 0, :], t1.t[:], den.t[:], ALU.mult, [t1.b, den.b], [cf.b])
                tt(e, t1.t[:], pw.t[:, 1, 1, :], aa.t[:, 0, :], ALU.mult, [pw.b, aa.b], [t1.b])
                tt(e, t2.t[:], x_.t[:], aa.t[:, 1, :], ALU.mult, [x_.b, aa.b], [t2.b])
                tt(e, t1.t[:], t1.t[:], t2.t[:], ALU.subtract, [t1.b, t2.b], [t1.b])
                tt(e, cf.t[:, 1, :], t1.t[:], den.t[:], ALU.mult, [t1.b, den.b], [cf.b])
                return pw, cf

            def cmul(or_, oi_, ar_, ai_, br_, bi_, R, W, t1, t2, e="dve", neg_im=False):
                sh = tuple(or_.shape)
                a1_ = _view(t1, sh); a2_ = _view(t2, sh)
                tt(e, a1_, ar_, br_, ALU.mult, R, [t1.b])
                tt(e, a2_, ai_, bi_, ALU.mult, R, [t2.b])
                tt(e, or_, a1_, a2_, ALU.subtract, [t1.b, t2.b], W)
                tt(e, a1_, ar_, bi_, ALU.mult, R, [t1.b])
                tt(e, a2_, ai_, br_, ALU.mult, R, [t2.b])
                if neg_im:
                    tt(e, a1_, a1_, a2_, ALU.add, [t1.b, t2.b], [t1.b])
                    ts(e, oi_, a1_, -1.0, None, ALU.mult, None, [t1.b], W)
                else:
                    tt(e, oi_, a1_, a2_, ALU.add, [t1.b, t2.b], W)

            def _view(t, sh):
                n = 1
                for s_ in sh[1:]:
                    n *= s_
                flat = t.t[:, 0:n]
                if len(sh) == 2:
                    return flat
                if len(sh) == 3:
                    return flat.rearrange("p (a b) -> p a b", a=sh[1])
                return flat.rearrange("p (a b c) -> p a b c", a=sh[1], b=sh[2])

        with ExitStack() as s1:
            scope_holder[0] = s1
            a1t = pre_a1
            bt1t = pre_bt1
            pw1, cf1 = cgen(a1t, 512, 7, "g1")
            T1 = sb0([128, 512], F32, "T1"); T2 = sb0([128, 512], F32, "T2")
            bb1 = sb0([128, 2, 512], F32, "bb1")
            cmul(bb1.t[:, 0, :], bb1.t[:, 1, :], cf1.t[:, 0, :], cf1.t[:, 1, :], bt1t.t[:, 0, :], bt1t.t[:, 1, :], [cf1.b, bt1t.b], [bb1.b], T1, T2)
            wtmp = sb0([128, 2, 4, 128], F32, "wtmp")
            for tau in range(8):
                m = 7 - tau
                cmul(wtmp.t[:, 0, :, :].rearrange("p a b -> p (a b)"), wtmp.t[:, 1, :, :].rearrange("p a b -> p (a b)"),
                     pw1.t[:, 0, m, :], pw1.t[:, 1, m, :], bb1.t[:, 0, :], bb1.t[:, 1, :], [pw1.b, bb1.b], [wtmp.b], T1, T2)
                for ri in range(2):
                    cp("act", Wb.t[:, :, tau, ri, :], wtmp.t[:, ri, :, :], [wtmp.b], [Wb.b])

            P.barrier()
        P.barrier()
        sA.close()
        with ExitStack() as s2:
            scope_holder[0] = s2
            a2t = sb0([128, 3, 16], F32, "a2"); P.dma(a2t.t[:], a2[:, :, :], [], [a2t.b])
            b2t = sb0([128, 2, 16, 128], F32, "b2"); P.dma(b2t.t[:], b2[:, :, :, :], [], [b2t.b])
            c2t = sb0([128, 2, 16, 32], F32, "c2"); P.dma(c2t.t[:], c2[:, :, :, :], [], [c2t.b])
            pw2, cf2 = cgen(a2t, 16, 8, "g2")
            ninc = sb0([128, 8], F32, "nin"); nqc = sb0([128, 2], F32, "nq"); noutc = sb0([128, 8], F32, "nout")
            P.dma(ninc.t[:], nin[:, :], [], [ninc.b]); P.dma(nqc.t[:], nq[:, :], [], [nqc.b]); P.dma(noutc.t[:], nout[:, :], [], [noutc.b])
            ts("dve", noutc.t[:, 0:4], noutc.t[:, 0:4], 0.125, None, ALU.mult, None, [noutc.b], [noutc.b])
            ts("dve", noutc.t[:, 4:8], noutc.t[:, 4:8], 0.5, None, ALU.mult, None, [noutc.b], [noutc.b])
            stg = [sb0([128, 2048], F32, "stg") for _ in range(2)]
            si = [0]

            def load_w(dst_ap, src_ap, ncol, gain_ap, e):
                s_ = stg[si[0] % 2]; si[0] += 1
                P.dma(s_.t[:, 0:ncol], src_ap, [], [s_.b])
                if gain_ap is None:
                    cp(e, dst_ap, s_.t[:, 0:ncol], [s_.b], [dstb[0]])
                elif e == "act":
                    act(dst_ap, s_.t[:, 0:ncol], AF.Copy, [s_.b, gainb[0]], [dstb[0]], scale=gain_ap)
                else:
                    ts(e, dst_ap, s_.t[:, 0:ncol], gain_ap, None, ALU.mult, None, [s_.b, gainb[0]], [dstb[0]])

            wst = [sb0([128, 8, 160], F32, "wst") for _ in range(2)]
            wbf = [sb0([128, 8, 160], BF16, "wbf") for _ in range(2)]
            for pi_, c0_ in enumerate(PIECE_COL0 + [1280]):
                w_ = 160 if pi_ == 14 else 128
                a_ = wst[pi_ % 2]; b_ = wbf[pi_ % 2]
                P.dma(a_.t[:, :, 0:w_], w_in[:, :, c0_:c0_ + w_], [], [a_.b])
                for d_ in range(8):
                    act(b_.t[:, d_, 0:w_], a_.t[:, d_, 0:w_], AF.Copy, [a_.b, ninc.b], [b_.b], scale=ninc.t[:, d_:d_ + 1])
                if pi_ < 14:
                    P.dma(WinD[:, pi_, :].rearrange("p (a b) -> p a b", a=8), b_.t[:, :, 0:128], [b_.b], [WinD_b])
                else:
                    P.dma(WinT[:, :].rearrange("p (a b) -> p a b", a=8), b_.t[:, :, 0:160], [b_.b], [WinT_b])
            dstb = [Wuq.b]; gainb = [nqc.b]
            for kt in range(2):
                load_w(Wuq.t[:, kt, :], w_uq[:, kt, :], 1024, nqc.t[:, kt:kt + 1], "act")
            dstb = [Wukv.b]
            load_w(Wukv.t[:, :], w_ukv[:, :], 1024, None, "act")
            dstb = [Wglu.b]
            load_w(Wglu.t[:, :, :].rearrange("p a b -> p (a b)"), w_glu[:, :, :].rearrange("p a b -> p (a b)"), 2048, None, "act")
            dstb = [Wout.b]; gainb = [noutc.b]
            for kt in range(8):
                load_w(Wout.t[:, kt, :], w_out[:, kt, :], 1024, noutc.t[:, kt:kt + 1], "act")
            cp("dve", A8.t[:, 0, :], pw2.t[:, 0, 8, :], [pw2.b], [A8.b])
            cp("dve", A8.t[:, 1, :], pw2.t[:, 1, 8, :], [pw2.b], [A8.b])
            U1 = sb0([128, 2048], F32, "U1"); U2 = sb0([128, 2048], F32, "U2")
            VdF = sb0([128, 16, 2, 9, 32], F32, "VdF")
            for m in range(9):
                cmul(VdF.t[:, :, 0, m, :], VdF.t[:, :, 1, m, :],
                     c2t.t[:, 0, :, :], c2t.t[:, 1, :, :],
                     pw2.t[:, 0, m, :].unsqueeze(2).to_broadcast([128, 16, 32]), pw2.t[:, 1, m, :].unsqueeze(2).to_broadcast([128, 16, 32]),
                     [c2t.b, pw2.b], [VdF.b], U1, U2, neg_im=True)
            for ri in range(2):
                for pr in range(16):
                    cp("act" if pr % 2 else "pool", Vd.t[:, pr, ri, :, :], VdF.t[:, pr, ri, 1:9, :], [VdF.b], [Vd.b])
            bb2 = sb0([128, 2, 16, 128], F32, "bb2")
            cmul(bb2.t[:, 0, :, :], bb2.t[:, 1, :, :],
                 cf2.t[:, 0, :].unsqueeze(2).to_broadcast([128, 16, 128]), cf2.t[:, 1, :].unsqueeze(2).to_broadcast([128, 16, 128]),
                 b2t.t[:, 0, :, :], b2t.t[:, 1, :, :], [cf2.b, b2t.b], [bb2.b], U1, U2)
            for ct in range(4):
                for lg in range(2):
                    kp = nextF()
                    for p4 in range(4):
                        pr = ct * 4 + p4
                        for ri in range(2):
                            mm(kp.t[:, 128 * p4:128 * p4 + 128], bb2.t[:, ri, pr, :], VdF.t[:, pr, ri, 4 * lg:4 * lg + 4, :],
                               ri == 0, ri == 1, [bb2.b, VdF.b], [kp.b])
                    kv4 = kp.t[:, :].rearrange("p (a l c) -> p a l c", a=4, l=4)
                    if lg == 0:
                        stt(Kbd.t[:, ct, 0, :].rearrange("p (a c) -> p a c", a=4), ident_f.t[:].rearrange("p (a c) -> p a c", a=4),
                            gc.t[:, 8 + ct:9 + ct], kv4[:, :, 0, :], ALU.mult, ALU.add, [ident_f.b, gc.b, kp.b], [Kbd.b])
                        for l_ in range(1, 4):
                            cp("dve", Kbd.t[:, ct, l_, :].rearrange("p (a c) -> p a c", a=4), kv4[:, :, l_, :], [kp.b], [Kbd.b])
                    else:
                        for l_ in range(4):
                            cp("dve", Kbd.t[:, ct, 4 + l_, :].rearrange("p (a c) -> p a c", a=4), kv4[:, :, l_, :], [kp.b], [Kbd.b])
            P.barrier()
        P.barrier()
        with ExitStack() as s3:
            scope_holder[0] = s3
            T1 = sb0([128, 1024], F32, "T1"); T2 = sb0([128, 1024], F32, "T2")
            cF = sb0([128, 2048], F32, "cF"); sF = sb0([128, 2048], F32, "sF")
            inv = sb0([128, 1], F32, "inv"); wv = sb0([128, 4], F32, "wv"); hp2 = sb0([128, 1], F32, "hp2")
            P.op("pool", lambda g: g.memset(hp2.t[:], PI / 2), [], [hp2.b])
            act(inv.t[:], gc.t[:, 7:8], AF.Exp, [gc.b], [inv.b], scale=float(-np.log(10000.0) / 16))
            act(wv.t[:, 1:2], inv.t[:], AF.Sin, [inv.b], [wv.b])
            act(wv.t[:, 0:1], inv.t[:], AF.Sin, [inv.b, hp2.b], [wv.b], bias=hp2.t[:, 0:1])
            P.op("pool", lambda g: g.memset(cF.t[:, 0:1], 1.0), [], [cF.b])
            P.op("pool", lambda g: g.memset(sF.t[:, 0:1], 0.0), [], [sF.b])
            for k in range(11):
                n = 1 << k
                ts("dve", T1.t[:, 0:n], sF.t[:, 0:n], wv.t[:, 1:2], None, ALU.mult, None, [sF.b, wv.b], [T1.b])
                ts("dve", T2.t[:, 0:n], cF.t[:, 0:n], wv.t[:, 1:2], None, ALU.mult, None, [cF.b, wv.b], [T2.b])
                stt(cF.t[:, n:2 * n], cF.t[:, 0:n], wv.t[:, 0:1], T1.t[:, 0:n], ALU.mult, ALU.subtract, [cF.b, wv.b, T1.b], [cF.b])
                stt(sF.t[:, n:2 * n], sF.t[:, 0:n], wv.t[:, 0:1], T2.t[:, 0:n], ALU.mult, ALU.add, [sF.b, wv.b, T2.b], [sF.b])
                tt("dve", wv.t[:, 2:3], wv.t[:, 0:1], wv.t[:, 0:1], ALU.mult, [wv.b], [wv.b])
                tt("dve", wv.t[:, 3:4], wv.t[:, 1:2], wv.t[:, 1:2], ALU.mult, [wv.b], [wv.b])
                tt("dve", wv.t[:, 1:2], wv.t[:, 0:1], wv.t[:, 1:2], ALU.mult, [wv.b], [wv.b])
                ts("dve", wv.t[:, 1:2], wv.t[:, 1:2], 2.0, None, ALU.mult, None, [wv.b], [wv.b])
                tt("dve", wv.t[:, 0:1], wv.t[:, 2:3], wv.t[:, 3:4], ALU.subtract, [wv.b], [wv.b])
            ts("dve", sF.t[:], sF.t[:], gc.t[:, 6:7], None, ALU.mult, None, [sF.b, gc.b], [sF.b])
            cp("act", cosF.t[:], cF.t[:], [cF.b], [cosF.b])
            cp("act", sinF.t[:], sF.t[:], [sF.b], [sinF.b])
            for src, dst in ((cF, cosT), (sF, sinT)):
                for t_ in range(17):
                    pt = nextF()
                    if t_ < 16:
                        in_ap = src.t[:, 128 * t_:128 * t_ + 128]
                        rb_ = src.b
                    else:
                        cp("dve", T1.t[:, 0:128].rearrange("p (a b) -> p a b", a=4), src.t[:, 1024:1056].unsqueeze(1).to_broadcast([128, 4, 32]), [src.b], [T1.b])
                        in_ap = T1.t[:, 0:128]
                        rb_ = T1.b
                    P.op("pe", lambda g: g.transpose(out=pt.t[:, 0:128], in_=in_ap, identity=ident_f.t[:]), [rb_, ident_f.b], [pt.b])
                    cp("dve", dst.t[:, t_, :], pt.t[:, 0:32], [pt.b], [dst.b])
            P.barrier()
        P.barrier()

        KTn = sb([128, 4, SEQ + 0], BF16, "KTn"); KTr = sb([128, SEQ], BF16, "KTr")
        Vc = sb([128, 16, 8, 65], BF16, "Vc")
        KTn_b = [Buf("KTn%d" % j_) for j_ in range(16)]; KTr_b = [Buf("KTr%d" % j_) for j_ in range(16)]; Vc_b = [Buf("Vc%d" % j_) for j_ in range(16)]
        P.op("pool", lambda g: g.memset(Vc.t[:, :, :, 64:65], 1.0), [], Vc_b)
        Hst = sb([128, 2, 16], F32, "Hst")
        xin = [sb([128, D], F32, "xin") for _ in range(2)]
        xslot = [0]
        ectr = [0]
        wctr = [0]

        def tile_pass(N, xsrc, yout, ckv_out, kr_out, pos0, tabidx0, sample, seq_first, seq_last, seqidx):
            NS = N // 128
            NC = N // L
            stage[0] = 'F'
            tidx[0] += 1
            xt = []
            for s_ in range(NS):
                x_ = xin[xslot[0] % 2]; xslot[0] += 1
                P.dma(x_.t[:], xsrc[128 * s_:128 * s_ + 128, :], [], [x_.b])
                xt.append(x_)
            ss = sb_t("ss", [128, 4], F32); junk = sb_t("xn", [128, D], BF16)
            for s_ in range(NS):
                act(junk.t[:], xt[s_].t[:], AF.Square, [xt[s_].b], [junk.b, ss.b], accum_out=ss.t[:, s_:s_ + 1])
            rs = sb_t("rs", [128, 4], F32)
            rsqrt_pow(rs.t[:, 0:NS], ss.t[:, 0:NS], [ss.b], [rs.b], scale=1.0 / D)
            hT = sb_t("hT", [128, 8, NT], BF16)
            xn = sb_t("xn", [128, D], BF16)
            for s_ in range(NS):
                ts("dve", xn.t[:], xt[s_].t[:], rs.t[:, s_:s_ + 1], None, ALU.mult, None, [xt[s_].b, rs.b], [xn.b])
                pb = nextB()
                with P.grp("pe"):
                    for d_ in range(8):
                        tr(pb.t[:, 128 * d_:128 * d_ + 128], xn.t[:, 128 * d_:128 * d_ + 128], ident_b.t[:], [xn.b, ident_b.b], [pb.b], inc=(d_ == 7))
                cp("act", hT.t[:, :, 128 * s_:128 * s_ + 128], pb.t[:, :].rearrange("p (a b) -> p a b", a=8), [pb.b], [hT.b])

            def proj_fm(col0, M):
                ps = nextF()
                wctr[0] += 1
                wp = sb_t("wp%d" % (wctr[0] % 3), [128, 8, 128], BF16)
                P.dma(wp.t[:], WinD[:, PIECE_COL0.index(col0), :].rearrange("p (a b) -> p a b", a=8), [WinD_b], [wp.b], nbytes=262144)
                with P.grp("pe"):
                    for d_ in range(8):
                        mm(ps.t[0:M, 0:N], wp.t[:, d_, 0:M], hT.t[:, d_, 0:N], d_ == 0, d_ == 7, [wp.b, hT.b], [ps.b])
                return ps

            uT = sb_t("uT", [128, 4, NT], BF16)
            ubd = sb_t("ubd", [128, 4, 4, NT], BF16)
            sg = sb_t("sg", [128, 4, NT], BF16)
            sgm = sb_t("sgm", [128, 4, NT], BF16)
            cq = sb_t("cq", [128, 2, NT], BF16)
            for i in range(4):
                ps = proj_fm(128 * i, 128)
                cp("dve", uT.t[:, i, 0:N], ps.t[:, 0:N], [ps.b], [uT.b])
                for k_ in range(4):
                    ts("dve", ubd.t[:, i, k_, 0:N], ps.t[:, 0:N], gc.t[:, 18 + k_:19 + k_], None, ALU.mult, None, [ps.b, gc.b], [ubd.b])
            for i in range(4):
                ps = proj_fm(512 + 128 * i, 128)
                th_ = sb_t("jk", [128, 160], F32)
                act(th_.t[:, 0:N], ps.t[:, 0:N], AF.Tanh, [ps.b], [th_.b], scale=0.5)
                stt(sg.t[:, i, 0:N], th_.t[:, 0:N], 1.0, ps.t[:, 0:N], ALU.add, ALU.mult, [th_.b, ps.b], [sg.b])
            for i in range(4):
                ps = proj_fm(1440 + 128 * i, 128)
                th_ = sb_t("jk", [128, 160], F32)
                act(th_.t[:, 0:N], ps.t[:, 0:N], AF.Tanh, [ps.b], [th_.b], scale=0.5)
                stt(sgm.t[:, i, 0:N], th_.t[:, 0:N], 1.0, ps.t[:, 0:N], ALU.add, ALU.mult, [th_.b, ps.b], [sgm.b])
            for i in range(2):
                ps = proj_fm(1024 + 128 * i, 128)
                cp("dve", cq.t[:, i, 0:N], ps.t[:, 0:N], [ps.b], [cq.b])
            ckvT = sb_t("ckvT", [128, NT], BF16)
            krT = sb_t("krT", [128, NT], BF16)
            for s_ in range(NS):
                ps = nextF()
                wt_ = sb_t("wtm", [128, 8, 160], BF16)
                if s_ == 0:
                    P.dma(wt_.t[:], WinT[:, :].rearrange("p (a b) -> p a b", a=8), [WinT_b], [wt_.b], nbytes=327680)
                with P.grp("pe"):
                    for d_ in range(8):
                        mm(ps.t[:, 0:160], hT.t[:, d_, 128 * s_:128 * s_ + 128], wt_.t[:, d_, :], d_ == 0, d_ == 7, [wt_.b, hT.b], [ps.b])
                st2 = sb_t("st2", [128, 4], F32); jk = sb_t("jk", [128, 160], F32)
                act(jk.t[:, 0:128], ps.t[:, 0:128], AF.Square, [ps.b], [jk.b, st2.b], accum_out=st2.t[:, 0:1])
                act(jk.t[:, 128:160], ps.t[:, 128:160], AF.Square, [ps.b], [jk.b, st2.b], accum_out=st2.t[:, 1:2])
                rsqrt_pow(st2.t[:, 2:3], st2.t[:, 0:1], [st2.b], [st2.b], scale=1.0 / 128)
                rsqrt_pow(st2.t[:, 3:4], st2.t[:, 1:2], [st2.b], [st2.b], scale=1.0 / 32)
                okv = sb_t("okv", [128, 128], F32); okr = sb_t("okr", [128, 32], F32); kn_ = sb_t("kn_", [128, 32], F32)
                stt(okv.t[:], ps.t[:, 0:128], st2.t[:, 2:3], gr.t[:, 0:128], ALU.mult, ALU.mult, [ps.b, st2.b, gr.b], [okv.b])
                stt(kn_.t[:], ps.t[:, 128:160], st2.t[:, 3:4], gr.t[:, 128:160], ALU.mult, ALU.mult, [ps.b, st2.b, gr.b], [kn_.b])
                ti = tabidx0 + s_
                r1 = sb_t("r1", [128, 32], F32); r2 = sb_t("r2", [128, 32], F32)
                tt("dve", r1.t[:], kn_.t[:], cosT.t[:, ti, :], ALU.mult, [kn_.b, cosT.b], [r1.b])
                tt("dve", r2.t[:, 0:16], kn_.t[:, 16:32], sinT.t[:, ti, 0:16], ALU.mult, [kn_.b, sinT.b], [r2.b])
                tt("dve", r2.t[:, 16:32], kn_.t[:, 0:16], sinT.t[:, ti, 16:32], ALU.mult, [kn_.b, sinT.b], [r2.b])
                tt("dve", okr.t[:], r1.t[:], r2.t[:], ALU.add, [r1.b, r2.b], [okr.b])
                P.dma(ckv_out[128 * s_:128 * s_ + 128, :], okv.t[:], [okv.b], [], is_output=True)
                P.dma(kr_out[128 * s_:128 * s_ + 128, :], okr.t[:], [okr.b], [], is_output=True)
                tb = sb_t("tb", [128, 256], BF16)
                cp("dve", tb.t[:, 0:128], okv.t[:], [okv.b], [tb.b])
                cp("dve", tb.t[:, 128:256].rearrange("p (a b) -> p a b", a=4), okr.t[:, :].unsqueeze(1).to_broadcast([128, 4, 32]), [okr.b], [tb.b])
                pb = nextB()
                with P.grp("pe"):
                    tr(pb.t[:, 0:128], tb.t[:, 0:128], ident_b.t[:], [tb.b, ident_b.b], [pb.b], inc=False)
                    tr(pb.t[:, 128:256], tb.t[:, 128:256], ident_b.t[:], [tb.b, ident_b.b], [pb.b])
                cp("act", ckvT.t[:, 128 * s_:128 * s_ + 128], pb.t[:, 0:128], [pb.b], [ckvT.b])
                cp("act", krT.t[:, 128 * s_:128 * s_ + 128], pb.t[:, 128:256], [pb.b], [krT.b])

            if flags.get('upto', 9) < 2:
                return
            stage[0] = 'B'
            Xs = sb_t("Xs", [128, 2, 16, NT // L], F32)
            Hs = sb_t("Hs", [128, 2, 16, NT // L + 4], F32)
            Hb = sb_t("Hb", [128, 2, 16, NT // L], BF16)
            for ri in range(2):
                xps = [nextF(), nextF()]
                for ct in range(4):
                    ps = xps[ct // 2]
                    c0_ = 4 * NC * (ct % 2)
                    with P.grp("pe"):
                        for tau in range(8):
                            mm(ps.t[:, c0_:c0_ + 4 * NC].rearrange("q (a k) -> q a k", a=4), Wb.t[:, ct, tau, ri, :], ubd.t[:, ct, :, tau:N:L],
                               tau == 0, tau == 7, [Wb.b, ubd.b], [ps.b])
                for hf in range(2):
                    cp("dve" if hf else "act", Xs.t[:, ri, 8 * hf:8 * hf + 8, 0:NC], xps[hf].t[:, 0:8 * NC].rearrange("p (a b) -> p a b", a=8), [xps[hf].b], [Xs.b])
            if flags.get('s3', 9) < 2:
                return
            SCAN_E = flags.get('scan_engine', 'pool')
            M1 = sb_t("M1", [128, 2, 16], F32); M2 = sb_t("M2", [128, 2, 16], F32)
            A8r = A8.t[:, 0, :].unsqueeze(1).to_broadcast([128, 2, 16]); A8i = A8.t[:, 1, :].unsqueeze(1).to_broadcast([128, 2, 16])
            segs = [(0, NC)] if not sample else [(4 * s_, 4) for s_ in range(4)]
            for sgi, (k0, nk) in enumerate(segs):
                base = k0 + sgi
                if sample:
                    h0t = sb_t("h0t", [128, 4, 2, 16], F32)
                    if sgi == 0:
                        P.dma(h0t.t[:, :, 0, :], h0r[:, :, :], [], [h0t.b])
                        P.dma(h0t.t[:, :, 1, :], h0i[:, :, :], [], [h0t.b])
                    cp(SCAN_E, Hs.t[:, :, :, base], h0t.t[:, sgi, :, :], [h0t.b], [Hs.b])
                elif seq_first:
                    mset(Hs.t[:, :, :, base], 0.0, [Hs.b], e=SCAN_E)
                else:
                    cp(SCAN_E, Hs.t[:, :, :, base], Hst.t[:, :, :], [Hst.b], [Hs.b])
                for j in range(nk):
                    hp_ = Hs.t[:, :, :, base + j]; hn_ = Hs.t[:, :, :, base + j + 1]
                    tt(SCAN_E, M1.t[:], hp_, A8r, ALU.mult, [Hs.b, A8.b], [M1.b])
                    tt(SCAN_E, M2.t[:], hp_, A8i, ALU.mult, [Hs.b, A8.b], [M2.b])
                    tt(SCAN_E, M1.t[:], M1.t[:], Xs.t[:, :, :, k0 + j], ALU.add, [M1.b, Xs.b], [M1.b])
                    tt(SCAN_E, Hs.t[:, 0, :, base + j + 1], M1.t[:, 0, :], M2.t[:, 1, :], ALU.subtract, [M1.b, M2.b], [Hs.b])
                    tt(SCAN_E, Hs.t[:, 1, :, base + j + 1], M1.t[:, 1, :], M2.t[:, 0, :], ALU.add, [M1.b, M2.b], [Hs.b])
                cp("dve", Hb.t[:, :, :, k0:k0 + nk], Hs.t[:, :, :, base:base + nk], [Hs.b], [Hb.b])
                if sample:
                    hso = sb_t("hso", [128, 4, 2, 16], F32)
                    cp(SCAN_E, hso.t[:, sgi, :, :], Hs.t[:, :, :, base + nk], [Hs.b], [hso.b])
                    if sgi == 3:
                        P.dma(ost_s[:, :, :, :], hso.t[:], [hso.b], [], is_output=True)
                else:
                    cp(SCAN_E, Hst.t[:, :, :], Hs.t[:, :, :, base + nk], [Hs.b], [Hst.b])
                    if seq_last:
                        hpo = sb_t("hpo", [128, 2, 16], F32)
                        cp(SCAN_E, hpo.t[:], Hst.t[:], [Hst.b], [hpo.b])
                        P.dma(ost_p[:, seqidx, :, :], hpo.t[:], [hpo.b], [], is_output=True)
            if flags.get('s3', 9) < 3:
                return
            yg = sb_t("yg", [128, 4, NT], BF16)
            for ct in range(4):
                yp_ = nextF()
                for tau in range(8):
                    o_ = yp_.t[:, NC * tau:NC * tau + NC]
                    with P.grp("pe"):
                        for lag in range(tau + 1):
                            mm(o_, Kbd.t[:, ct, lag, :], uT.t[:, ct, tau - lag:N:L], lag == 0, False, [Kbd.b, uT.b], [yp_.b], inc=False)
                        for p4 in range(4):
                            pr = 4 * ct + p4
                            for ri in range(2):
                                last = (p4 == 3 and ri == 1)
                                kw = {"tile_position": (0, 96)} if p4 == 3 else {}
                                mm(yp_.t[32 * p4:32 * p4 + 32, NC * tau:NC * tau + NC], Vd.t[:, pr, ri, tau, :], Hb.t[:, ri, pr, 0:NC], False, last,
                                   [Vd.b, Hb.b], [yp_.b], inc=last, **kw)
                g1 = sb_t("rs_ssm", [128, NT], F32); g2 = sb_t("sig", [128, NT], BF16)
                act(g1.t[:, 0:N], yp_.t[:, 0:N], AF.Square, [yp_.b], [g1.b])
                ts("dve", g1.t[:, 0:N], g1.t[:, 0:N], 0.044715, 1.0, ALU.mult, ALU.add, [g1.b], [g1.b])
                tt("dve", g1.t[:, 0:N], g1.t[:, 0:N], yp_.t[:, 0:N], ALU.mult, [g1.b, yp_.b], [g1.b])
                act(g2.t[:, 0:N], g1.t[:, 0:N], AF.Tanh, [g1.b], [g2.b], scale=0.7978845608028654)
                stt(yg.t[:, ct, 0:N].rearrange("p (k t) -> p t k", t=L), g2.t[:, 0:N].rearrange("p (t k) -> p t k", t=L), 1.0,
                    yp_.t[:, 0:N].rearrange("p (t k) -> p t k", t=L), ALU.add, ALU.mult, [g2.b, yp_.b], [yg.b])
            if flags.get('s3', 9) < 5:
                return
            ys_ = sb_t("ys_", [128, 4, NT], BF16)
            sq = sb_t("sq", [128, 4, NT], BF16)
            for co in range(4):
                ps = nextF()
                with P.grp("pe"):
                    for ci_ in range(4):
                        mm(ps.t[:, 0:N], Wglu.t[:, ci_, 128 * co:128 * co + 128], yg.t[:, ci_, 0:N], ci_ == 0, ci_ == 3, [Wglu.b, yg.b], [ps.b])
                sig = sb_t("sig", [128, NT], BF16)
                act(sig.t[:, 0:N], ps.t[:, 0:N], AF.Tanh, [ps.b, hbglu.b], [sig.b], bias=hbglu.t[:, co:co + 1], scale=0.25)
                stt(ys_.t[:, co, 0:N], sig.t[:, 0:N], 1.0, yg.t[:, co, 0:N], ALU.add, ALU.mult, [sig.b, yg.b], [ys_.b])
                act(sq.t[:, co, 0:N], ys_.t[:, co, 0:N], AF.Square, [ys_.b], [sq.b])
            def bc_rstd(sqt, ntile, onesT, name, scale=1.0):
                ps = nextF()
                with P.grp("pe"):
                    for i in range(ntile):
                        mm(ps.t[:, 0:N], onesT.t[:], sqt.t[:, i, 0:N], i == 0, i == ntile - 1, [onesT.b, sqt.b], [ps.b])
                r_ = sb_t(name, [128, NT], F32)
                rsqrt_pow(r_.t[:, 0:N], ps.t[:, 0:N], [ps.b], [r_.b], scale=scale)
                return r_
            rs_ssm = bc_rstd(sq, 4, ones512, "rs_ssm", scale=1.0 / 16)
            mix = sb_t("mix", [128, 8, NT], BF16)
            for ct in range(4):
                tt("dve", ys_.t[:, ct, 0:N], ys_.t[:, ct, 0:N], sg.t[:, ct, 0:N], ALU.mult, [ys_.b, sg.b], [ys_.b])
                tt("dve", mix.t[:, ct, 0:N], ys_.t[:, ct, 0:N], rs_ssm.t[:, 0:N], ALU.mult, [ys_.b, rs_ssm.b], [mix.b])

            if flags.get('upto', 9) < 3:
                return
            stage[0] = 'S4'
            sqq = sb_t("sq4", [128, 4, NT], BF16)
            for i in range(2):
                act(sqq.t[:, i, 0:N], cq.t[:, i, 0:N], AF.Square, [cq.b], [sqq.b])
            rq = bc_rstd(sqq, 2, ones256, "rq")
            rq2 = sb_t("rq2", [128, NT], F32)
            tt("dve", rq2.t[:, 0:N], rq.t[:, 0:N], rq.t[:, 0:N], ALU.mult, [rq.b], [rq2.b])
            QTn = sb_t("QTn", [128, 4, 2 * NT], BF16); QTr = sb_t("QTr", [128, 4, 2 * NT], BF16)

            def headnorm(ps, blk, rq_, rq2_, name):
                s2 = sb_t("hn_s2", [128, NT], BF16)
                act(s2.t[:, 0:N], ps.t[:, 0:N], AF.Square, [ps.b], [s2.b])
                p2 = nextF()
                mm(p2.t[:, 0:N], blk.t[:], s2.t[:, 0:N], True, True, [blk.b, s2.b], [p2.b])
                t_ = sb_t("hn_t", [128, NT], F32)
                if rq_ is not None:
                    tt("dve", t_.t[:, 0:N], p2.t[:, 0:N], rq2_.t[:, 0:N], ALU.mult, [p2.b, rq2_.b], [t_.b])
                    rsqrt_pow(t_.t[:, 0:N], t_.t[:, 0:N], [t_.b], [t_.b])
                else:
                    rsqrt_pow(t_.t[:, 0:N], p2.t[:, 0:N], [p2.b], [t_.b])
                if rq_ is not None:
                    tt("dve", t_.t[:, 0:N], t_.t[:, 0:N], rq_.t[:, 0:N], ALU.mult, [t_.b, rq_.b], [t_.b])
                return t_

            def qproj(col0):
                ps = nextF()
                with P.grp("pe"):
                    for kt in range(2):
                        mm(ps.t[:, 0:N], Wuq.t[:, kt, col0:col0 + 128], cq.t[:, kt, 0:N], kt == 0, kt == 1, [Wuq.b, cq.b], [ps.b])
                return ps

            for i in range(4):
                ps = qproj(128 * i)
                t_ = headnorm(ps, blk64, rq, rq2, "qn")
                stt(QTn.t[:, i, 0:N], ps.t[:, 0:N], gc.t[:, 16:17], t_.t[:, 0:N], ALU.mult, ALU.mult, [ps.b, gc.b, t_.b], [QTn.b])
                stt(QTn.t[:, i, NT:NT + N], ps.t[:, 0:N], gc.t[:, 17:18], t_.t[:, 0:N], ALU.mult, ALU.mult, [ps.b, gc.b, t_.b], [QTn.b])
            if sample:
                cos_q = cosF.t[:, 1024:1056].unsqueeze(1).to_broadcast([128, 4, 32]); sin_q = sinF.t[:, 1024:1056].unsqueeze(1).to_broadcast([128, 4, 32])
                vq = lambda ap: ap.rearrange("p (a b) -> p a b", a=4)
            else:
                cos_q = cosF.t[:, pos0:pos0 + N]; sin_q = sinF.t[:, pos0:pos0 + N]
                vq = lambda ap: ap
            for i in range(2):
                ps = qproj(512 + 128 * i)
                t_ = headnorm(ps, blk32, rq, rq2, "qr")
                qa = sb_t("qa", [128, NT], F32); qb = sb_t("qb", [128, NT], F32)
                stt(qa.t[:, 0:N], ps.t[:, 0:N], gc.t[:, 4:5], t_.t[:, 0:N], ALU.mult, ALU.mult, [ps.b, gc.b, t_.b], [qa.b])
                ps2 = qproj(768 + 128 * i)
                stt(qb.t[:, 0:N], ps2.t[:, 0:N], gc.t[:, 5:6], t_.t[:, 0:N], ALU.mult, ALU.mult, [ps2.b, gc.b, t_.b], [qb.b])
                tt("dve", vq(qa.t[:, 0:N]), vq(qa.t[:, 0:N]), cos_q, ALU.mult, [qa.b, cosF.b], [qa.b])
                tt("dve", vq(qb.t[:, 0:N]), vq(qb.t[:, 0:N]), sin_q, ALU.mult, [qb.b, sinF.b], [qb.b])
                tt("dve", qa.t[:, 0:N], qa.t[:, 0:N], qb.t[:, 0:N], ALU.add, [qa.b, qb.b], [qa.b])
                for k_ in range(4):
                    h_ = 4 * i + k_
                    ts("dve", QTr.t[:, h_ // 2, (h_ % 2) * NT:(h_ % 2) * NT + N], qa.t[:, 0:N], gc.t[:, 18 + k_:19 + k_], None, ALU.mult, None,
                       [qa.b, gc.b], [QTr.b])

            if sample:
                KTn_new = sb_t("KTnn", [128, 4, 128], BF16); KTr_new = krT
                Vn = sb_t("Vn", [32, 4, 8, 65], BF16)
                kdst = lambda i: KTn_new.t[:, i, 0:N]; kdb = KTn_new.b
            else:
                kdst = lambda i: KTn.t[:, i, pos0:pos0 + N]; kdb = KTn_b[pos0 // 128]
                cp("act", KTr.t[:, pos0:pos0 + N], krT.t[:, 0:N], [krT.b], [KTr_b[pos0 // 128]])
            for i in range(4):
                ps = nextF()
                mm(ps.t[:, 0:N], Wukv.t[:, 128 * i:128 * i + 128], ckvT.t[:, 0:N], True, True, [Wukv.b, ckvT.b], [ps.b])
                t_ = headnorm(ps, blk64, None, None, "kn")
                stt(kdst(i), ps.t[:, 0:N], gc.t[:, 12 + i:13 + i], t_.t[:, 0:N], ALU.mult, ALU.mult, [ps.b, gc.b, t_.b], [kdb])
            if sample:
                mset(Vn.t[:, :, :, 64:65], 1.0, [Vn.b])
                for s_ in range(4):
                    ps = nextF()
                    mm(ps.t[0:32, 0:512], ckvT.t[:, 32 * s_:32 * s_ + 32], Wukv.t[:, 512:1024], True, True, [ckvT.b, Wukv.b], [ps.b])
                    cp("act", Vn.t[:, s_, :, 0:64], ps.t[0:32, 0:512].rearrange("p (h v) -> p h v", h=8), [ps.b], [Vn.b])
            else:
                for s_ in range(NS):
                    ps = nextF()
                    mm(ps.t[:, 0:512], ckvT.t[:, 128 * s_:128 * s_ + 128], Wukv.t[:, 512:1024], True, True, [ckvT.b, Wukv.b], [ps.b])
                    cp("act", Vc.t[:, pos0 // 128 + s_, :, 0:64], ps.t[:, 0:512].rearrange("p (h v) -> p h v", h=8), [ps.b], [Vc_b[pos0 // 128 + s_]])

            attn = sb_t("attn", [128, 4, NT], BF16)
            sqa = sb_t("sq4", [128, 4, NT], BF16)

            def qview(Qt, p, c0, n):
                return Qt.t[:, p, :].rearrange("q (c n) -> q c n", c=2)[:, :, c0:c0 + n]

            def scores_pair(ps3, p, kn_ap, kr_ap, c0, n, R, Wb_):
                with P.grp("pe"):
                    mm(ps3, kn_ap, qview(QTn, p, c0, n), True, False, R + [QTn.b], [Wb_])
                    mm(ps3, kr_ap, qview(QTr, p, c0, n), False, True, R + [QTr.b], [Wb_])

            def finish_pair(ops, p, c0, n, stride):
                w_ = stride + n
                osb = sb_t("osb", [65, 2 * NT], F32)
                cp("act", osb.t[:, 0:w_], ops.t[0:65, 0:w_], [ops.b], [osb.b])
                dps = nextF()
                mm(dps.t[0:64, 0:w_], sel_f.t[0:65, :], osb.t[0:65, 0:w_], True, True, [sel_f.b, osb.b], [dps.b])
                rd = sb_t("rd", [64, 2 * NT], F32)
                recip(rd.t[:, 0:w_], dps.t[0:64, 0:w_], [dps.b], [rd.b])
                for c_ in range(2):
                    tt("dve", attn.t[64 * c_:64 * c_ + 64, p, c0:c0 + n], osb.t[0:64, c_ * stride:c_ * stride + n], rd.t[:, c_ * stride:c_ * stride + n],
                       ALU.mult, [osb.b, rd.b], [attn.b])

            if not sample:
                qb0 = pos0 // 128
                for p in range(4):
                    ops = nextO()
                    nj = qb0 + NS
                    for j in range(nj):
                        lo = max(j, qb0) - qb0
                        nq_ = N - 128 * lo
                        sp_ = nextF()
                        sp3 = sp_.t[:, 0:2 * nq_].rearrange("q (c n) -> q c n", c=2)
                        scores_pair(sp3, p, KTn.t[:, p, 128 * j:128 * j + 128], KTr.t[:, 128 * j:128 * j + 128], 128 * lo, nq_, [KTn_b[j], KTr_b[j]], sp_.b)
                        ectr[0] += 1
                        E = sb_t("E%d" % (ectr[0] % 3), [128, 2 * NT], BF16)
                        E3 = E.t[:, 0:2 * nq_].rearrange("q (c n) -> q c n", c=2)
                        act(E3, sp3, AF.Exp, [sp_.b], [E.b])
                        if j >= qb0:
                            mset(E.t[64:128, 0:2 * nq_].rearrange("q (c n) -> q c n", c=2)[:, :, 0:64], 0.0, [E.b], e="dve")
                        for c_ in range(2):
                            mm(ops.t[0:65, c_ * NT + 128 * lo:c_ * NT + N], Vc.t[:, j, 2 * p + c_, :], E.t[:, c_ * nq_:(c_ + 1) * nq_], j == 0 and c_ == 0,
                               j == nj - 1, [Vc_b[j], E.b], [ops.b], inc=True)
                    finish_pair(ops, p, 0, N, NT)
            else:
                for s_ in range(4):
                    prep_cache(s_)
                    for p in range(4):
                        ops = nextO()
                        for j in range(9):
                            sp_ = nextF()
                            ectr[0] += 1
                            E = sb_t("E%d" % (ectr[0] % 3), [128, 2 * NT], BF16)
                            if j < 8:
                                sp3 = sp_.t[:, 0:64].rearrange("q (c n) -> q c n", c=2)
                                jj = 8 * (s_ % 2) + j
                                scores_pair(sp3, p, KTc[s_].t[:, p, 128 * jj:128 * jj + 128], KRc[s_].t[:, 128 * jj:128 * jj + 128], 32 * s_, 32, [KTn_b[jj], KTr_b[jj]], sp_.b)
                                act(E.t[:, 0:64], sp_.t[:, 0:64], AF.Exp, [sp_.b], [E.b])
                                for c_ in range(2):
                                    mm(ops.t[0:65, 32 * c_:32 * c_ + 32], Vcc[s_].t[:, jj, 2 * p + c_, :], E.t[:, 32 * c_:32 * c_ + 32], j == 0 and c_ == 0, False, [Vc_b[jj], E.b], [ops.b], inc=True)
                            else:
                                sp3 = sp_.t[0:32, 0:64].rearrange("q (c n) -> q c n", c=2)
                                scores_pair(sp3, p, KTn_new.t[:, p, 32 * s_:32 * s_ + 32], KTr_new.t[:, 32 * s_:32 * s_ + 32], 32 * s_, 32, [KTn_new.b, KTr_new.b], sp_.b)
                                act(E.t[0:32, 0:64], sp_.t[0:32, 0:64], AF.Exp, [sp_.b], [E.b])
                                for c_ in range(2):
                                    mm(ops.t[0:65, 32 * c_:32 * c_ + 32], Vn.t[0:32, s_, 2 * p + c_, :], E.t[0:32, 32 * c_:32 * c_ + 32], False, True, [Vn.b, E.b], [ops.b], inc=True)
                        finish_pair(ops, p, 32 * s_, 32, 32)
            for i in range(4):
                act(sqa.t[:, i, 0:N], attn.t[:, i, 0:N], AF.Square, [attn.b], [sqa.b])
            rs_mla = bc_rstd(sqa, 4, ones512, "rs_mla")
            for i in range(4):
                tt("dve", attn.t[:, i, 0:N], attn.t[:, i, 0:N], sgm.t[:, i, 0:N], ALU.mult, [attn.b, sgm.b], [attn.b])
                tt("dve", mix.t[:, 4 + i, 0:N], attn.t[:, i, 0:N], rs_mla.t[:, 0:N], ALU.mult, [attn.b, rs_mla.b], [mix.b])

            if flags.get('upto', 9) < 4:
                return
            stage[0] = 'B'
            for s_ in range(NS):
                for half in range(2):
                    ps = nextF()
                    with P.grp("pe"):
                        for kt in range(8):
                            mm(ps.t[:, 0:512], mix.t[:, kt, 128 * s_:128 * s_ + 128], Wout.t[:, kt, 512 * half:512 * half + 512], kt == 0, kt == 7, [mix.b, Wout.b], [ps.b])
                    yo = sb_t("yo", [128, 512], F32)
                    tt("dve", yo.t[:], ps.t[:, 0:512], xt[s_].t[:, 512 * half:512 * half + 512], ALU.add, [ps.b, xt[s_].b], [yo.b])
                    P.dma(yout[128 * s_:128 * s_ + 128, 512 * half:512 * half + 512], yo.t[:], [yo.b], [], is_output=True)

        pool_tiles = {}

        tidx = [0]
        DOUBLE = set(flags.get("double", ("hT", "uT", "sg", "sgm", "cq", "ckvT", "krT", "st2", "okv", "okr", "kn_", "r1", "r2", "tb", "ss", "rs",
                                          "Xs", "Hs", "Hb", "yg", "ys_", "sig", "rs_ssm", "rq", "rq2", "hn_s2", "hn_t",
                                          "attn", "rs_mla", "M1", "M2")))

        def sb_t(name, shape, dt):
            key = name + ("_%d" % (tidx[0] % 2) if name in DOUBLE else "")
            if key not in pool_tiles:
                pool_tiles[key] = sb(shape, dt, key)
            return pool_tiles[key]

        KTc, KRc, Vcc = [], [], []
        if flags.get("sample", True):
            ktc = KTn; krc = KTr; vcc = Vc
            for s_ in range(4):
                KTc.append(ktc); KRc.append(krc); Vcc.append(vcc)

            def prep_cache(s_):
                o8 = 8 * (s_ % 2)
                cT = sb_t("cT", [128, 1024], BF16)
                for j in range(8):
                    cl = sb_t("cl", [128, 160], F32)
                    P.dma(cl.t[:, 0:128], cckv[s_, 128 * j:128 * j + 128, :], [], [cl.b])
                    P.dma(cl.t[:, 128:160], ckr[s_, 128 * j:128 * j + 128, :], [], [cl.b])
                    tb = sb_t("tb", [128, 256], BF16)
                    cp("dve", tb.t[:, 0:128], cl.t[:, 0:128], [cl.b], [tb.b])
                    cp("dve", tb.t[:, 128:256].rearrange("p (a b) -> p a b", a=4), cl.t[:, 128:160].unsqueeze(1).to_broadcast([128, 4, 32]), [cl.b], [tb.b])
                    pb = nextB()
                    with P.grp("pe"):
                        tr(pb.t[:, 0:128], tb.t[:, 0:128], ident_b.t[:], [tb.b, ident_b.b], [pb.b], inc=False)
                        tr(pb.t[:, 128:256], tb.t[:, 128:256], ident_b.t[:], [tb.b, ident_b.b], [pb.b])
                    cp("act", cT.t[:, 128 * j:128 * j + 128], pb.t[:, 0:128], [pb.b], [cT.b])
                    cp("act", krc.t[:, 128 * (o8 + j):128 * (o8 + j) + 128], pb.t[:, 128:256], [pb.b], [KTr_b[o8 + j]])
                for j in range(8):
                    ps = nextF()
                    mm(ps.t[:, 0:512], cT.t[:, 128 * j:128 * j + 128], Wukv.t[:, 512:1024], True, True, [cT.b, Wukv.b], [ps.b])
                    cp("act", vcc.t[:, o8 + j, :, 0:64], ps.t[:, 0:512].rearrange("p (h v) -> p h v", h=8), [ps.b], [Vc_b[o8 + j]])
                for i in range(4):
                    for hf in range(2):
                        ps = nextF()
                        mm(ps.t[:, 0:512], Wukv.t[:, 128 * i:128 * i + 128], cT.t[:, 512 * hf:512 * hf + 512], True, True, [Wukv.b, cT.b], [ps.b])
                        s2 = sb_t("sq4", [128, 4, NT], BF16)
                        s2v = s2.t[:, :, :].rearrange("p a b -> p (a b)")
                        act(s2v, ps.t[:, 0:512], AF.Square, [ps.b], [s2.b])
                        p2 = nextF()
                        mm(p2.t[:, 0:512], blk64.t[:], s2v, True, True, [blk64.b, s2.b], [p2.b])
                        t_ = sb_t("kt_", [128, 512], F32)
                        rsqrt_pow(t_.t[:], p2.t[:, 0:512], [p2.b], [t_.b])
                        stt(ktc.t[:, i, 128 * o8 + 512 * hf:128 * o8 + 512 * hf + 512], ps.t[:, 0:512], gc.t[:, 12 + i:13 + i], t_.t[:], ALU.mult, ALU.mult, [ps.b, gc.b, t_.b],
                            KTn_b[o8 + 4 * hf:o8 + 4 * hf + 4])

        if flags.get('sched', True):
            P.start_recording()
        if flags.get("sample", True) and flags.get('upto', 9) >= 1:
            tile_pass(128, xs, ys, ockv_s, okr_s, 1024, 16, True, True, True, 0)
        nseq = flags.get("nseq", 2) if flags.get('upto', 9) >= 1 else 0
        ntile = flags.get("ntile", SEQ // NT)
        for sq_ in range(nseq):
            for it in range(ntile):
                r0 = sq_ * SEQ + it * NT
                tile_pass(NT, xp[r0:r0 + NT, :], yp[r0:r0 + NT, :], ockv_p[r0:r0 + NT, :], okr_p[r0:r0 + NT, :],
                          it * NT, (it * NT) // 128, False, it == 0, it == ntile - 1, sq_)
        if flags.get('sched', True):
            P.sched_slack = flags.get('slack', 0.25)
            P.sched_tabpen = flags.get('tabpen', 2.0)
            P.sched_xlat = flags.get('xlat', 0.3)
            P.schedule_and_emit(flags.get('window', 800))
        P.finish()
    return nc


def _host_weights(inp):
    f = lambda a: np.ascontiguousarray(np.asarray(a, dtype=np.float32))
    W = {}
    W["w_in"] = f(inp["w_in"][0].reshape(8, 128, 1952).transpose(1, 0, 2))
    W["nin"] = f(inp["norm_in"][0].reshape(8, 128).T)
    wuq = np.asarray(inp["w_uq"][0]).reshape(256, 8, 96)
    sw = np.concatenate([np.arange(16, 32), np.arange(0, 16)])
    wq = np.concatenate([wuq[:, :, :64].reshape(256, 512), wuq[:, :, 64:].reshape(256, 256), wuq[:, :, 64:][:, :, sw].reshape(256, 256)], axis=1)
    W["w_uq"] = f(wq.reshape(2, 128, 1024).transpose(1, 0, 2))
    W["nq"] = f(inp["q_lora_norm"][0].reshape(2, 128).T)
    wkv = np.asarray(inp["w_ukv"][0]).reshape(128, 8, 128)
    W["w_ukv"] = f(np.concatenate([wkv[:, :, :64].reshape(128, 512), wkv[:, :, 64:].reshape(128, 512)], axis=1))
    W["w_glu"] = f(inp["w_glu"][0].reshape(4, 128, 512).transpose(1, 0, 2))
    W["b_glu"] = f(inp["b_glu"][0].reshape(4, 128).T)
    W["w_out"] = f(inp["w_out"][0].reshape(8, 128, 1024).transpose(1, 0, 2))
    W["nout"] = f(np.concatenate([inp["out_norm_ssm"][0], inp["out_norm_mla"][0]]).reshape(8, 128).T)
    gcols = np.zeros((128, 24), np.float32)
    qn = np.asarray(inp["q_nope_norm"][0]); qr = np.asarray(inp["q_rope_norm"][0]); kn = np.asarray(inp["k_nope_norm"][0])
    for i in range(4):
        gcols[:, i] = np.tile(qn, 2)
        gcols[:, 12 + i] = np.tile(kn, 2)
        gcols[:, 8 + i] = np.asarray(inp["ssm_d"][0])[128 * i:128 * i + 128]
    gcols[:, 4] = np.tile(qr, 4)
    pidx = np.arange(128)
    gcols[:, 16] = np.tile(qn, 2) * (pidx < 64)
    gcols[:, 17] = np.tile(qn, 2) * (pidx >= 64)
    for k_ in range(4):
        gcols[:, 18 + k_] = (pidx // 32 == k_)
    gcols[:, 5] = np.tile(qr[sw], 4)
    gcols[:, 6] = np.tile(np.concatenate([-np.ones(16), np.ones(16)]), 4)
    gcols[:, 7] = np.tile(np.concatenate([np.arange(16), np.arange(16)]), 4)
    W["gcols"] = gcols
    W["grows"] = f(np.tile(np.concatenate([inp["kv_lora_norm"][0], inp["k_rope_norm"][0]])[None, :], (128, 1)))
    cmx = np.zeros((128, 5, 128), np.float32)
    cmx[:, 0] = np.eye(128)
    p = np.arange(128)
    cmx[:, 1] = (p[:, None] // 64 == p[None, :] // 64) / 64.0
    cmx[:, 2] = (p[:, None] // 32 == p[None, :] // 32) / 32.0
    cmx[:, 3] = 1.0 / 256
    cmx[:, 4] = 1.0 / 512
    W["cm"] = cmx
    sel = np.zeros((128, 64), np.float32); sel[64, :] = 1.0
    W["sel"] = sel
    are = np.asarray(inp["ssm_a_re"][0]); aim = np.asarray(inp["ssm_a_im"][0]); ldt = np.asarray(inp["ssm_log_dt"][0])
    bre = np.asarray(inp["ssm_b_re"][0]); bim = np.asarray(inp["ssm_b_im"][0])
    cre = np.asarray(inp["ssm_c_re"][0]); cim = np.asarray(inp["ssm_c_im"][0])
    def l2(a):
        return a.reshape(16, 2, 64).transpose(1, 2, 0).reshape(128, 16)
    W["a2"] = f(np.stack([l2(are), l2(aim), l2(np.tile(ldt[:, None], (1, 64)))], axis=1))
    def l1(a):
        v = a.reshape(4, 4, 2, 64)
        v = v.transpose(1, 0, 2, 3)
        v = np.broadcast_to(v[:, None, None], (4, 2, 16, 4, 2, 64))
        return v.reshape(128, 512)
    W["a1"] = f(np.stack([l1(are), l1(aim), l1(np.tile(ldt[:, None], (1, 64)))], axis=1))
    def lbt(b):
        v = b.reshape(4, 4, 2, 64, 16)
        out = np.zeros((4, 2, 16, 4, 2, 64), np.float32)
        for g2 in range(2):
            out[:, g2, :, :, g2, :] = v[:, :, g2].transpose(1, 3, 0, 2)
        return out.reshape(128, 512)
    W["bt1"] = f(np.stack([lbt(bre), lbt(bim)], axis=1))
    def lb2(b):
        v = b.reshape(16, 2, 64, 16)
        out = np.zeros((2, 64, 16, 4, 2, 16), np.float32)
        for pr in range(16):
            for g2 in range(2):
                out[g2, :, pr, pr % 4, g2, :] = v[pr, g2]
        return out.reshape(128, 16, 128)
    W["b2"] = f(np.stack([lb2(bre), lb2(bim)], axis=1))
    def lc2(c_):
        v = c_.reshape(16, 2, 16, 64)
        out = np.zeros((2, 64, 16, 2, 16), np.float32)
        for g2 in range(2):
            out[g2, :, :, g2, :] = v[:, g2].transpose(2, 0, 1)
        return out.reshape(128, 16, 32)
    W["c2"] = f(np.stack([lc2(cre), lc2(cim)], axis=1))
    return W


FLAGS = {}


def kernel(**inp):
    inp = {k: np.asarray(v) for k, v in inp.items()}
    W = _host_weights(inp)
    nc = build_program(FLAGS)
    in_maps = []
    for c in range(NCORES):
        m = dict(W)
        m["xp"] = np.ascontiguousarray(inp["x_prompt"][2 * c:2 * c + 2].reshape(2 * SEQ, D))
        m["xs"] = np.ascontiguousarray(inp["x_sample"][4 * c:4 * c + 4].reshape(128, D))
        m["cckv"] = np.ascontiguousarray(inp["cache_ckv"][0, 4 * c:4 * c + 4])
        m["ckr"] = np.ascontiguousarray(inp["cache_krope"][0, 4 * c:4 * c + 4])
        for nm, key in (("h0r", "state_ssm_re"), ("h0i", "state_ssm_im")):
            v = inp[key][0, 4 * c:4 * c + 4].reshape(4, 16, 2, 64)
            m[nm] = np.ascontiguousarray(v.transpose(2, 3, 0, 1).reshape(128, 4, 16))
        in_maps.append(m)
    res = run_bass_kernel_spmd(nc, in_maps, core_ids=list(range(NCORES)))
    R = res.results
    yp = np.concatenate([R[c]["yp"].reshape(2, SEQ, D) for c in range(NCORES)], axis=0)
    ysm = np.concatenate([R[c]["ys"].reshape(4, 32, D) for c in range(NCORES)], axis=0)
    ckv_p = np.concatenate([R[c]["ockv_p"].reshape(2, SEQ, 128) for c in range(NCORES)], axis=0)[None]
    kr_p = np.concatenate([R[c]["okr_p"].reshape(2, SEQ, 32) for c in range(NCORES)], axis=0)[None]
    ckv_s = np.concatenate([R[c]["ockv_s"].reshape(4, 32, 128) for c in range(NCORES)], axis=0)[None]
    kr_s = np.concatenate([R[c]["okr_s"].reshape(4, 32, 32) for c in range(NCORES)], axis=0)[None]

    def unst(a, nseq):
        v = a.reshape(2, 64, nseq, 2, 16).transpose(2, 3, 4, 0, 1)
        v = v.reshape(nseq, 2, 32, 64)
        return v[:, 0], v[:, 1]
    sp_ = [unst(R[c]["ost_p"], 2) for c in range(NCORES)]
    ss_ = [unst(R[c]["ost_s"], 4) for c in range(NCORES)]
    re_p = np.concatenate([a for a, _ in sp_], axis=0)[None]; im_p = np.concatenate([b for _, b in sp_], axis=0)[None]
    re_s = np.concatenate([a for a, _ in ss_], axis=0)[None]; im_s = np.concatenate([b for _, b in ss_], axis=0)[None]
    f = lambda a: np.ascontiguousarray(a, dtype=np.float32)
    return (f(yp), f(ysm), f(ckv_p), f(kr_p), f(re_p), f(im_p), f(ckv_s), f(kr_s), f(re_s), f(im_s))
```

```python
import numpy as np
from contextlib import ExitStack
import concourse.bass as bass
import concourse.mybir as mybir
from concourse.bass_utils import run_bass_kernel_spmd

F32 = mybir.dt.float32
BF16 = mybir.dt.bfloat16
AF = mybir.ActivationFunctionType
ALU = mybir.AluOpType

NCORES = 8
D = 1024
SEQ = 2048
NT = 128
L = 8
EPS = 1e-6
PI = float(np.pi)
ENGS = ("pe", "act", "dve", "pool", "sp")


class Buf:
    __slots__ = ("name", "last_w", "readers", "sem", "semcnt")

    def __init__(self, name):
        self.name = name
        self.last_w = None
        self.readers = []
        self.sem = None
        self.semcnt = 0


class Prog:
    def __init__(self, nc, stack):
        self.nc = nc
        self.stack = stack
        self.eng = {"pe": nc.tensor, "act": nc.scalar, "dve": nc.vector, "pool": nc.gpsimd, "sp": nc.sync}
        self.sem = {e: stack.enter_context(nc.semaphore("s_" + e)) for e in ENGS}
        self.cnt = {e: 0 for e in ENGS}
        self.known = {e: {} for e in ENGS}
        self.vc = {e: [] for e in ENGS}
        self.pending = {e: [] for e in ENGS}
        self.out_events = []
        self.nsem = 0
        self.recording = False
        self.cur = None
        self.nodes = []
        self.dma_latest = {}
        self.sched_slack = 0.0
        self.sched_tabpen = 0.0
        self.sched_xlat = 0.0

    def _need(self, e, dep):
        key, n = dep[0], dep[1]
        if n is None:
            raise RuntimeError("dependency on un-incremented instruction")
        if self.known[e].get(key, 0) >= n:
            return
        if isinstance(key, str):
            self.eng[e].wait_ge(self.sem[key], n)
            snap = self.vc[key][n - 1]
            for g, v in snap.items():
                if v > self.known[e].get(g, 0):
                    self.known[e][g] = v
        else:
            self.eng[e].wait_ge(key, n)
        self.known[e][key] = n

    def _deps(self, e, reads, writes):
        for b in reads:
            if b.last_w is not None:
                self._need(e, b.last_w)
        for b in writes:
            if b.last_w is not None and (b.last_w[0] != e or e != "pe"):
                self._need(e, b.last_w)
            for r in b.readers:
                if r[0] != e or e != "pe":
                    self._need(e, r)

    def start_recording(self):
        self.recording = True
        self.nodes = []
        self.cur = None
        self.rw = {}

    def grp(self, e):
        prog = self

        class _G:
            def __enter__(self_):
                prog.cur = prog._new_node(e)
                return self_

            def __exit__(self_, *a):
                nd = prog.cur
                prog.cur = None
                if nd["ops"]:
                    for k in range(len(nd["ops"]) - 1, -1, -1):
                        if nd["ops"][k][0] == "op":
                            nd["ops"][k][5] = True
                            break
                    prog.nodes.append(nd)
                return False
        return _G()

    def _new_node(self, e):
        return {"e": e, "ops": [], "deps": set(), "cost": 0.0, "lat": 0.0}

    def _rec(self, kind, e, payload, reads, writes, inc, cost, lat=None, tab=None):
        if self.cur is not None:
            nd = self.cur
            assert nd["e"] == e, (nd["e"], e)
        else:
            nd = self._new_node(e)
        me = id(nd)
        for b in reads:
            st_ = self.rw.setdefault(id(b), [None, []])
            if st_[0] is not None and st_[0] is not nd:
                nd["deps"].add(st_[0]["i"] if "i" in st_[0] else None)
        for b in writes:
            st_ = self.rw.setdefault(id(b), [None, []])
            if st_[0] is not None and st_[0] is not nd:
                nd["deps"].add(st_[0].get("i"))
            for r in st_[1]:
                if r is not nd:
                    nd["deps"].add(r.get("i"))
        nd["ops"].append([kind, payload, list(reads), list(writes), None, inc])
        if tab is not None:
            nd["tab"] = tab
        nd["cost"] += cost
        nd["lat"] = max(nd["lat"], lat if lat is not None else 0.0)
        if self.cur is None:
            if kind == "op":
                nd["ops"][-1][5] = True
            nd["i"] = len(self.nodes)
            self.nodes.append(nd)
        else:
            if "i" not in nd:
                nd["i"] = len(self.nodes)
        for b in reads:
            self.rw[id(b)][1].append(nd)
        for b in writes:
            self.rw[id(b)][0] = nd
            self.rw[id(b)][1] = []

    def schedule_and_emit(self, window=48):
        self.recording = False
        nodes = self.nodes
        n = len(nodes)
        for i, nd in enumerate(nodes):
            assert nd["i"] == i, (nd["i"], i)
            nd["deps"].discard(None)
            nd["deps"].discard(i)
        succ = [[] for _ in range(n)]
        left = [0] * n
        for i, nd in enumerate(nodes):
            left[i] = len(nd["deps"])
            for d in nd["deps"]:
                assert d < i
                succ[d].append(i)
        rt = [0.0] * n
        bl = [0.0] * n
        for i in range(n - 1, -1, -1):
            m_ = 0.0
            for j in succ[i]:
                if bl[j] > m_:
                    m_ = bl[j]
            bl[i] = nodes[i]["cost"] + nodes[i]["lat"] + m_
        SLACK = self.sched_slack
        TABPEN = self.sched_tabpen
        cur_tab = [None]
        queues = {e: [i for i in range(n) if nodes[i]["e"] == e] for e in ENGS}
        qpos = {e: 0 for e in ENGS}
        done = [False] * n
        tfree = {e: 0.0 for e in ENGS}
        order = []
        remaining = n
        while remaining:
            best = None
            for e in ENGS:
                q = queues[e]
                p = qpos[e]
                while p < len(q) and done[q[p]]:
                    p += 1
                qpos[e] = p
                cnt = 0
                k = p
                cands_e = []
                while k < len(q) and cnt < window:
                    i = q[k]
                    k += 1
                    if done[i]:
                        continue
                    cnt += 1
                    if left[i] == 0:
                        stt_ = max(tfree[e], rt[i])
                        pen = 0.0
                        if e == "act" and TABPEN > 0:
                            tb_ = nodes[i].get("tab")
                            if tb_ is not None and tb_ != cur_tab[0]:
                                pen = TABPEN
                        cands_e.append((stt_ + pen, i, pen))
                if cands_e:
                    m0 = min(c_[0] for c_ in cands_e)
                    pick = None
                    for c_ in cands_e:
                        if c_[0] <= m0 + SLACK:
                            if pick is None or bl[c_[1]] > bl[pick[1]] + 1e-9 or (abs(bl[c_[1]] - bl[pick[1]]) <= 1e-9 and c_[1] < pick[1]):
                                pick = c_
                    if best is None or pick[0] < best[0] - 1e-9 or (abs(pick[0] - best[0]) <= 1e-9 and pick[1] < best[1]):
                        best = (pick[0], pick[1], e, pick[2])
            assert best is not None, "scheduler deadlock"
            stt_, i, e, pen = best
            nd = nodes[i]
            done[i] = True
            remaining -= 1
            order.append(i)
            if e == "act" and nd.get("tab") is not None:
                cur_tab[0] = nd["tab"]
            tfree[e] = stt_ + nd["cost"]
            fin = stt_ + nd["cost"] + nd["lat"]
            for j in succ[i]:
                left[j] -= 1
                f_ = fin + (self.sched_xlat if nodes[j]["e"] != e else 0.0)
                if f_ > rt[j]:
                    rt[j] = f_
        self.est_makespan = max(tfree.values())
        for i in order:
            nd = nodes[i]
            for kind, payload, reads, writes, _, inc in nd["ops"]:
                if kind == "op":
                    self._emit_op(nd["e"], payload, reads, writes, inc)
                elif kind == "dma":
                    self._emit_dma(*payload)
                elif kind == "selfwait":
                    self._need(nd["e"], (nd["e"], self.cnt[nd["e"]]))
        self.nodes = []

    def selfwait(self, e):
        if getattr(self, "recording", False):
            self._rec("selfwait", e, None, [], [], False, 0.2)
        else:
            self._need(e, (e, self.cnt[e]))

    def op(self, e, fn, reads=(), writes=(), inc=True, cost=0.3, tab=None):
        if getattr(self, "recording", False):
            self._rec("op", e, fn, reads, writes, inc, cost, tab=tab)
            return None
        return self._emit_op(e, fn, reads, writes, inc)

    def _emit_op(self, e, fn, reads=(), writes=(), inc=True):
        self._deps(e, reads, writes)
        ins = fn(self.eng[e])
        if inc:
            self.cnt[e] += 1
            n = self.cnt[e]
            ins.then_inc(self.sem[e], 1)
            snap = dict(self.known[e])
            snap[e] = n
            self.vc[e].append(snap)
            me = (e, n)
            for rec in self.pending[e]:
                rec[1] = n
            self.pending[e] = []
        else:
            me = [e, None]
            self.pending[e].append(me)
        for b in reads:
            b.readers.append(me)
        for b in writes:
            b.last_w = me
            b.readers = []
        return ins

    def dma(self, out, in_, reads=(), writes=(), is_output=False, q="sp", nbytes=65536, **kw):
        if getattr(self, "recording", False):
            self._rec("dma", q, (out, in_, list(reads), list(writes), is_output, q, kw), reads, writes, False, 0.15, lat=3.5 + nbytes / 80e3)
            return None
        return self._emit_dma(out, in_, reads, writes, is_output, q, kw)

    def _emit_dma(self, out, in_, reads, writes, is_output, q, kw):
        self._deps(q, reads, writes)
        owner = writes[0] if (writes and not is_output) else reads[0]
        if owner.sem is None:
            owner.sem = self.stack.enter_context(self.nc.semaphore("d%d" % self.nsem))
            self.nsem += 1
        owner.semcnt += 16
        ins = self.eng[q].dma_start(out=out, in_=in_, **kw)
        ins.then_inc(owner.sem, 16)
        me = (owner.sem, owner.semcnt)
        self.dma_latest[owner.sem] = owner.semcnt
        for b in reads:
            b.readers.append(me)
        for b in writes:
            b.last_w = me
            b.readers = []
        if is_output:
            self.out_events.append(me)
        return ins

    def barrier(self):
        for e in ENGS:
            for sem_, c_ in self.dma_latest.items():
                self._need(e, (sem_, c_))
        for e in ENGS:
            for f in ENGS:
                if f != e and self.cnt[f] > 0:
                    self._need(e, (f, self.cnt[f]))

    def finish(self):
        for ev in self.out_events:
            self._need("sp", ev)


class T:
    __slots__ = ("t", "b")

    def __init__(self, t, name):
        self.t = t
        self.b = Buf(name)


def build_program(flags):
    nc = bass.Bass("TRN2", target_bir_lowering=False)
    dr = {}

    def din(name, shape, dt=F32):
        dr[name] = nc.dram_tensor(name, list(shape), dt, kind="ExternalInput").ap()
        return dr[name]

    def dout(name, shape, dt=F32):
        dr[name] = nc.dram_tensor(name, list(shape), dt, kind="ExternalOutput").ap()
        return dr[name]

    xp = din("xp", [2 * SEQ, D]); xs = din("xs", [128, D])
    cckv = din("cckv", [4, 1024, 128]); ckr = din("ckr", [4, 1024, 32])
    h0r = din("h0r", [128, 4, 16]); h0i = din("h0i", [128, 4, 16])
    w_in = din("w_in", [128, 8, 1952]); nin = din("nin", [128, 8])
    w_uq = din("w_uq", [128, 2, 1024]); nq = din("nq", [128, 2])
    w_ukv = din("w_ukv", [128, 1024])
    w_glu = din("w_glu", [128, 4, 512]); b_glu = din("b_glu", [128, 4])
    w_out = din("w_out", [128, 8, 1024]); nout = din("nout", [128, 8])
    gcols = din("gcols", [128, 24])
    grows = din("grows", [128, 160])
    cm = din("cm", [128, 5, 128])
    sel = din("sel", [128, 64])
    a1 = din("a1", [128, 3, 512])
    bt1 = din("bt1", [128, 2, 512])
    a2 = din("a2", [128, 3, 16])
    b2 = din("b2", [128, 2, 16, 128])
    c2 = din("c2", [128, 2, 16, 32])
    dbg = dout("dbg", [128, 2048]) if flags.get("dbg") else None

    yp = dout("yp", [2 * SEQ, D]); ys = dout("ys", [128, D])
    ockv_p = dout("ockv_p", [2 * SEQ, 128]); okr_p = dout("okr_p", [2 * SEQ, 32])
    ost_p = dout("ost_p", [128, 2, 2, 16])
    ockv_s = dout("ockv_s", [128, 128]); okr_s = dout("okr_s", [128, 32])
    ost_s = dout("ost_s", [128, 4, 2, 16])

    with ExitStack() as st:
        P = Prog(nc, st)
        uid = [0]

        def sb(shape, dt=F32, name=None, stack=st):
            uid[0] += 1
            nm = (name or "t") + "_%d" % uid[0]
            return T(stack.enter_context(nc.sbuf_tensor(nm, list(shape), dt)), nm)

        def psum(shape, dt=F32, name=None):
            uid[0] += 1
            nm = (name or "ps") + "_%d" % uid[0]
            return T(st.enter_context(nc.psum_tensor(nm, list(shape), dt)), nm)

        psF = [psum([128, 512], F32, "psF") for _ in range(7)]
        psB = [psum([128, 1024], BF16, "psB") for _ in range(1)]
        rr = {"F": 0, "B": 0}

        stage = ["F"]
        rr["Bk"] = 0

        def nextF():
            if stage[0] == "F":
                rr["F"] = (rr["F"] + 1) % 2
                return psF[rr["F"]]
            if stage[0] == "S4":
                rr["S4"] = (rr.get("S4", 0) + 1) % 2
                return psF[(5, 6)[rr["S4"]]]
            rr["Bk"] = (rr["Bk"] + 1) % 2
            return psF[2 + rr["Bk"]]

        rr["O"] = 0

        def nextO():
            return psF[4]

        def nextB():
            return psB[0]

        def fsz(ap):
            n = 1
            for d_ in tuple(ap.shape)[1:]:
                n *= d_
            return n

        def ecost(e, ap, mul=1.0):
            n = fsz(ap)
            if e == "act":
                return 0.22 + n / 1200.0
            if e == "dve":
                return 0.08 + mul * n / 900.0
            if e == "pool":
                return 0.25 + n / 450.0
            return 0.3

        def tt(e, out, in0, in1, op, R, W):
            return P.op(e, lambda g: g.tensor_tensor(out=out, in0=in0, in1=in1, op=op), R, W, cost=ecost(e, out))

        def ts(e, out, in0, s1, s2, op0, op1, R, W):
            if s2 is None:
                return P.op(e, lambda g: g.tensor_scalar(out=out, in0=in0, scalar1=s1, scalar2=None, op0=op0), R, W, cost=ecost(e, out))
            return P.op(e, lambda g: g.tensor_scalar(out=out, in0=in0, scalar1=s1, scalar2=s2, op0=op0, op1=op1), R, W, cost=ecost(e, out))

        def stt(out, in0, scalar, in1, op0, op1, R, W):
            return P.op("dve", lambda g: g.scalar_tensor_tensor(out=out, in0=in0, scalar=scalar, in1=in1, op0=op0, op1=op1), R, W, cost=ecost("dve", out))

        def act(out, in_, func, R, W, **kw):
            tab = "T" if func == AF.Tanh else ("L" if func == AF.Ln else None)
            return P.op("act", lambda g: g.activation(out=out, in_=in_, func=func, **kw), R, W, cost=ecost("act", out), tab=tab)

        def cp(e, out, in_, R, W):
            if e == "act":
                return P.op("act", lambda g: g.copy(out=out, in_=in_), R, W, cost=ecost(e, out))
            return P.op(e, lambda g: g.tensor_copy(out=out, in_=in_), R, W, cost=ecost(e, out))

        def recip(out, in_, R, W):
            return P.op("dve", lambda g: g.reciprocal(out=out, in_=in_), R, W, cost=ecost("dve", out, 6.5))

        def rsqrt_pow(out, in_, R, W, scale=1.0, from_psum=True):
            np_ = int(tuple(out.shape)[0])
            act(out, in_, AF.Ln, list(R) + [epsc.b], W, scale=float(scale), bias=epsc.t[0:np_, 0:1])
            act(out, out, AF.Exp, W, W, scale=-0.5)

        def ppow(out, W):
            shp = [int(d_) for d_ in tuple(out.shape)]
            mh = mhalf.t[0:shp[0], 0:1].to_broadcast(shp)
            P.op("pool", lambda g: g.tensor_tensor(out=out, in0=out, in1=mh, op=ALU.pow), list(W) + [mhalf.b], W, cost=ecost("pool", out))

        def mset(ap, val, W, e="pool"):
            return P.op(e, lambda g: g.memset(ap, val), [], W, cost=ecost(e, ap))

        def mm(out, lhsT, rhs, start, stop, R, W, inc=None, **kw):
            if inc is None:
                inc = stop
            ncol = fsz(rhs)
            c_ = 0.035 + max(ncol, 64) * (4.0 if rhs.dtype == F32 else 1.0) / 1600.0
            return P.op("pe", lambda g: g.matmul(out, lhsT=lhsT, rhs=rhs, start=start, stop=stop, **kw), R, W, inc=inc, cost=c_)

        def tr(out, in_, ident, R, W, inc=True):
            return P.op("pe", lambda g: g.transpose(out=out, in_=in_, identity=ident), R, W, inc=inc, cost=0.12)

        ident_b = sb([128, 128], BF16, "ident"); blk64 = sb([128, 128], BF16, "blk64"); blk32 = sb([128, 128], BF16, "blk32")
        ones256 = sb([128, 128], BF16, "o256"); ones512 = sb([128, 128], BF16, "o512")
        ident_f = sb([128, 128], F32, "identf")
        sel_f = sb([128, 64], F32, "sel")
        epsc = sb([128, 1], F32, "eps")
        mhalf = sb([128, 1], F32, "mhalf")
        hbglu = sb([128, 4], F32, "hbglu")
        WinD = nc.dram_tensor("WinD", [128, 14, 1024], BF16).ap(); WinD_b = Buf("WinD")
        WinT = nc.dram_tensor("WinT", [128, 8 * 160], BF16).ap(); WinT_b = Buf("WinT")
        PIECE_COL0 = [0, 128, 256, 384, 512, 640, 768, 896, 1440, 1568, 1696, 1824, 1024, 1152]
        Wuq = sb([128, 2, 1024], BF16, "Wuq")
        Wukv = sb([128, 1024], BF16, "Wukv")
        Wglu = sb([128, 4, 512], BF16, "Wglu")
        Wout = sb([128, 8, 1024], BF16, "Wout")
        bglu = sb([128, 4], F32, "bglu")
        gc = sb([128, 24], F32, "gcols")
        gr = sb([128, 160], F32, "grows")
        Kbd = sb([128, 4, 8, 128], BF16, "Kbd")
        Wb = sb([128, 4, 8, 2, 128], BF16, "Wb")
        Vd = sb([128, 16, 2, 8, 32], BF16, "Vd")
        A8 = sb([128, 2, 16], F32, "A8")
        cosF = sb([128, 2048], BF16, "cosF"); sinF = sb([128, 2048], BF16, "sinF")
        cosT = sb([128, 17, 32], F32, "cosT"); sinT = sb([128, 17, 32], F32, "sinT")

        sA = ExitStack()
        with ExitStack() as s0:
            def sb0(shape, dt=F32, name=None):
                return sb(shape, dt, name, stack=sA)

            cmf = sb0([128, 5, 128], F32, "cmf")
            P.dma(cmf.t[:], cm[:, :, :], [], [cmf.b])
            for i, dst in enumerate((ident_b, blk64, blk32, ones256, ones512)):
                cp("dve", dst.t[:], cmf.t[:, i, :], [cmf.b], [dst.b])
            cp("dve", ident_f.t[:], cmf.t[:, 0, :], [cmf.b], [ident_f.b])
            P.dma(sel_f.t[:], sel[:, :], [], [sel_f.b])
            P.op("pool", lambda g: g.memset(epsc.t[:], EPS), [], [epsc.b])
            P.op("pool", lambda g: g.memset(mhalf.t[:], -0.5), [], [mhalf.b])
            P.dma(bglu.t[:], b_glu[:, :], [], [bglu.b])
            ts("dve", hbglu.t[:], bglu.t[:], 0.5, None, ALU.mult, None, [bglu.b], [hbglu.b])
            P.dma(gc.t[:], gcols[:, :], [], [gc.b])
            P.dma(gr.t[:], grows[:, :], [], [gr.b])
            ts("dve", gc.t[:, 0:6], gc.t[:, 0:6], float(96 ** -0.5), None, ALU.mult, None, [gc.b], [gc.b])
            ts("dve", gc.t[:, 16:18], gc.t[:, 16:18], float(96 ** -0.5), None, ALU.mult, None, [gc.b], [gc.b])

            pre_a1 = sb0([128, 3, 512], F32, "a1"); P.dma(pre_a1.t[:], a1[:, :, :], [], [pre_a1.b])
            pre_bt1 = sb0([128, 2, 512], F32, "bt1"); P.dma(pre_bt1.t[:], bt1[:, :, :], [], [pre_bt1.b])
        scope_holder = [None]

        def sb0(shape, dt=F32, name=None):
            return sb(shape, dt, name, stack=scope_holder[0])

        if True:
            def cgen(aa, F_, npow, tag):
                e = "dve"
                dt_ = sb0([128, F_], F32, tag + "dt"); act(dt_.t[:], aa.t[:, 2, :], AF.Exp, [aa.b], [dt_.b])
                mag = sb0([128, F_], F32, tag + "mag"); th = sb0([128, F_], F32, tag + "th")
                tt(e, mag.t[:], aa.t[:, 0, :], dt_.t[:], ALU.mult, [aa.b, dt_.b], [mag.b])
                act(mag.t[:], mag.t[:], AF.Exp, [mag.b], [mag.b])
                tt(e, th.t[:], aa.t[:, 1, :], dt_.t[:], ALU.mult, [aa.b, dt_.b], [th.b])
                cr = sb0([128, F_], F32, tag + "cr"); ci = sb0([128, F_], F32, tag + "ci")
                t1 = sb0([128, F_], F32, tag + "t1"); t2 = sb0([128, F_], F32, tag + "t2")
                hp = sb0([128, 1], F32, tag + "hp"); P.op("pool", lambda g: g.memset(hp.t[:], PI / 2), [], [hp.b])
                act(ci.t[:], th.t[:], AF.Sin, [th.b], [ci.b], scale=1.0 / 64)
                act(cr.t[:], th.t[:], AF.Sin, [th.b, hp.b], [cr.b], scale=1.0 / 64, bias=hp.t[:, 0:1])
                for _ in range(6):
                    tt(e, t1.t[:], cr.t[:], cr.t[:], ALU.mult, [cr.b], [t1.b])
                    tt(e, t2.t[:], ci.t[:], ci.t[:], ALU.mult, [ci.b], [t2.b])
                    tt(e, ci.t[:], cr.t[:], ci.t[:], ALU.mult, [cr.b, ci.b], [ci.b])
                    ts(e, ci.t[:], ci.t[:], 2.0, None, ALU.mult, None, [ci.b], [ci.b])
                    tt(e, cr.t[:], t1.t[:], t2.t[:], ALU.subtract, [t1.b, t2.b], [cr.b])
                pw = sb0([128, 2, npow + 1, F_], F32, tag + "pw")
                P.op("pool", lambda g: g.memset(pw.t[:, 0, 0, :], 1.0), [], [pw.b])
                P.op("pool", lambda g: g.memset(pw.t[:, 1, 0, :], 0.0), [], [pw.b])
                tt(e, pw.t[:, 0, 1, :], cr.t[:], mag.t[:], ALU.mult, [cr.b, mag.b], [pw.b])
                tt(e, pw.t[:, 1, 1, :], ci.t[:], mag.t[:], ALU.mult, [ci.b, mag.b], [pw.b])
                for m in range(2, npow + 1):
                    cmul(pw.t[:, 0, m, :], pw.t[:, 1, m, :], pw.t[:, 0, m - 1, :], pw.t[:, 1, m - 1, :], pw.t[:, 0, 1, :], pw.t[:, 1, 1, :],
                         [pw.b], [pw.b], t1, t2)
                x_ = sb0([128, F_], F32, tag + "x"); den = sb0([128, F_], F32, tag + "den")
                cf = sb0([128, 2, F_], F32, tag + "cf")
                ts(e, x_.t[:], pw.t[:, 0, 1, :], -1.0, None, ALU.add, None, [pw.b], [x_.b])
                tt(e, den.t[:], aa.t[:, 0, :], aa.t[:, 0, :], ALU.mult, [aa.b], [den.b])
                tt(e, t1.t[:], aa.t[:, 1, :], aa.t[:, 1, :], ALU.mult, [aa.b], [t1.b])
                tt(e, den.t[:], den.t[:], t1.t[:], ALU.add, [den.b, t1.b], [den.b])
                P.op(e, lambda g: g.reciprocal(out=den.t[:], in_=den.t[:]), [den.b], [den.b])
                tt(e, t1.t[:], x_.t[:], aa.t[:, 0, :], ALU.mult, [x_.b, aa.b], [t1.b])
                tt(e, t2.t[:], pw.t[:, 1, 1, :], aa.t[:, 1, :], ALU.mult, [pw.b, aa.b], [t2.b])
                tt(e, t1.t[:], t1.t[:], t2.t[:], ALU.add, [t1.b, t2.b], [t1.b])
                tt(e, cf.t[:, 0, :], t1.t[:], den.t[:], ALU.mult, [t1.b, den.b], [cf.b])
                tt(e, t1.t[:], pw.t[:, 1, 1, :], aa.t[:, 0, :], ALU.mult, [pw.b, aa.b], [t1.b])
                tt(e, t2.t[:], x_.t[:], aa.t[:, 1, :], ALU.mult, [x_.b, aa.b], [t2.b])
                tt(e, t1.t[:], t1.t[:], t2.t[:], ALU.subtract, [t1.b, t2.b], [t1.b])
                tt(e, cf.t[:, 1, :], t1.t[:], den.t[:], ALU.mult, [t1.b, den.b], [cf.b])
                return pw, cf

            def cmul(or_, oi_, ar_, ai_, br_, bi_, R, W, t1, t2, e="dve", neg_im=False):
                sh = tuple(or_.shape)
                a1_ = _view(t1, sh); a2_ = _view(t2, sh)
                tt(e, a1_, ar_, br_, ALU.mult, R, [t1.b])
                tt(e, a2_, ai_, bi_, ALU.mult, R, [t2.b])
                tt(e, or_, a1_, a2_, ALU.subtract, [t1.b, t2.b], W)
                tt(e, a1_, ar_, bi_, ALU.mult, R, [t1.b])
                tt(e, a2_, ai_, br_, ALU.mult, R, [t2.b])
                if neg_im:
                    tt(e, a1_, a1_, a2_, ALU.add, [t1.b, t2.b], [t1.b])
                    ts(e, oi_, a1_, -1.0, None, ALU.mult, None, [t1.b], W)
                else:
                    tt(e, oi_, a1_, a2_, ALU.add, [t1.b, t2.b], W)

            def _view(t, sh):
                n = 1
                for s_ in sh[1:]:
                    n *= s_
                flat = t.t[:, 0:n]
                if len(sh) == 2:
                    return flat
                if len(sh) == 3:
                    return flat.rearrange("p (a b) -> p a b", a=sh[1])
                return flat.rearrange("p (a b c) -> p a b c", a=sh[1], b=sh[2])

        with ExitStack() as s1:
            scope_holder[0] = s1
            a1t = pre_a1
            bt1t = pre_bt1
            pw1, cf1 = cgen(a1t, 512, 7, "g1")
            T1 = sb0([128, 512], F32, "T1"); T2 = sb0([128, 512], F32, "T2")
            bb1 = sb0([128, 2, 512], F32, "bb1")
            cmul(bb1.t[:, 0, :], bb1.t[:, 1, :], cf1.t[:, 0, :], cf1.t[:, 1, :], bt1t.t[:, 0, :], bt1t.t[:, 1, :], [cf1.b, bt1t.b], [bb1.b], T1, T2)
            wtmp = sb0([128, 2, 4, 128], F32, "wtmp")
            for tau in range(8):
                m = 7 - tau
                cmul(wtmp.t[:, 0, :, :].rearrange("p a b -> p (a b)"), wtmp.t[:, 1, :, :].rearrange("p a b -> p (a b)"),
                     pw1.t[:, 0, m, :], pw1.t[:, 1, m, :], bb1.t[:, 0, :], bb1.t[:, 1, :], [pw1.b, bb1.b], [wtmp.b], T1, T2)
                for ri in range(2):
                    cp("act", Wb.t[:, :, tau, ri, :], wtmp.t[:, ri, :, :], [wtmp.b], [Wb.b])

            P.barrier()
        P.barrier()
        sA.close()
        with ExitStack() as s2:
            scope_holder[0] = s2
            a2t = sb0([128, 3, 16], F32, "a2"); P.dma(a2t.t[:], a2[:, :, :], [], [a2t.b])
            b2t = sb0([128, 2, 16, 128], F32, "b2"); P.dma(b2t.t[:], b2[:, :, :, :], [], [b2t.b])
            c2t = sb0([128, 2, 16, 32], F32, "c2"); P.dma(c2t.t[:], c2[:, :, :, :], [], [c2t.b])
            pw2, cf2 = cgen(a2t, 16, 8, "g2")
            ninc = sb0([128, 8], F32, "nin"); nqc = sb0([128, 2], F32, "nq"); noutc = sb0([128, 8], F32, "nout")
            P.dma(ninc.t[:], nin[:, :], [], [ninc.b]); P.dma(nqc.t[:], nq[:, :], [], [nqc.b]); P.dma(noutc.t[:], nout[:, :], [], [noutc.b])
            ts("dve", noutc.t[:, 0:4], noutc.t[:, 0:4], 0.125, None, ALU.mult, None, [noutc.b], [noutc.b])
            ts("dve", noutc.t[:, 4:8], noutc.t[:, 4:8], 0.5, None, ALU.mult, None, [noutc.b], [noutc.b])
            stg = [sb0([128, 2048], F32, "stg") for _ in range(2)]
            si = [0]

            def load_w(dst_ap, src_ap, ncol, gain_ap, e):
                s_ = stg[si[0] % 2]; si[0] += 1
                P.dma(s_.t[:, 0:ncol], src_ap, [], [s_.b])
                if gain_ap is None:
                    cp(e, dst_ap, s_.t[:, 0:ncol], [s_.b], [dstb[0]])
                elif e == "act":
                    act(dst_ap, s_.t[:, 0:ncol], AF.Copy, [s_.b, gainb[0]], [dstb[0]], scale=gain_ap)
                else:
                    ts(e, dst_ap, s_.t[:, 0:ncol], gain_ap, None, ALU.mult, None, [s_.b, gainb[0]], [dstb[0]])

            wst = [sb0([128, 8, 160], F32, "wst") for _ in range(2)]
            wbf = [sb0([128, 8, 160], BF16, "wbf") for _ in range(2)]
            for pi_, c0_ in enumerate(PIECE_COL0 + [1280]):
                w_ = 160 if pi_ == 14 else 128
                a_ = wst[pi_ % 2]; b_ = wbf[pi_ % 2]
                P.dma(a_.t[:, :, 0:w_], w_in[:, :, c0_:c0_ + w_], [], [a_.b])
                for d_ in range(8):
                    act(b_.t[:, d_, 0:w_], a_.t[:, d_, 0:w_], AF.Copy, [a_.b, ninc.b], [b_.b], scale=ninc.t[:, d_:d_ + 1])
                if pi_ < 14:
                    P.dma(WinD[:, pi_, :].rearrange("p (a b) -> p a b", a=8), b_.t[:, :, 0:128], [b_.b], [WinD_b])
                else:
                    P.dma(WinT[:, :].rearrange("p (a b) -> p a b", a=8), b_.t[:, :, 0:160], [b_.b], [WinT_b])
            dstb = [Wuq.b]; gainb = [nqc.b]
            for kt in range(2):
                load_w(Wuq.t[:, kt, :], w_uq[:, kt, :], 1024, nqc.t[:, kt:kt + 1], "act")
            dstb = [Wukv.b]
            load_w(Wukv.t[:, :], w_ukv[:, :], 1024, None, "act")
            dstb = [Wglu.b]
            load_w(Wglu.t[:, :, :].rearrange("p a b -> p (a b)"), w_glu[:, :, :].rearrange("p a b -> p (a b)"), 2048, None, "act")
            dstb = [Wout.b]; gainb = [noutc.b]
            for kt in range(8):
                load_w(Wout.t[:, kt, :], w_out[:, kt, :], 1024, noutc.t[:, kt:kt + 1], "act")
            cp("dve", A8.t[:, 0, :], pw2.t[:, 0, 8, :], [pw2.b], [A8.b])
            cp("dve", A8.t[:, 1, :], pw2.t[:, 1, 8, :], [pw2.b], [A8.b])
            U1 = sb0([128, 2048], F32, "U1"); U2 = sb0([128, 2048], F32, "U2")
            VdF = sb0([128, 16, 2, 9, 32], F32, "VdF")
            for m in range(9):
                cmul(VdF.t[:, :, 0, m, :], VdF.t[:, :, 1, m, :],
                     c2t.t[:, 0, :, :], c2t.t[:, 1, :, :],
                     pw2.t[:, 0, m, :].unsqueeze(2).to_broadcast([128, 16, 32]), pw2.t[:, 1, m, :].unsqueeze(2).to_broadcast([128, 16, 32]),
                     [c2t.b, pw2.b], [VdF.b], U1, U2, neg_im=True)
            for ri in range(2):
                for pr in range(16):
                    cp("act" if pr % 2 else "pool", Vd.t[:, pr, ri, :, :], VdF.t[:, pr, ri, 1:9, :], [VdF.b], [Vd.b])
            bb2 = sb0([128, 2, 16, 128], F32, "bb2")
            cmul(bb2.t[:, 0, :, :], bb2.t[:, 1, :, :],
                 cf2.t[:, 0, :].unsqueeze(2).to_broadcast([128, 16, 128]), cf2.t[:, 1, :].unsqueeze(2).to_broadcast([128, 16, 128]),
                 b2t.t[:, 0, :, :], b2t.t[:, 1, :, :], [cf2.b, b2t.b], [bb2.b], U1, U2)
            for ct in range(4):
                for lg in range(2):
                    kp = nextF()
                    for p4 in range(4):
                        pr = ct * 4 + p4
                        for ri in range(2):
                            mm(kp.t[:, 128 * p4:128 * p4 + 128], bb2.t[:, ri, pr, :], VdF.t[:, pr, ri, 4 * lg:4 * lg + 4, :],
                               ri == 0, ri == 1, [bb2.b, VdF.b], [kp.b])
                    kv4 = kp.t[:, :].rearrange("p (a l c) -> p a l c", a=4, l=4)
                    if lg == 0:
                        stt(Kbd.t[:, ct, 0, :].rearrange("p (a c) -> p a c", a=4), ident_f.t[:].rearrange("p (a c) -> p a c", a=4),
                            gc.t[:, 8 + ct:9 + ct], kv4[:, :, 0, :], ALU.mult, ALU.add, [ident_f.b, gc.b, kp.b], [Kbd.b])
                        for l_ in range(1, 4):
                            cp("dve", Kbd.t[:, ct, l_, :].rearrange("p (a c) -> p a c", a=4), kv4[:, :, l_, :], [kp.b], [Kbd.b])
                    else:
                        for l_ in range(4):
                            cp("dve", Kbd.t[:, ct, 4 + l_, :].rearrange("p (a c) -> p a c", a=4), kv4[:, :, l_, :], [kp.b], [Kbd.b])
            P.barrier()
        P.barrier()
        with ExitStack() as s3:
            scope_holder[0] = s3
            T1 = sb0([128, 1024], F32, "T1"); T2 = sb0([128, 1024], F32, "T2")
            cF = sb0([128, 2048], F32, "cF"); sF = sb0([128, 2048], F32, "sF")
            inv = sb0([128, 1], F32, "inv"); wv = sb0([128, 4], F32, "wv"); hp2 = sb0([128, 1], F32, "hp2")
            P.op("pool", lambda g: g.memset(hp2.t[:], PI / 2), [], [hp2.b])
            act(inv.t[:], gc.t[:, 7:8], AF.Exp, [gc.b], [inv.b], scale=float(-np.log(10000.0) / 16))
            act(wv.t[:, 1:2], inv.t[:], AF.Sin, [inv.b], [wv.b])
            act(wv.t[:, 0:1], inv.t[:], AF.Sin, [inv.b, hp2.b], [wv.b], bias=hp2.t[:, 0:1])
            P.op("pool", lambda g: g.memset(cF.t[:, 0:1], 1.0), [], [cF.b])
            P.op("pool", lambda g: g.memset(sF.t[:, 0:1], 0.0), [], [sF.b])
            for k in range(11):
                n = 1 << k
                ts("dve", T1.t[:, 0:n], sF.t[:, 0:n], wv.t[:, 1:2], None, ALU.mult, None, [sF.b, wv.b], [T1.b])
                ts("dve", T2.t[:, 0:n], cF.t[:, 0:n], wv.t[:, 1:2], None, ALU.mult, None, [cF.b, wv.b], [T2.b])
                stt(cF.t[:, n:2 * n], cF.t[:, 0:n], wv.t[:, 0:1], T1.t[:, 0:n], ALU.mult, ALU.subtract, [cF.b, wv.b, T1.b], [cF.b])
                stt(sF.t[:, n:2 * n], sF.t[:, 0:n], wv.t[:, 0:1], T2.t[:, 0:n], ALU.mult, ALU.add, [sF.b, wv.b, T2.b], [sF.b])
                tt("dve", wv.t[:, 2:3], wv.t[:, 0:1], wv.t[:, 0:1], ALU.mult, [wv.b], [wv.b])
                tt("dve", wv.t[:, 3:4], wv.t[:, 1:2], wv.t[:, 1:2], ALU.mult, [wv.b], [wv.b])
                tt("dve", wv.t[:, 1:2], wv.t[:, 0:1], wv.t[:, 1:2], ALU.mult, [wv.b], [wv.b])
                ts("dve", wv.t[:, 1:2], wv.t[:, 1:2], 2.0, None, ALU.mult, None, [wv.b], [wv.b])
                tt("dve", wv.t[:, 0:1], wv.t[:, 2:3], wv.t[:, 3:4], ALU.subtract, [wv.b], [wv.b])
            ts("dve", sF.t[:], sF.t[:], gc.t[:, 6:7], None, ALU.mult, None, [sF.b, gc.b], [sF.b])
            cp("act", cosF.t[:], cF.t[:], [cF.b], [cosF.b])
            cp("act", sinF.t[:], sF.t[:], [sF.b], [sinF.b])
            for src, dst in ((cF, cosT), (sF, sinT)):
                for t_ in range(17):
                    pt = nextF()
                    if t_ < 16:
                        in_ap = src.t[:, 128 * t_:128 * t_ + 128]
                        rb_ = src.b
                    else:
                        cp("dve", T1.t[:, 0:128].rearrange("p (a b) -> p a b", a=4), src.t[:, 1024:1056].unsqueeze(1).to_broadcast([128, 4, 32]), [src.b], [T1.b])
                        in_ap = T1.t[:, 0:128]
                        rb_ = T1.b
                    P.op("pe", lambda g: g.transpose(out=pt.t[:, 0:128], in_=in_ap, identity=ident_f.t[:]), [rb_, ident_f.b], [pt.b])
                    cp("dve", dst.t[:, t_, :], pt.t[:, 0:32], [pt.b], [dst.b])
            P.barrier()
        P.barrier()

        KTn = sb([128, 4, SEQ + 0], BF16, "KTn"); KTr = sb([128, SEQ], BF16, "KTr")
        Vc = sb([128, 16, 8, 65], BF16, "Vc")
        KTn_b = [Buf("KTn%d" % j_) for j_ in range(16)]; KTr_b = [Buf("KTr%d" % j_) for j_ in range(16)]; Vc_b = [Buf("Vc%d" % j_) for j_ in range(16)]
        P.op("pool", lambda g: g.memset(Vc.t[:, :, :, 64:65], 1.0), [], Vc_b)
        Hst = sb([128, 2, 16], F32, "Hst")
        xin = [sb([128, D], F32, "xin") for _ in range(2)]
        xslot = [0]
        ectr = [0]
        wctr = [0]

        def tile_pass(N, xsrc, yout, ckv_out, kr_out, pos0, tabidx0, sample, seq_first, seq_last, seqidx):
            NS = N // 128
            NC = N // L
            stage[0] = 'F'
            tidx[0] += 1
            xt = []
            for s_ in range(NS):
                x_ = xin[xslot[0] % 2]; xslot[0] += 1
                P.dma(x_.t[:], xsrc[128 * s_:128 * s_ + 128, :], [], [x_.b])
                xt.append(x_)
            ss = sb_t("ss", [128, 4], F32); junk = sb_t("xn", [128, D], BF16)
            for s_ in range(NS):
                act(junk.t[:], xt[s_].t[:], AF.Square, [xt[s_].b], [junk.b, ss.b], accum_out=ss.t[:, s_:s_ + 1])
            rs = sb_t("rs", [128, 4], F32)
            rsqrt_pow(rs.t[:, 0:NS], ss.t[:, 0:NS], [ss.b], [rs.b], scale=1.0 / D)
            hT = sb_t("hT", [128, 8, NT], BF16)
            xn = sb_t("xn", [128, D], BF16)
            for s_ in range(NS):
                ts("dve", xn.t[:], xt[s_].t[:], rs.t[:, s_:s_ + 1], None, ALU.mult, None, [xt[s_].b, rs.b], [xn.b])
                pb = nextB()
                with P.grp("pe"):
                    for d_ in range(8):
                        tr(pb.t[:, 128 * d_:128 * d_ + 128], xn.t[:, 128 * d_:128 * d_ + 128], ident_b.t[:], [xn.b, ident_b.b], [pb.b], inc=(d_ == 7))
                cp("act", hT.t[:, :, 128 * s_:128 * s_ + 128], pb.t[:, :].rearrange("p (a b) -> p a b", a=8), [pb.b], [hT.b])

            def proj_fm(col0, M):
                ps = nextF()
                wctr[0] += 1
                wp = sb_t("wp%d" % (wctr[0] % 3), [128, 8, 128], BF16)
                P.dma(wp.t[:], WinD[:, PIECE_COL0.index(col0), :].rearrange("p (a b) -> p a b", a=8), [WinD_b], [wp.b], nbytes=262144)
                with P.grp("pe"):
                    for d_ in range(8):
                        mm(ps.t[0:M, 0:N], wp.t[:, d_, 0:M], hT.t[:, d_, 0:N], d_ == 0, d_ == 7, [wp.b, hT.b], [ps.b])
                return ps

            uT = sb_t("uT", [128, 4, NT], BF16)
            ubd = sb_t("ubd", [128, 4, 4, NT], BF16)
            sg = sb_t("sg", [128, 4, NT], BF16)
            sgm = sb_t("sgm", [128, 4, NT], BF16)
            cq = sb_t("cq", [128, 2, NT], BF16)
            for i in range(4):
                ps = proj_fm(128 * i, 128)
                cp("dve", uT.t[:, i, 0:N], ps.t[:, 0:N], [ps.b], [uT.b])
                for k_ in range(4):
                    ts("dve", ubd.t[:, i, k_, 0:N], ps.t[:, 0:N], gc.t[:, 18 + k_:19 + k_], None, ALU.mult, None, [ps.b, gc.b], [ubd.b])
            for i in range(4):
                ps = proj_fm(512 + 128 * i, 128)
                th_ = sb_t("jk", [128, 160], F32)
                act(th_.t[:, 0:N], ps.t[:, 0:N], AF.Tanh, [ps.b], [th_.b], scale=0.5)
                stt(sg.t[:, i, 0:N], th_.t[:, 0:N], 1.0, ps.t[:, 0:N], ALU.add, ALU.mult, [th_.b, ps.b], [sg.b])
            for i in range(4):
                ps = proj_fm(1440 + 128 * i, 128)
                th_ = sb_t("jk", [128, 160], F32)
                act(th_.t[:, 0:N], ps.t[:, 0:N], AF.Tanh, [ps.b], [th_.b], scale=0.5)
                stt(sgm.t[:, i, 0:N], th_.t[:, 0:N], 1.0, ps.t[:, 0:N], ALU.add, ALU.mult, [th_.b, ps.b], [sgm.b])
            for i in range(2):
                ps = proj_fm(1024 + 128 * i, 128)
                cp("dve", cq.t[:, i, 0:N], ps.t[:, 0:N], [ps.b], [cq.b])
            ckvT = sb_t("ckvT", [128, NT], BF16)
            krT = sb_t("krT", [128, NT], BF16)
            for s_ in range(NS):
                ps = nextF()
                wt_ = sb_t("wtm", [128, 8, 160], BF16)
                if s_ == 0:
                    P.dma(wt_.t[:], WinT[:, :].rearrange("p (a b) -> p a b", a=8), [WinT_b], [wt_.b], nbytes=327680)
                with P.grp("pe"):
                    for d_ in range(8):
                        mm(ps.t[:, 0:160], hT.t[:, d_, 128 * s_:128 * s_ + 128], wt_.t[:, d_, :], d_ == 0, d_ == 7, [wt_.b, hT.b], [ps.b])
                st2 = sb_t("st2", [128, 4], F32); jk = sb_t("jk", [128, 160], F32)
                act(jk.t[:, 0:128], ps.t[:, 0:128], AF.Square, [ps.b], [jk.b, st2.b], accum_out=st2.t[:, 0:1])
                act(jk.t[:, 128:160], ps.t[:, 128:160], AF.Square, [ps.b], [jk.b, st2.b], accum_out=st2.t[:, 1:2])
                rsqrt_pow(st2.t[:, 2:3], st2.t[:, 0:1], [st2.b], [st2.b], scale=1.0 / 128)
                rsqrt_pow(st2.t[:, 3:4], st2.t[:, 1:2], [st2.b], [st2.b], scale=1.0 / 32)
                okv = sb_t("okv", [128, 128], F32); okr = sb_t("okr", [128, 32], F32); kn_ = sb_t("kn_", [128, 32], F32)
                stt(okv.t[:], ps.t[:, 0:128], st2.t[:, 2:3], gr.t[:, 0:128], ALU.mult, ALU.mult, [ps.b, st2.b, gr.b], [okv.b])
                stt(kn_.t[:], ps.t[:, 128:160], st2.t[:, 3:4], gr.t[:, 128:160], ALU.mult, ALU.mult, [ps.b, st2.b, gr.b], [kn_.b])
                ti = tabidx0 + s_
                r1 = sb_t("r1", [128, 32], F32); r2 = sb_t("r2", [128, 32], F32)
                tt("dve", r1.t[:], kn_.t[:], cosT.t[:, ti, :], ALU.mult, [kn_.b, cosT.b], [r1.b])
                tt("dve", r2.t[:, 0:16], kn_.t[:, 16:32], sinT.t[:, ti, 0:16], ALU.mult, [kn_.b, sinT.b], [r2.b])
                tt("dve", r2.t[:, 16:32], kn_.t[:, 0:16], sinT.t[:, ti, 16:32], ALU.mult, [kn_.b, sinT.b], [r2.b])
                tt("dve", okr.t[:], r1.t[:], r2.t[:], ALU.add, [r1.b, r2.b], [okr.b])
                P.dma(ckv_out[128 * s_:128 * s_ + 128, :], okv.t[:], [okv.b], [], is_output=True)
                P.dma(kr_out[128 * s_:128 * s_ + 128, :], okr.t[:], [okr.b], [], is_output=True)
                tb = sb_t("tb", [128, 256], BF16)
                cp("dve", tb.t[:, 0:128], okv.t[:], [okv.b], [tb.b])
                cp("dve", tb.t[:, 128:256].rearrange("p (a b) -> p a b", a=4), okr.t[:, :].unsqueeze(1).to_broadcast([128, 4, 32]), [okr.b], [tb.b])
                pb = nextB()
                with P.grp("pe"):
                    tr(pb.t[:, 0:128], tb.t[:, 0:128], ident_b.t[:], [tb.b, ident_b.b], [pb.b], inc=False)
                    tr(pb.t[:, 128:256], tb.t[:, 128:256], ident_b.t[:], [tb.b, ident_b.b], [pb.b])
                cp("act", ckvT.t[:, 128 * s_:128 * s_ + 128], pb.t[:, 0:128], [pb.b], [ckvT.b])
                cp("act", krT.t[:, 128 * s_:128 * s_ + 128], pb.t[:, 128:256], [pb.b], [krT.b])

            if flags.get('upto', 9) < 2:
                return
            stage[0] = 'B'
            Xs = sb_t("Xs", [128, 2, 16, NT // L], F32)
            Hs = sb_t("Hs", [128, 2, 16, NT // L + 4], F32)
            Hb = sb_t("Hb", [128, 2, 16, NT // L], BF16)
            for ri in range(2):
                xps = [nextF(), nextF()]
                for ct in range(4):
                    ps = xps[ct // 2]
                    c0_ = 4 * NC * (ct % 2)
                    with P.grp("pe"):
                        for tau in range(8):
                            mm(ps.t[:, c0_:c0_ + 4 * NC].rearrange("q (a k) -> q a k", a=4), Wb.t[:, ct, tau, ri, :], ubd.t[:, ct, :, tau:N:L],
                               tau == 0, tau == 7, [Wb.b, ubd.b], [ps.b])
                for hf in range(2):
                    cp("dve" if hf else "act", Xs.t[:, ri, 8 * hf:8 * hf + 8, 0:NC], xps[hf].t[:, 0:8 * NC].rearrange("p (a b) -> p a b", a=8), [xps[hf].b], [Xs.b])
            if flags.get('s3', 9) < 2:
                return
            SCAN_E = flags.get('scan_engine', 'pool')
            M1 = sb_t("M1", [128, 2, 16], F32); M2 = sb_t("M2", [128, 2, 16], F32)
            A8r = A8.t[:, 0, :].unsqueeze(1).to_broadcast([128, 2, 16]); A8i = A8.t[:, 1, :].unsqueeze(1).to_broadcast([128, 2, 16])
            segs = [(0, NC)] if not sample else [(4 * s_, 4) for s_ in range(4)]
            for sgi, (k0, nk) in enumerate(segs):
                base = k0 + sgi
                if sample:
                    h0t = sb_t("h0t", [128, 4, 2, 16], F32)
                    if sgi == 0:
                        P.dma(h0t.t[:, :, 0, :], h0r[:, :, :], [], [h0t.b])
                        P.dma(h0t.t[:, :, 1, :], h0i[:, :, :], [], [h0t.b])
                    cp(SCAN_E, Hs.t[:, :, :, base], h0t.t[:, sgi, :, :], [h0t.b], [Hs.b])
                elif seq_first:
                    mset(Hs.t[:, :, :, base], 0.0, [Hs.b], e=SCAN_E)
                else:
                    cp(SCAN_E, Hs.t[:, :, :, base], Hst.t[:, :, :], [Hst.b], [Hs.b])
                for j in range(nk):
                    hp_ = Hs.t[:, :, :, base + j]; hn_ = Hs.t[:, :, :, base + j + 1]
                    tt(SCAN_E, M1.t[:], hp_, A8r, ALU.mult, [Hs.b, A8.b], [M1.b])
                    tt(SCAN_E, M2.t[:], hp_, A8i, ALU.mult, [Hs.b, A8.b], [M2.b])
                    tt(SCAN_E, M1.t[:], M1.t[:], Xs.t[:, :, :, k0 + j], ALU.add, [M1.b, Xs.b], [M1.b])
                    tt(SCAN_E, Hs.t[:, 0, :, base + j + 1], M1.t[:, 0, :], M2.t[:, 1, :], ALU.subtract, [M1.b, M2.b], [Hs.b])
                    tt(SCAN_E, Hs.t[:, 1, :, base + j + 1], M1.t[:, 1, :], M2.t[:, 0, :], ALU.add, [M1.b, M2.b], [Hs.b])
                cp("dve", Hb.t[:, :, :, k0:k0 + nk], Hs.t[:, :, :, base:base + nk], [Hs.b], [Hb.b])
                if sample:
                    hso = sb_t("hso", [128, 4, 2, 16], F32)
                    cp(SCAN_E, hso.t[:, sgi, :, :], Hs.t[:, :, :, base + nk], [Hs.b], [hso.b])
                    if sgi == 3:
                        P.dma(ost_s[:, :, :, :], hso.t[:], [hso.b], [], is_output=True)
                else:
                    cp(SCAN_E, Hst.t[:, :, :], Hs.t[:, :, :, base + nk], [Hs.b], [Hst.b])
                    if seq_last:
                        hpo = sb_t("hpo", [128, 2, 16], F32)
                        cp(SCAN_E, hpo.t[:], Hst.t[:], [Hst.b], [hpo.b])
                        P.dma(ost_p[:, seqidx, :, :], hpo.t[:], [hpo.b], [], is_output=True)
            if flags.get('s3', 9) < 3:
                return
            yg = sb_t("yg", [128, 4, NT], BF16)
            for ct in range(4):
                yp_ = nextF()
                for tau in range(8):
                    o_ = yp_.t[:, NC * tau:NC * tau + NC]
                    with P.grp("pe"):
                        for lag in range(tau + 1):
                            mm(o_, Kbd.t[:, ct, lag, :], uT.t[:, ct, tau - lag:N:L], lag == 0, False, [Kbd.b, uT.b], [yp_.b], inc=False)
                        for p4 in range(4):
                            pr = 4 * ct + p4
                            for ri in range(2):
                                last = (p4 == 3 and ri == 1)
                                kw = {"tile_position": (0, 96)} if p4 == 3 else {}
                                mm(yp_.t[32 * p4:32 * p4 + 32, NC * tau:NC * tau + NC], Vd.t[:, pr, ri, tau, :], Hb.t[:, ri, pr, 0:NC], False, last,
                                   [Vd.b, Hb.b], [yp_.b], inc=last, **kw)
                g1 = sb_t("rs_ssm", [128, NT], F32); g2 = sb_t("sig", [128, NT], BF16)
                act(g1.t[:, 0:N], yp_.t[:, 0:N], AF.Square, [yp_.b], [g1.b])
                ts("dve", g1.t[:, 0:N], g1.t[:, 0:N], 0.044715, 1.0, ALU.mult, ALU.add, [g1.b], [g1.b])
                tt("dve", g1.t[:, 0:N], g1.t[:, 0:N], yp_.t[:, 0:N], ALU.mult, [g1.b, yp_.b], [g1.b])
                act(g2.t[:, 0:N], g1.t[:, 0:N], AF.Tanh, [g1.b], [g2.b], scale=0.7978845608028654)
                stt(yg.t[:, ct, 0:N].rearrange("p (k t) -> p t k", t=L), g2.t[:, 0:N].rearrange("p (t k) -> p t k", t=L), 1.0,
                    yp_.t[:, 0:N].rearrange("p (t k) -> p t k", t=L), ALU.add, ALU.mult, [g2.b, yp_.b], [yg.b])
            if flags.get('s3', 9) < 5:
                return
            ys_ = sb_t("ys_", [128, 4, NT], BF16)
            sq = sb_t("sq", [128, 4, NT], BF16)
            for co in range(4):
                ps = nextF()
                with P.grp("pe"):
                    for ci_ in range(4):
                        mm(ps.t[:, 0:N], Wglu.t[:, ci_, 128 * co:128 * co + 128], yg.t[:, ci_, 0:N], ci_ == 0, ci_ == 3, [Wglu.b, yg.b], [ps.b])
                sig = sb_t("sig", [128, NT], BF16)
                act(sig.t[:, 0:N], ps.t[:, 0:N], AF.Tanh, [ps.b, hbglu.b], [sig.b], bias=hbglu.t[:, co:co + 1], scale=0.25)
                stt(ys_.t[:, co, 0:N], sig.t[:, 0:N], 1.0, yg.t[:, co, 0:N], ALU.add, ALU.mult, [sig.b, yg.b], [ys_.b])
                act(sq.t[:, co, 0:N], ys_.t[:, co, 0:N], AF.Square, [ys_.b], [sq.b])
            def bc_rstd(sqt, ntile, onesT, name, scale=1.0):
                ps = nextF()
                with P.grp("pe"):
                    for i in range(ntile):
                        mm(ps.t[:, 0:N], onesT.t[:], sqt.t[:, i, 0:N], i == 0, i == ntile - 1, [onesT.b, sqt.b], [ps.b])
                r_ = sb_t(name, [128, NT], F32)
                rsqrt_pow(r_.t[:, 0:N], ps.t[:, 0:N], [ps.b], [r_.b], scale=scale)
                return r_
            rs_ssm = bc_rstd(sq, 4, ones512, "rs_ssm", scale=1.0 / 16)
            mix = sb_t("mix", [128, 8, NT], BF16)
            for ct in range(4):
                tt("dve", ys_.t[:, ct, 0:N], ys_.t[:, ct, 0:N], sg.t[:, ct, 0:N], ALU.mult, [ys_.b, sg.b], [ys_.b])
                tt("dve", mix.t[:, ct, 0:N], ys_.t[:, ct, 0:N], rs_ssm.t[:, 0:N], ALU.mult, [ys_.b, rs_ssm.b], [mix.b])

            if flags.get('upto', 9) < 3:
                return
            stage[0] = 'S4'
            sqq = sb_t("sq4", [128, 4, NT], BF16)
            for i in range(2):
                act(sqq.t[:, i, 0:N], cq.t[:, i, 0:N], AF.Square, [cq.b], [sqq.b])
            rq = bc_rstd(sqq, 2, ones256, "rq")
            rq2 = sb_t("rq2", [128, NT], F32)
            tt("dve", rq2.t[:, 0:N], rq.t[:, 0:N], rq.t[:, 0:N], ALU.mult, [rq.b], [rq2.b])
            QTn = sb_t("QTn", [128, 4, 2 * NT], BF16); QTr = sb_t("QTr", [128, 4, 2 * NT], BF16)

            def headnorm(ps, blk, rq_, rq2_, name):
                s2 = sb_t("hn_s2", [128, NT], BF16)
                act(s2.t[:, 0:N], ps.t[:, 0:N], AF.Square, [ps.b], [s2.b])
                p2 = nextF()
                mm(p2.t[:, 0:N], blk.t[:], s2.t[:, 0:N], True, True, [blk.b, s2.b], [p2.b])
                t_ = sb_t("hn_t", [128, NT], F32)
                if rq_ is not None:
                    tt("dve", t_.t[:, 0:N], p2.t[:, 0:N], rq2_.t[:, 0:N], ALU.mult, [p2.b, rq2_.b], [t_.b])
                    rsqrt_pow(t_.t[:, 0:N], t_.t[:, 0:N], [t_.b], [t_.b])
                else:
                    rsqrt_pow(t_.t[:, 0:N], p2.t[:, 0:N], [p2.b], [t_.b])
                if rq_ is not None:
                    tt("dve", t_.t[:, 0:N], t_.t[:, 0:N], rq_.t[:, 0:N], ALU.mult, [t_.b, rq_.b], [t_.b])
                return t_

            def qproj(col0):
                ps = nextF()
                with P.grp("pe"):
                    for kt in range(2):
                        mm(ps.t[:, 0:N], Wuq.t[:, kt, col0:col0 + 128], cq.t[:, kt, 0:N], kt == 0, kt == 1, [Wuq.b, cq.b], [ps.b])
                return ps

            for i in range(4):
                ps = qproj(128 * i)
                t_ = headnorm(ps, blk64, rq, rq2, "qn")
                stt(QTn.t[:, i, 0:N], ps.t[:, 0:N], gc.t[:, 16:17], t_.t[:, 0:N], ALU.mult, ALU.mult, [ps.b, gc.b, t_.b], [QTn.b])
                stt(QTn.t[:, i, NT:NT + N], ps.t[:, 0:N], gc.t[:, 17:18], t_.t[:, 0:N], ALU.mult, ALU.mult, [ps.b, gc.b, t_.b], [QTn.b])
            if sample:
                cos_q = cosF.t[:, 1024:1056].unsqueeze(1).to_broadcast([128, 4, 32]); sin_q = sinF.t[:, 1024:1056].unsqueeze(1).to_broadcast([128, 4, 32])
                vq = lambda ap: ap.rearrange("p (a b) -> p a b", a=4)
            else:
                cos_q = cosF.t[:, pos0:pos0 + N]; sin_q = sinF.t[:, pos0:pos0 + N]
                vq = lambda ap: ap
            for i in range(2):
                ps = qproj(512 + 128 * i)
                t_ = headnorm(ps, blk32, rq, rq2, "qr")
                qa = sb_t("qa", [128, NT], F32); qb = sb_t("qb", [128, NT], F32)
                stt(qa.t[:, 0:N], ps.t[:, 0:N], gc.t[:, 4:5], t_.t[:, 0:N], ALU.mult, ALU.mult, [ps.b, gc.b, t_.b], [qa.b])
                ps2 = qproj(768 + 128 * i)
                stt(qb.t[:, 0:N], ps2.t[:, 0:N], gc.t[:, 5:6], t_.t[:, 0:N], ALU.mult, ALU.mult, [ps2.b, gc.b, t_.b], [qb.b])
                tt("dve", vq(qa.t[:, 0:N]), vq(qa.t[:, 0:N]), cos_q, ALU.mult, [qa.b, cosF.b], [qa.b])
                tt("dve", vq(qb.t[:, 0:N]), vq(qb.t[:, 0:N]), sin_q, ALU.mult, [qb.b, sinF.b], [qb.b])
                tt("dve", qa.t[:, 0:N], qa.t[:, 0:N], qb.t[:, 0:N], ALU.add, [qa.b, qb.b], [qa.b])
                for k_ in range(4):
                    h_ = 4 * i + k_
                    ts("dve", QTr.t[:, h_ // 2, (h_ % 2) * NT:(h_ % 2) * NT + N], qa.t[:, 0:N], gc.t[:, 18 + k_:19 + k_], None, ALU.mult, None,
                       [qa.b, gc.b], [QTr.b])

            if sample:
                KTn_new = sb_t("KTnn", [128, 4, 128], BF16); KTr_new = krT
                Vn = sb_t("Vn", [32, 4, 8, 65], BF16)
                kdst = lambda i: KTn_new.t[:, i, 0:N]; kdb = KTn_new.b
            else:
                kdst = lambda i: KTn.t[:, i, pos0:pos0 + N]; kdb = KTn_b[pos0 // 128]
                cp("act", KTr.t[:, pos0:pos0 + N], krT.t[:, 0:N], [krT.b], [KTr_b[pos0 // 128]])
            for i in range(4):
                ps = nextF()
                mm(ps.t[:, 0:N], Wukv.t[:, 128 * i:128 * i + 128], ckvT.t[:, 0:N], True, True, [Wukv.b, ckvT.b], [ps.b])
                t_ = headnorm(ps, blk64, None, None, "kn")
                stt(kdst(i), ps.t[:, 0:N], gc.t[:, 12 + i:13 + i], t_.t[:, 0:N], ALU.mult, ALU.mult, [ps.b, gc.b, t_.b], [kdb])
            if sample:
                mset(Vn.t[:, :, :, 64:65], 1.0, [Vn.b])
                for s_ in range(4):
                    ps = nextF()
                    mm(ps.t[0:32, 0:512], ckvT.t[:, 32 * s_:32 * s_ + 32], Wukv.t[:, 512:1024], True, True, [ckvT.b, Wukv.b], [ps.b])
                    cp("act", Vn.t[:, s_, :, 0:64], ps.t[0:32, 0:512].rearrange("p (h v) -> p h v", h=8), [ps.b], [Vn.b])
            else:
                for s_ in range(NS):
                    ps = nextF()
                    mm(ps.t[:, 0:512], ckvT.t[:, 128 * s_:128 * s_ + 128], Wukv.t[:, 512:1024], True, True, [ckvT.b, Wukv.b], [ps.b])
                    cp("act", Vc.t[:, pos0 // 128 + s_, :, 0:64], ps.t[:, 0:512].rearrange("p (h v) -> p h v", h=8), [ps.b], [Vc_b[pos0 // 128 + s_]])

            attn = sb_t("attn", [128, 4, NT], BF16)
            sqa = sb_t("sq4", [128, 4, NT], BF16)

            def qview(Qt, p, c0, n):
                return Qt.t[:, p, :].rearrange("q (c n) -> q c n", c=2)[:, :, c0:c0 + n]

            def scores_pair(ps3, p, kn_ap, kr_ap, c0, n, R, Wb_):
                with P.grp("pe"):
                    mm(ps3, kn_ap, qview(QTn, p, c0, n), True, False, R + [QTn.b], [Wb_])
                    mm(ps3, kr_ap, qview(QTr, p, c0, n), False, True, R + [QTr.b], [Wb_])

            def finish_pair(ops, p, c0, n, stride):
                w_ = stride + n
                osb = sb_t("osb", [65, 2 * NT], F32)
                cp("act", osb.t[:, 0:w_], ops.t[0:65, 0:w_], [ops.b], [osb.b])
                dps = nextF()
                mm(dps.t[0:64, 0:w_], sel_f.t[0:65, :], osb.t[0:65, 0:w_], True, True, [sel_f.b, osb.b], [dps.b])
                rd = sb_t("rd", [64, 2 * NT], F32)
                recip(rd.t[:, 0:w_], dps.t[0:64, 0:w_], [dps.b], [rd.b])
                for c_ in range(2):
                    tt("dve", attn.t[64 * c_:64 * c_ + 64, p, c0:c0 + n], osb.t[0:64, c_ * stride:c_ * stride + n], rd.t[:, c_ * stride:c_ * stride + n],
                       ALU.mult, [osb.b, rd.b], [attn.b])

            if not sample:
                qb0 = pos0 // 128
                for p in range(4):
                    ops = nextO()
                    nj = qb0 + NS
                    for j in range(nj):
                        lo = max(j, qb0) - qb0
                        nq_ = N - 128 * lo
                        sp_ = nextF()
                        sp3 = sp_.t[:, 0:2 * nq_].rearrange("q (c n) -> q c n", c=2)
                        scores_pair(sp3, p, KTn.t[:, p, 128 * j:128 * j + 128], KTr.t[:, 128 * j:128 * j + 128], 128 * lo, nq_, [KTn_b[j], KTr_b[j]], sp_.b)
                        ectr[0] += 1
                        E = sb_t("E%d" % (ectr[0] % 3), [128, 2 * NT], BF16)
                        E3 = E.t[:, 0:2 * nq_].rearrange("q (c n) -> q c n", c=2)
                        act(E3, sp3, AF.Exp, [sp_.b], [E.b])
                        if j >= qb0:
                            mset(E.t[64:128, 0:2 * nq_].rearrange("q (c n) -> q c n", c=2)[:, :, 0:64], 0.0, [E.b], e="dve")
                        for c_ in range(2):
                            mm(ops.t[0:65, c_ * NT + 128 * lo:c_ * NT + N], Vc.t[:, j, 2 * p + c_, :], E.t[:, c_ * nq_:(c_ + 1) * nq_], j == 0 and c_ == 0,
                               j == nj - 1, [Vc_b[j], E.b], [ops.b], inc=True)
                    finish_pair(ops, p, 0, N, NT)
            else:
                for s_ in range(4):
                    prep_cache(s_)
                    for p in range(4):
                        ops = nextO()
                        for j in range(9):
                            sp_ = nextF()
                            ectr[0] += 1
                            E = sb_t("E%d" % (ectr[0] % 3), [128, 2 * NT], BF16)
                            if j < 8:
                                sp3 = sp_.t[:, 0:64].rearrange("q (c n) -> q c n", c=2)
                                jj = 8 * (s_ % 2) + j
                                scores_pair(sp3, p, KTc[s_].t[:, p, 128 * jj:128 * jj + 128], KRc[s_].t[:, 128 * jj:128 * jj + 128], 32 * s_, 32, [KTn_b[jj], KTr_b[jj]], sp_.b)
                                act(E.t[:, 0:64], sp_.t[:, 0:64], AF.Exp, [sp_.b], [E.b])
                                for c_ in range(2):
                                    mm(ops.t[0:65, 32 * c_:32 * c_ + 32], Vcc[s_].t[:, jj, 2 * p + c_, :], E.t[:, 32 * c_:32 * c_ + 32], j == 0 and c_ == 0, False, [Vc_b[jj], E.b], [ops.b], inc=True)
                            else:
                                sp3 = sp_.t[0:32, 0:64].rearrange("q (c n) -> q c n", c=2)
                                scores_pair(sp3, p, KTn_new.t[:, p, 32 * s_:32 * s_ + 32], KTr_new.t[:, 32 * s_:32 * s_ + 32], 32 * s_, 32, [KTn_new.b, KTr_new.b], sp_.b)
                                act(E.t[0:32, 0:64], sp_.t[0:32, 0:64], AF.Exp, [sp_.b], [E.b])
                                for c_ in range(2):
                                    mm(ops.t[0:65, 32 * c_:32 * c_ + 32], Vn.t[0:32, s_, 2 * p + c_, :], E.t[0:32, 32 * c_:32 * c_ + 32], False, True, [Vn.b, E.b], [ops.b], inc=True)
                        finish_pair(ops, p, 32 * s_, 32, 32)
            for i in range(4):
                act(sqa.t[:, i, 0:N], attn.t[:, i, 0:N], AF.Square, [attn.b], [sqa.b])
            rs_mla = bc_rstd(sqa, 4, ones512, "rs_mla")
            for i in range(4):
                tt("dve", attn.t[:, i, 0:N], attn.t[:, i, 0:N], sgm.t[:, i, 0:N], ALU.mult, [attn.b, sgm.b], [attn.b])
                tt("dve", mix.t[:, 4 + i, 0:N], attn.t[:, i, 0:N], rs_mla.t[:, 0:N], ALU.mult, [attn.b, rs_mla.b], [mix.b])

            if flags.get('upto', 9) < 4:
                return
            stage[0] = 'B'
            for s_ in range(NS):
                for half in range(2):
                    ps = nextF()
                    with P.grp("pe"):
                        for kt in range(8):
                            mm(ps.t[:, 0:512], mix.t[:, kt, 128 * s_:128 * s_ + 128], Wout.t[:, kt, 512 * half:512 * half + 512], kt == 0, kt == 7, [mix.b, Wout.b], [ps.b])
                    yo = sb_t("yo", [128, 512], F32)
                    tt("dve", yo.t[:], ps.t[:, 0:512], xt[s_].t[:, 512 * half:512 * half + 512], ALU.add, [ps.b, xt[s_].b], [yo.b])
                    P.dma(yout[128 * s_:128 * s_ + 128, 512 * half:512 * half + 512], yo.t[:], [yo.b], [], is_output=True)

        pool_tiles = {}

        tidx = [0]
        DOUBLE = set(flags.get("double", ("hT", "uT", "sg", "sgm", "cq", "ckvT", "krT", "st2", "okv", "okr", "kn_", "r1", "r2", "tb", "ss", "rs",
                                          "Xs", "Hs", "Hb", "yg", "ys_", "sig", "rs_ssm", "rq", "rq2", "hn_s2", "hn_t",
                                          "attn", "rs_mla", "M1", "M2")))

        def sb_t(name, shape, dt):
            key = name + ("_%d" % (tidx[0] % 2) if name in DOUBLE else "")
            if key not in pool_tiles:
                pool_tiles[key] = sb(shape, dt, key)
            return pool_tiles[key]

        KTc, KRc, Vcc = [], [], []
        if flags.get("sample", True):
            ktc = KTn; krc = KTr; vcc = Vc
            for s_ in range(4):
                KTc.append(ktc); KRc.append(krc); Vcc.append(vcc)

            def prep_cache(s_):
                o8 = 8 * (s_ % 2)
                cT = sb_t("cT", [128, 1024], BF16)
                for j in range(8):
                    cl = sb_t("cl", [128, 160], F32)
                    P.dma(cl.t[:, 0:128], cckv[s_, 128 * j:128 * j + 128, :], [], [cl.b])
                    P.dma(cl.t[:, 128:160], ckr[s_, 128 * j:128 * j + 128, :], [], [cl.b])
                    tb = sb_t("tb", [128, 256], BF16)
                    cp("dve", tb.t[:, 0:128], cl.t[:, 0:128], [cl.b], [tb.b])
                    cp("dve", tb.t[:, 128:256].rearrange("p (a b) -> p a b", a=4), cl.t[:, 128:160].unsqueeze(1).to_broadcast([128, 4, 32]), [cl.b], [tb.b])
                    pb = nextB()
                    with P.grp("pe"):
                        tr(pb.t[:, 0:128], tb.t[:, 0:128], ident_b.t[:], [tb.b, ident_b.b], [pb.b], inc=False)
                        tr(pb.t[:, 128:256], tb.t[:, 128:256], ident_b.t[:], [tb.b, ident_b.b], [pb.b])
                    cp("act", cT.t[:, 128 * j:128 * j + 128], pb.t[:, 0:128], [pb.b], [cT.b])
                    cp("act", krc.t[:, 128 * (o8 + j):128 * (o8 + j) + 128], pb.t[:, 128:256], [pb.b], [KTr_b[o8 + j]])
                for j in range(8):
                    ps = nextF()
                    mm(ps.t[:, 0:512], cT.t[:, 128 * j:128 * j + 128], Wukv.t[:, 512:1024], True, True, [cT.b, Wukv.b], [ps.b])
                    cp("act", vcc.t[:, o8 + j, :, 0:64], ps.t[:, 0:512].rearrange("p (h v) -> p h v", h=8), [ps.b], [Vc_b[o8 + j]])
                for i in range(4):
                    for hf in range(2):
                        ps = nextF()
                        mm(ps.t[:, 0:512], Wukv.t[:, 128 * i:128 * i + 128], cT.t[:, 512 * hf:512 * hf + 512], True, True, [Wukv.b, cT.b], [ps.b])
                        s2 = sb_t("sq4", [128, 4, NT], BF16)
                        s2v = s2.t[:, :, :].rearrange("p a b -> p (a b)")
                        act(s2v, ps.t[:, 0:512], AF.Square, [ps.b], [s2.b])
                        p2 = nextF()
                        mm(p2.t[:, 0:512], blk64.t[:], s2v, True, True, [blk64.b, s2.b], [p2.b])
                        t_ = sb_t("kt_", [128, 512], F32)
                        rsqrt_pow(t_.t[:], p2.t[:, 0:512], [p2.b], [t_.b])
                        stt(ktc.t[:, i, 128 * o8 + 512 * hf:128 * o8 + 512 * hf + 512], ps.t[:, 0:512], gc.t[:, 12 + i:13 + i], t_.t[:], ALU.mult, ALU.mult, [ps.b, gc.b, t_.b],
                            KTn_b[o8 + 4 * hf:o8 + 4 * hf + 4])

        if flags.get('sched', True):
            P.start_recording()
        if flags.get("sample", True) and flags.get('upto', 9) >= 1:
            tile_pass(128, xs, ys, ockv_s, okr_s, 1024, 16, True, True, True, 0)
        nseq = flags.get("nseq", 2) if flags.get('upto', 9) >= 1 else 0
        ntile = flags.get("ntile", SEQ // NT)
        for sq_ in range(nseq):
            for it in range(ntile):
                r0 = sq_ * SEQ + it * NT
                tile_pass(NT, xp[r0:r0 + NT, :], yp[r0:r0 + NT, :], ockv_p[r0:r0 + NT, :], okr_p[r0:r0 + NT, :],
                          it * NT, (it * NT) // 128, False, it == 0, it == ntile - 1, sq_)
        if flags.get('sched', True):
            P.sched_slack = flags.get('slack', 0.25)
            P.sched_tabpen = flags.get('tabpen', 1.3)
            P.sched_xlat = flags.get('xlat', 0.4)
            P.schedule_and_emit(flags.get('window', 800))
        P.finish()
    return nc


def _host_weights(inp):
    f = lambda a: np.ascontiguousarray(np.asarray(a, dtype=np.float32))
    W = {}
    W["w_in"] = f(inp["w_in"][0].reshape(8, 128, 1952).transpose(1, 0, 2))
    W["nin"] = f(inp["norm_in"][0].reshape(8, 128).T)
    wuq = np.asarray(inp["w_uq"][0]).reshape(256, 8, 96)
    sw = np.concatenate([np.arange(16, 32), np.arange(0, 16)])
    wq = np.concatenate([wuq[:, :, :64].reshape(256, 512), wuq[:, :, 64:].reshape(256, 256), wuq[:, :, 64:][:, :, sw].reshape(256, 256)], axis=1)
    W["w_uq"] = f(wq.reshape(2, 128, 1024).transpose(1, 0, 2))
    W["nq"] = f(inp["q_lora_norm"][0].reshape(2, 128).T)
    wkv = np.asarray(inp["w_ukv"][0]).reshape(128, 8, 128)
    W["w_ukv"] = f(np.concatenate([wkv[:, :, :64].reshape(128, 512), wkv[:, :, 64:].reshape(128, 512)], axis=1))
    W["w_glu"] = f(inp["w_glu"][0].reshape(4, 128, 512).transpose(1, 0, 2))
    W["b_glu"] = f(inp["b_glu"][0].reshape(4, 128).T)
    W["w_out"] = f(inp["w_out"][0].reshape(8, 128, 1024).transpose(1, 0, 2))
    W["nout"] = f(np.concatenate([inp["out_norm_ssm"][0], inp["out_norm_mla"][0]]).reshape(8, 128).T)
    gcols = np.zeros((128, 24), np.float32)
    qn = np.asarray(inp["q_nope_norm"][0]); qr = np.asarray(inp["q_rope_norm"][0]); kn = np.asarray(inp["k_nope_norm"][0])
    for i in range(4):
        gcols[:, i] = np.tile(qn, 2)
        gcols[:, 12 + i] = np.tile(kn, 2)
        gcols[:, 8 + i] = np.asarray(inp["ssm_d"][0])[128 * i:128 * i + 128]
    gcols[:, 4] = np.tile(qr, 4)
    pidx = np.arange(128)
    gcols[:, 16] = np.tile(qn, 2) * (pidx < 64)
    gcols[:, 17] = np.tile(qn, 2) * (pidx >= 64)
    for k_ in range(4):
        gcols[:, 18 + k_] = (pidx // 32 == k_)
    gcols[:, 5] = np.tile(qr[sw], 4)
    gcols[:, 6] = np.tile(np.concatenate([-np.ones(16), np.ones(16)]), 4)
    gcols[:, 7] = np.tile(np.concatenate([np.arange(16), np.arange(16)]), 4)
    W["gcols"] = gcols
    W["grows"] = f(np.tile(np.concatenate([inp["kv_lora_norm"][0], inp["k_rope_norm"][0]])[None, :], (128, 1)))
    cmx = np.zeros((128, 5, 128), np.float32)
    cmx[:, 0] = np.eye(128)
    p = np.arange(128)
    cmx[:, 1] = (p[:, None] // 64 == p[None, :] // 64) / 64.0
    cmx[:, 2] = (p[:, None] // 32 == p[None, :] // 32) / 32.0
    cmx[:, 3] = 1.0 / 256
    cmx[:, 4] = 1.0 / 512
    W["cm"] = cmx
    sel = np.zeros((128, 64), np.float32); sel[64, :] = 1.0
    W["sel"] = sel
    are = np.asarray(inp["ssm_a_re"][0]); aim = np.asarray(inp["ssm_a_im"][0]); ldt = np.asarray(inp["ssm_log_dt"][0])
    bre = np.asarray(inp["ssm_b_re"][0]); bim = np.asarray(inp["ssm_b_im"][0])
    cre = np.asarray(inp["ssm_c_re"][0]); cim = np.asarray(inp["ssm_c_im"][0])
    def l2(a):
        return a.reshape(16, 2, 64).transpose(1, 2, 0).reshape(128, 16)
    W["a2"] = f(np.stack([l2(are), l2(aim), l2(np.tile(ldt[:, None], (1, 64)))], axis=1))
    def l1(a):
        v = a.reshape(4, 4, 2, 64)
        v = v.transpose(1, 0, 2, 3)
        v = np.broadcast_to(v[:, None, None], (4, 2, 16, 4, 2, 64))
        return v.reshape(128, 512)
    W["a1"] = f(np.stack([l1(are), l1(aim), l1(np.tile(ldt[:, None], (1, 64)))], axis=1))
    def lbt(b):
        v = b.reshape(4, 4, 2, 64, 16)
        out = np.zeros((4, 2, 16, 4, 2, 64), np.float32)
        for g2 in range(2):
            out[:, g2, :, :, g2, :] = v[:, :, g2].transpose(1, 3, 0, 2)
        return out.reshape(128, 512)
    W["bt1"] = f(np.stack([lbt(bre), lbt(bim)], axis=1))
    def lb2(b):
        v = b.reshape(16, 2, 64, 16)
        out = np.zeros((2, 64, 16, 4, 2, 16), np.float32)
        for pr in range(16):
            for g2 in range(2):
                out[g2, :, pr, pr % 4, g2, :] = v[pr, g2]
        return out.reshape(128, 16, 128)
    W["b2"] = f(np.stack([lb2(bre), lb2(bim)], axis=1))
    def lc2(c_):
        v = c_.reshape(16, 2, 16, 64)
        out = np.zeros((2, 64, 16, 2, 16), np.float32)
        for g2 in range(2):
            out[g2, :, :, g2, :] = v[:, g2].transpose(2, 0, 1)
        return out.reshape(128, 16, 32)
    W["c2"] = f(np.stack([lc2(cre), lc2(cim)], axis=1))
    return W


FLAGS = {}


def kernel(**inp):
    inp = {k: np.asarray(v) for k, v in inp.items()}
    W = _host_weights(inp)
    nc = build_program(FLAGS)
    in_maps = []
    for c in range(NCORES):
        m = dict(W)
        m["xp"] = np.ascontiguousarray(inp["x_prompt"][2 * c:2 * c + 2].reshape(2 * SEQ, D))
        m["xs"] = np.ascontiguousarray(inp["x_sample"][4 * c:4 * c + 4].reshape(128, D))
        m["cckv"] = np.ascontiguousarray(inp["cache_ckv"][0, 4 * c:4 * c + 4])
        m["ckr"] = np.ascontiguousarray(inp["cache_krope"][0, 4 * c:4 * c + 4])
        for nm, key in (("h0r", "state_ssm_re"), ("h0i", "state_ssm_im")):
            v = inp[key][0, 4 * c:4 * c + 4].reshape(4, 16, 2, 64)
            m[nm] = np.ascontiguousarray(v.transpose(2, 3, 0, 1).reshape(128, 4, 16))
        in_maps.append(m)
    res = run_bass_kernel_spmd(nc, in_maps, core_ids=list(range(NCORES)))
    R = res.results
    yp = np.concatenate([R[c]["yp"].reshape(2, SEQ, D) for c in range(NCORES)], axis=0)
    ysm = np.concatenate([R[c]["ys"].reshape(4, 32, D) for c in range(NCORES)], axis=0)
    ckv_p = np.concatenate([R[c]["ockv_p"].reshape(2, SEQ, 128) for c in range(NCORES)], axis=0)[None]
    kr_p = np.concatenate([R[c]["okr_p"].reshape(2, SEQ, 32) for c in range(NCORES)], axis=0)[None]
    ckv_s = np.concatenate([R[c]["ockv_s"].reshape(4, 32, 128) for c in range(NCORES)], axis=0)[None]
    kr_s = np.concatenate([R[c]["okr_s"].reshape(4, 32, 32) for c in range(NCORES)], axis=0)[None]

    def unst(a, nseq):
        v = a.reshape(2, 64, nseq, 2, 16).transpose(2, 3, 4, 0, 1)
        v = v.reshape(nseq, 2, 32, 64)
        return v[:, 0], v[:, 1]
    sp_ = [unst(R[c]["ost_p"], 2) for c in range(NCORES)]
    ss_ = [unst(R[c]["ost_s"], 4) for c in range(NCORES)]
    re_p = np.concatenate([a for a, _ in sp_], axis=0)[None]; im_p = np.concatenate([b for _, b in sp_], axis=0)[None]
    re_s = np.concatenate([a for a, _ in ss_], axis=0)[None]; im_s = np.concatenate([b for _, b in ss_], axis=0)[None]
    f = lambda a: np.ascontiguousarray(a, dtype=np.float32)
    return (f(yp), f(ysm), f(ckv_p), f(kr_p), f(re_p), f(im_p), f(ckv_s), f(kr_s), f(re_s), f(im_s))
```

```python
import numpy as np
from contextlib import ExitStack
import concourse.bass as bass
import concourse.mybir as mybir
from concourse.bass_utils import run_bass_kernel_spmd

F32 = mybir.dt.float32
BF16 = mybir.dt.bfloat16
AF = mybir.ActivationFunctionType
ALU = mybir.AluOpType

NCORES = 8
D = 1024
SEQ = 2048
NT = 128
L = 8
EPS = 1e-6
PI = float(np.pi)
ENGS = ("pe", "act", "dve", "pool", "sp")


class Buf:
    __slots__ = ("name", "last_w", "readers", "sem", "semcnt")

    def __init__(self, name):
        self.name = name
        self.last_w = None
        self.readers = []
        self.sem = None
        self.semcnt = 0


class Prog:
    def __init__(self, nc, stack):
        self.nc = nc
        self.stack = stack
        self.eng = {"pe": nc.tensor, "act": nc.scalar, "dve": nc.vector, "pool": nc.gpsimd, "sp": nc.sync}
        self.sem = {e: stack.enter_context(nc.semaphore("s_" + e)) for e in ENGS}
        self.cnt = {e: 0 for e in ENGS}
        self.known = {e: {} for e in ENGS}
        self.vc = {e: [] for e in ENGS}
        self.pending = {e: [] for e in ENGS}
        self.out_events = []
        self.nsem = 0
        self.recording = False
        self.cur = None
        self.nodes = []
        self.dma_latest = {}
        self.sched_slack = 0.0
        self.sched_tabpen = 0.0
        self.sched_xlat = 0.0

    def _need(self, e, dep):
        key, n = dep[0], dep[1]
        if n is None:
            raise RuntimeError("dependency on un-incremented instruction")
        if self.known[e].get(key, 0) >= n:
            return
        if isinstance(key, str):
            self.eng[e].wait_ge(self.sem[key], n)
            snap = self.vc[key][n - 1]
            for g, v in snap.items():
                if v > self.known[e].get(g, 0):
                    self.known[e][g] = v
        else:
            self.eng[e].wait_ge(key, n)
        self.known[e][key] = n

    def _deps(self, e, reads, writes):
        for b in reads:
            if b.last_w is not None:
                self._need(e, b.last_w)
        for b in writes:
            if b.last_w is not None and (b.last_w[0] != e or e != "pe"):
                self._need(e, b.last_w)
            for r in b.readers:
                if r[0] != e or e != "pe":
                    self._need(e, r)

    def start_recording(self):
        self.recording = True
        self.nodes = []
        self.cur = None
        self.rw = {}

    def grp(self, e):
        prog = self

        class _G:
            def __enter__(self_):
                prog.cur = prog._new_node(e)
                return self_

            def __exit__(self_, *a):
                nd = prog.cur
                prog.cur = None
                if nd["ops"]:
                    for k in range(len(nd["ops"]) - 1, -1, -1):
                        if nd["ops"][k][0] == "op":
                            nd["ops"][k][5] = True
                            break
                    prog.nodes.append(nd)
                return False
        return _G()

    def _new_node(self, e):
        return {"e": e, "ops": [], "deps": set(), "cost": 0.0, "lat": 0.0}

    def _rec(self, kind, e, payload, reads, writes, inc, cost, lat=None, tab=None):
        if self.cur is not None:
            nd = self.cur
            assert nd["e"] == e, (nd["e"], e)
        else:
            nd = self._new_node(e)
        me = id(nd)
        for b in reads:
            st_ = self.rw.setdefault(id(b), [None, []])
            if st_[0] is not None and st_[0] is not nd:
                nd["deps"].add(st_[0]["i"] if "i" in st_[0] else None)
        for b in writes:
            st_ = self.rw.setdefault(id(b), [None, []])
            if st_[0] is not None and st_[0] is not nd:
                nd["deps"].add(st_[0].get("i"))
            for r in st_[1]:
                if r is not nd:
                    nd["deps"].add(r.get("i"))
        nd["ops"].append([kind, payload, list(reads), list(writes), None, inc])
        if tab is not None:
            nd["tab"] = tab
        nd["cost"] += cost
        nd["lat"] = max(nd["lat"], lat if lat is not None else 0.0)
        if self.cur is None:
            if kind == "op":
                nd["ops"][-1][5] = True
            nd["i"] = len(self.nodes)
            self.nodes.append(nd)
        else:
            if "i" not in nd:
                nd["i"] = len(self.nodes)
        for b in reads:
            self.rw[id(b)][1].append(nd)
        for b in writes:
            self.rw[id(b)][0] = nd
            self.rw[id(b)][1] = []

    def schedule_and_emit(self, window=48):
        self.recording = False
        nodes = self.nodes
        n = len(nodes)
        for i, nd in enumerate(nodes):
            assert nd["i"] == i, (nd["i"], i)
            nd["deps"].discard(None)
            nd["deps"].discard(i)
        succ = [[] for _ in range(n)]
        left = [0] * n
        for i, nd in enumerate(nodes):
            left[i] = len(nd["deps"])
            for d in nd["deps"]:
                assert d < i
                succ[d].append(i)
        rt = [0.0] * n
        bl = [0.0] * n
        for i in range(n - 1, -1, -1):
            m_ = 0.0
            for j in succ[i]:
                if bl[j] > m_:
                    m_ = bl[j]
            bl[i] = nodes[i]["cost"] + nodes[i]["lat"] + m_
        SLACK = self.sched_slack
        TABPEN = self.sched_tabpen
        cur_tab = [None]
        queues = {e: [i for i in range(n) if nodes[i]["e"] == e] for e in ENGS}
        qpos = {e: 0 for e in ENGS}
        done = [False] * n
        tfree = {e: 0.0 for e in ENGS}
        order = []
        remaining = n
        while remaining:
            best = None
            for e in ENGS:
                q = queues[e]
                p = qpos[e]
                while p < len(q) and done[q[p]]:
                    p += 1
                qpos[e] = p
                cnt = 0
                k = p
                cands_e = []
                while k < len(q) and cnt < window:
                    i = q[k]
                    k += 1
                    if done[i]:
                        continue
                    cnt += 1
                    if left[i] == 0:
                        stt_ = max(tfree[e], rt[i])
                        pen = 0.0
                        if e == "act" and TABPEN > 0:
                            tb_ = nodes[i].get("tab")
                            if tb_ is not None and tb_ != cur_tab[0]:
                                pen = TABPEN
                        cands_e.append((stt_ + pen, i, pen))
                if cands_e:
                    m0 = min(c_[0] for c_ in cands_e)
                    pick = None
                    for c_ in cands_e:
                        if c_[0] <= m0 + SLACK:
                            if pick is None or bl[c_[1]] > bl[pick[1]] + 1e-9 or (abs(bl[c_[1]] - bl[pick[1]]) <= 1e-9 and c_[1] < pick[1]):
                                pick = c_
                    if best is None or pick[0] < best[0] - 1e-9 or (abs(pick[0] - best[0]) <= 1e-9 and pick[1] < best[1]):
                        best = (pick[0], pick[1], e, pick[2])
            assert best is not None, "scheduler deadlock"
            stt_, i, e, pen = best
            nd = nodes[i]
            done[i] = True
            remaining -= 1
            order.append(i)
            if e == "act" and nd.get("tab") is not None:
                cur_tab[0] = nd["tab"]
            tfree[e] = stt_ + nd["cost"]
            fin = stt_ + nd["cost"] + nd["lat"]
            for j in succ[i]:
                left[j] -= 1
                f_ = fin + (self.sched_xlat if nodes[j]["e"] != e else 0.0)
                if f_ > rt[j]:
                    rt[j] = f_
        self.est_makespan = max(tfree.values())
        for i in order:
            nd = nodes[i]
            for kind, payload, reads, writes, _, inc in nd["ops"]:
                if kind == "op":
                    self._emit_op(nd["e"], payload, reads, writes, inc)
                elif kind == "dma":
                    self._emit_dma(*payload)
                elif kind == "selfwait":
                    self._need(nd["e"], (nd["e"], self.cnt[nd["e"]]))
        self.nodes = []

    def selfwait(self, e):
        if getattr(self, "recording", False):
            self._rec("selfwait", e, None, [], [], False, 0.2)
        else:
            self._need(e, (e, self.cnt[e]))

    def op(self, e, fn, reads=(), writes=(), inc=True, cost=0.3, tab=None):
        if getattr(self, "recording", False):
            self._rec("op", e, fn, reads, writes, inc, cost, tab=tab)
            return None
        return self._emit_op(e, fn, reads, writes, inc)

    def _emit_op(self, e, fn, reads=(), writes=(), inc=True):
        self._deps(e, reads, writes)
        ins = fn(self.eng[e])
        if inc:
            self.cnt[e] += 1
            n = self.cnt[e]
            ins.then_inc(self.sem[e], 1)
            snap = dict(self.known[e])
            snap[e] = n
            self.vc[e].append(snap)
            me = (e, n)
            for rec in self.pending[e]:
                rec[1] = n
            self.pending[e] = []
        else:
            me = [e, None]
            self.pending[e].append(me)
        for b in reads:
            b.readers.append(me)
        for b in writes:
            b.last_w = me
            b.readers = []
        return ins

    def dma(self, out, in_, reads=(), writes=(), is_output=False, q="sp", nbytes=65536, **kw):
        if getattr(self, "recording", False):
            self._rec("dma", q, (out, in_, list(reads), list(writes), is_output, q, kw), reads, writes, False, 0.15, lat=3.5 + nbytes / 80e3)
            return None
        return self._emit_dma(out, in_, reads, writes, is_output, q, kw)

    def _emit_dma(self, out, in_, reads, writes, is_output, q, kw):
        self._deps(q, reads, writes)
        owner = writes[0] if (writes and not is_output) else reads[0]
        if owner.sem is None:
            owner.sem = self.stack.enter_context(self.nc.semaphore("d%d" % self.nsem))
            self.nsem += 1
        owner.semcnt += 16
        ins = self.eng[q].dma_start(out=out, in_=in_, **kw)
        ins.then_inc(owner.sem, 16)
        me = (owner.sem, owner.semcnt)
        self.dma_latest[owner.sem] = owner.semcnt
        for b in reads:
            b.readers.append(me)
        for b in writes:
            b.last_w = me
            b.readers = []
        if is_output:
            self.out_events.append(me)
        return ins

    def barrier(self):
        for e in ENGS:
            for sem_, c_ in self.dma_latest.items():
                self._need(e, (sem_, c_))
        for e in ENGS:
            for f in ENGS:
                if f != e and self.cnt[f] > 0:
                    self._need(e, (f, self.cnt[f]))

    def finish(self):
        for ev in self.out_events:
            self._need("sp", ev)


class T:
    __slots__ = ("t", "b")

    def __init__(self, t, name):
        self.t = t
        self.b = Buf(name)


def build_program(flags):
    nc = bass.Bass("TRN2", target_bir_lowering=False)
    dr = {}

    def din(name, shape, dt=F32):
        dr[name] = nc.dram_tensor(name, list(shape), dt, kind="ExternalInput").ap()
        return dr[name]

    def dout(name, shape, dt=F32):
        dr[name] = nc.dram_tensor(name, list(shape), dt, kind="ExternalOutput").ap()
        return dr[name]

    xp = din("xp", [2 * SEQ, D]); xs = din("xs", [128, D])
    cckv = din("cckv", [4, 1024, 128]); ckr = din("ckr", [4, 1024, 32])
    h0r = din("h0r", [128, 4, 16]); h0i = din("h0i", [128, 4, 16])
    w_in = din("w_in", [128, 8, 1952]); nin = din("nin", [128, 8])
    w_uq = din("w_uq", [128, 2, 1024]); nq = din("nq", [128, 2])
    w_ukv = din("w_ukv", [128, 1024])
    w_glu = din("w_glu", [128, 4, 512]); b_glu = din("b_glu", [128, 4])
    w_out = din("w_out", [128, 8, 1024]); nout = din("nout", [128, 8])
    gcols = din("gcols", [128, 24])
    grows = din("grows", [128, 160])
    cm = din("cm", [128, 5, 128])
    sel = din("sel", [128, 64])
    a1 = din("a1", [128, 3, 512])
    bt1 = din("bt1", [128, 2, 512])
    a2 = din("a2", [128, 3, 16])
    b2 = din("b2", [128, 2, 16, 128])
    c2 = din("c2", [128, 2, 16, 32])
    dbg = dout("dbg", [128, 2048]) if flags.get("dbg") else None

    yp = dout("yp", [2 * SEQ, D]); ys = dout("ys", [128, D])
    ockv_p = dout("ockv_p", [2 * SEQ, 128]); okr_p = dout("okr_p", [2 * SEQ, 32])
    ost_p = dout("ost_p", [128, 2, 2, 16])
    ockv_s = dout("ockv_s", [128, 128]); okr_s = dout("okr_s", [128, 32])
    ost_s = dout("ost_s", [128, 4, 2, 16])

    with ExitStack() as st:
        P = Prog(nc, st)
        uid = [0]

        def sb(shape, dt=F32, name=None, stack=st):
            uid[0] += 1
            nm = (name or "t") + "_%d" % uid[0]
            return T(stack.enter_context(nc.sbuf_tensor(nm, list(shape), dt)), nm)

        def psum(shape, dt=F32, name=None):
            uid[0] += 1
            nm = (name or "ps") + "_%d" % uid[0]
            return T(st.enter_context(nc.psum_tensor(nm, list(shape), dt)), nm)

        psF = [psum([128, 512], F32, "psF") for _ in range(7)]
        psB = [psum([128, 1024], BF16, "psB") for _ in range(1)]
        rr = {"F": 0, "B": 0}

        stage = ["F"]
        rr["Bk"] = 0

        def nextF():
            if stage[0] == "F":
                rr["F"] = (rr["F"] + 1) % 2
                return psF[rr["F"]]
            if stage[0] == "S4":
                rr["S4"] = (rr.get("S4", 0) + 1) % 2
                return psF[(5, 6)[rr["S4"]]]
            rr["Bk"] = (rr["Bk"] + 1) % 2
            return psF[2 + rr["Bk"]]

        rr["O"] = 0

        def nextO():
            return psF[4]

        def nextB():
            return psB[0]

        def fsz(ap):
            n = 1
            for d_ in tuple(ap.shape)[1:]:
                n *= d_
            return n

        def ecost(e, ap, mul=1.0):
            n = fsz(ap)
            if e == "act":
                return 0.22 + n / 1200.0
            if e == "dve":
                return 0.08 + mul * n / 900.0
            if e == "pool":
                return 0.25 + n / 450.0
            return 0.3

        def tt(e, out, in0, in1, op, R, W):
            return P.op(e, lambda g: g.tensor_tensor(out=out, in0=in0, in1=in1, op=op), R, W, cost=ecost(e, out))

        def ts(e, out, in0, s1, s2, op0, op1, R, W):
            if s2 is None:
                return P.op(e, lambda g: g.tensor_scalar(out=out, in0=in0, scalar1=s1, scalar2=None, op0=op0), R, W, cost=ecost(e, out))
            return P.op(e, lambda g: g.tensor_scalar(out=out, in0=in0, scalar1=s1, scalar2=s2, op0=op0, op1=op1), R, W, cost=ecost(e, out))

        def stt(out, in0, scalar, in1, op0, op1, R, W):
            return P.op("dve", lambda g: g.scalar_tensor_tensor(out=out, in0=in0, scalar=scalar, in1=in1, op0=op0, op1=op1), R, W, cost=ecost("dve", out))

        def act(out, in_, func, R, W, **kw):
            tab = "T" if func == AF.Tanh else ("L" if func == AF.Ln else None)
            return P.op("act", lambda g: g.activation(out=out, in_=in_, func=func, **kw), R, W, cost=ecost("act", out), tab=tab)

        def cp(e, out, in_, R, W):
            if e == "act":
                return P.op("act", lambda g: g.copy(out=out, in_=in_), R, W, cost=ecost(e, out))
            return P.op(e, lambda g: g.tensor_copy(out=out, in_=in_), R, W, cost=ecost(e, out))

        def recip(out, in_, R, W):
            return P.op("dve", lambda g: g.reciprocal(out=out, in_=in_), R, W, cost=ecost("dve", out, 6.5))

        def rsqrt_pow(out, in_, R, W, scale=1.0, from_psum=True):
            np_ = int(tuple(out.shape)[0])
            act(out, in_, AF.Ln, list(R) + [epsc.b], W, scale=float(scale), bias=epsc.t[0:np_, 0:1])
            act(out, out, AF.Exp, W, W, scale=-0.5)

        def ppow(out, W):
            shp = [int(d_) for d_ in tuple(out.shape)]
            mh = mhalf.t[0:shp[0], 0:1].to_broadcast(shp)
            P.op("pool", lambda g: g.tensor_tensor(out=out, in0=out, in1=mh, op=ALU.pow), list(W) + [mhalf.b], W, cost=ecost("pool", out))

        def mset(ap, val, W, e="pool"):
            return P.op(e, lambda g: g.memset(ap, val), [], W, cost=ecost(e, ap))

        def mm(out, lhsT, rhs, start, stop, R, W, inc=None, **kw):
            if inc is None:
                inc = stop
            ncol = fsz(rhs)
            c_ = 0.035 + max(ncol, 64) * (4.0 if rhs.dtype == F32 else 1.0) / 1600.0
            return P.op("pe", lambda g: g.matmul(out, lhsT=lhsT, rhs=rhs, start=start, stop=stop, **kw), R, W, inc=inc, cost=c_)

        def tr(out, in_, ident, R, W, inc=True):
            return P.op("pe", lambda g: g.transpose(out=out, in_=in_, identity=ident), R, W, inc=inc, cost=0.12)

        ident_b = sb([128, 128], BF16, "ident"); blk64 = sb([128, 128], BF16, "blk64"); blk32 = sb([128, 128], BF16, "blk32")
        ones256 = sb([128, 128], BF16, "o256"); ones512 = sb([128, 128], BF16, "o512")
        ident_f = sb([128, 128], F32, "identf")
        sel_f = sb([128, 64], F32, "sel")
        epsc = sb([128, 1], F32, "eps")
        mhalf = sb([128, 1], F32, "mhalf")
        hbglu = sb([128, 4], F32, "hbglu")
        WinD = nc.dram_tensor("WinD", [128, 14, 1024], BF16).ap(); WinD_b = Buf("WinD")
        WinT = nc.dram_tensor("WinT", [128, 8 * 160], BF16).ap(); WinT_b = Buf("WinT")
        PIECE_COL0 = [0, 128, 256, 384, 512, 640, 768, 896, 1440, 1568, 1696, 1824, 1024, 1152]
        Wuq = sb([128, 2, 1024], BF16, "Wuq")
        Wukv = sb([128, 1024], BF16, "Wukv")
        Wglu = sb([128, 4, 512], BF16, "Wglu")
        Wout = sb([128, 8, 1024], BF16, "Wout")
        bglu = sb([128, 4], F32, "bglu")
        gc = sb([128, 24], F32, "gcols")
        gr = sb([128, 160], F32, "grows")
        Kbd = sb([128, 4, 8, 128], BF16, "Kbd")
        Wb = sb([128, 4, 8, 2, 128], BF16, "Wb")
        Vd = sb([128, 16, 2, 8, 32], BF16, "Vd")
        A8 = sb([128, 2, 16], F32, "A8")
        cosF = sb([128, 2048], BF16, "cosF"); sinF = sb([128, 2048], BF16, "sinF")
        cosT = sb([128, 17, 32], F32, "cosT"); sinT = sb([128, 17, 32], F32, "sinT")

        sA = ExitStack()
        with ExitStack() as s0:
            def sb0(shape, dt=F32, name=None):
                return sb(shape, dt, name, stack=sA)

            cmf = sb0([128, 5, 128], F32, "cmf")
            P.dma(cmf.t[:], cm[:, :, :], [], [cmf.b])
            for i, dst in enumerate((ident_b, blk64, blk32, ones256, ones512)):
                cp("dve", dst.t[:], cmf.t[:, i, :], [cmf.b], [dst.b])
            cp("dve", ident_f.t[:], cmf.t[:, 0, :], [cmf.b], [ident_f.b])
            P.dma(sel_f.t[:], sel[:, :], [], [sel_f.b])
            P.op("pool", lambda g: g.memset(epsc.t[:], EPS), [], [epsc.b])
            P.op("pool", lambda g: g.memset(mhalf.t[:], -0.5), [], [mhalf.b])
            P.dma(bglu.t[:], b_glu[:, :], [], [bglu.b])
            ts("dve", hbglu.t[:], bglu.t[:], 0.5, None, ALU.mult, None, [bglu.b], [hbglu.b])
            P.dma(gc.t[:], gcols[:, :], [], [gc.b])
            P.dma(gr.t[:], grows[:, :], [], [gr.b])
            ts("dve", gc.t[:, 0:6], gc.t[:, 0:6], float(96 ** -0.5), None, ALU.mult, None, [gc.b], [gc.b])
            ts("dve", gc.t[:, 16:18], gc.t[:, 16:18], float(96 ** -0.5), None, ALU.mult, None, [gc.b], [gc.b])

            pre_a1 = sb0([128, 3, 512], F32, "a1"); P.dma(pre_a1.t[:], a1[:, :, :], [], [pre_a1.b])
            pre_bt1 = sb0([128, 2, 512], F32, "bt1"); P.dma(pre_bt1.t[:], bt1[:, :, :], [], [pre_bt1.b])
        scope_holder = [None]

        def sb0(shape, dt=F32, name=None):
            return sb(shape, dt, name, stack=scope_holder[0])

        if True:
            def cgen(aa, F_, npow, tag):
                e = "dve"
                dt_ = sb0([128, F_], F32, tag + "dt"); act(dt_.t[:], aa.t[:, 2, :], AF.Exp, [aa.b], [dt_.b])
                mag = sb0([128, F_], F32, tag + "mag"); th = sb0([128, F_], F32, tag + "th")
                tt(e, mag.t[:], aa.t[:, 0, :], dt_.t[:], ALU.mult, [aa.b, dt_.b], [mag.b])
                act(mag.t[:], mag.t[:], AF.Exp, [mag.b], [mag.b])
                tt(e, th.t[:], aa.t[:, 1, :], dt_.t[:], ALU.mult, [aa.b, dt_.b], [th.b])
                cr = sb0([128, F_], F32, tag + "cr"); ci = sb0([128, F_], F32, tag + "ci")
                t1 = sb0([128, F_], F32, tag + "t1"); t2 = sb0([128, F_], F32, tag + "t2")
                hp = sb0([128, 1], F32, tag + "hp"); P.op("pool", lambda g: g.memset(hp.t[:], PI / 2), [], [hp.b])
                act(ci.t[:], th.t[:], AF.Sin, [th.b], [ci.b], scale=1.0 / 64)
                act(cr.t[:], th.t[:], AF.Sin, [th.b, hp.b], [cr.b], scale=1.0 / 64, bias=hp.t[:, 0:1])
                for _ in range(6):
                    tt(e, t1.t[:], cr.t[:], cr.t[:], ALU.mult, [cr.b], [t1.b])
                    tt(e, t2.t[:], ci.t[:], ci.t[:], ALU.mult, [ci.b], [t2.b])
                    tt(e, ci.t[:], cr.t[:], ci.t[:], ALU.mult, [cr.b, ci.b], [ci.b])
                    ts(e, ci.t[:], ci.t[:], 2.0, None, ALU.mult, None, [ci.b], [ci.b])
                    tt(e, cr.t[:], t1.t[:], t2.t[:], ALU.subtract, [t1.b, t2.b], [cr.b])
                pw = sb0([128, 2, npow + 1, F_], F32, tag + "pw")
                P.op("pool", lambda g: g.memset(pw.t[:, 0, 0, :], 1.0), [], [pw.b])
                P.op("pool", lambda g: g.memset(pw.t[:, 1, 0, :], 0.0), [], [pw.b])
                tt(e, pw.t[:, 0, 1, :], cr.t[:], mag.t[:], ALU.mult, [cr.b, mag.b], [pw.b])
                tt(e, pw.t[:, 1, 1, :], ci.t[:], mag.t[:], ALU.mult, [ci.b, mag.b], [pw.b])
                for m in range(2, npow + 1):
                    cmul(pw.t[:, 0, m, :], pw.t[:, 1, m, :], pw.t[:, 0, m - 1, :], pw.t[:, 1, m - 1, :], pw.t[:, 0, 1, :], pw.t[:, 1, 1, :],
                         [pw.b], [pw.b], t1, t2)
                x_ = sb0([128, F_], F32, tag + "x"); den = sb0([128, F_], F32, tag + "den")
                cf = sb0([128, 2, F_], F32, tag + "cf")
                ts(e, x_.t[:], pw.t[:, 0, 1, :], -1.0, None, ALU.add, None, [pw.b], [x_.b])
                tt(e, den.t[:], aa.t[:, 0, :], aa.t[:, 0, :], ALU.mult, [aa.b], [den.b])
                tt(e, t1.t[:], aa.t[:, 1, :], aa.t[:, 1, :], ALU.mult, [aa.b], [t1.b])
                tt(e, den.t[:], den.t[:], t1.t[:], ALU.add, [den.b, t1.b], [den.b])
                P.op(e, lambda g: g.reciprocal(out=den.t[:], in_=den.t[:]), [den.b], [den.b])
                tt(e, t1.t[:], x_.t[:], aa.t[:, 0, :], ALU.mult, [x_.b, aa.b], [t1.b])
                tt(e, t2.t[:], pw.t[:, 1, 1, :], aa.t[:, 1, :], ALU.mult, [pw.b, aa.b], [t2.b])
                tt(e, t1.t[:], t1.t[:], t2.t[:], ALU.add, [t1.b, t2.b], [t1.b])
                tt(e, cf.t[:, 0, :], t1.t[:], den.t[:], ALU.mult, [t1.b, den.b], [cf.b])
                tt(e, t1.t[:], pw.t[:, 1, 1, :], aa.t[:, 0, :], ALU.mult, [pw.b, aa.b], [t1.b])
                tt(e, t2.t[:], x_.t[:], aa.t[:, 1, :], ALU.mult, [x_.b, aa.b], [t2.b])
                tt(e, t1.t[:], t1.t[:], t2.t[:], ALU.subtract, [t1.b, t2.b], [t1.b])
                tt(e, cf.t[:, 1, :], t1.t[:], den.t[:], ALU.mult, [t1.b, den.b], [cf.b])
                return pw, cf

            def cmul(or_, oi_, ar_, ai_, br_, bi_, R, W, t1, t2, e="dve", neg_im=False):
                sh = tuple(or_.shape)
                a1_ = _view(t1, sh); a2_ = _view(t2, sh)
                tt(e, a1_, ar_, br_, ALU.mult, R, [t1.b])
                tt(e, a2_, ai_, bi_, ALU.mult, R, [t2.b])
                tt(e, or_, a1_, a2_, ALU.subtract, [t1.b, t2.b], W)
                tt(e, a1_, ar_, bi_, ALU.mult, R, [t1.b])
                tt(e, a2_, ai_, br_, ALU.mult, R, [t2.b])
                if neg_im:
                    tt(e, a1_, a1_, a2_, ALU.add, [t1.b, t2.b], [t1.b])
                    ts(e, oi_, a1_, -1.0, None, ALU.mult, None, [t1.b], W)
                else:
                    tt(e, oi_, a1_, a2_, ALU.add, [t1.b, t2.b], W)

            def _view(t, sh):
                n = 1
                for s_ in sh[1:]:
                    n *= s_
                flat = t.t[:, 0:n]
                if len(sh) == 2:
                    return flat
                if len(sh) == 3:
                    return flat.rearrange("p (a b) -> p a b", a=sh[1])
                return flat.rearrange("p (a b c) -> p a b c", a=sh[1], b=sh[2])

        with ExitStack() as s1:
            scope_holder[0] = s1
            a1t = pre_a1
            bt1t = pre_bt1
            pw1, cf1 = cgen(a1t, 512, 7, "g1")
            T1 = sb0([128, 512], F32, "T1"); T2 = sb0([128, 512], F32, "T2")
            bb1 = sb0([128, 2, 512], F32, "bb1")
            cmul(bb1.t[:, 0, :], bb1.t[:, 1, :], cf1.t[:, 0, :], cf1.t[:, 1, :], bt1t.t[:, 0, :], bt1t.t[:, 1, :], [cf1.b, bt1t.b], [bb1.b], T1, T2)
            wtmp = sb0([128, 2, 4, 128], F32, "wtmp")
            for tau in range(8):
                m = 7 - tau
                cmul(wtmp.t[:, 0, :, :].rearrange("p a b -> p (a b)"), wtmp.t[:, 1, :, :].rearrange("p a b -> p (a b)"),
                     pw1.t[:, 0, m, :], pw1.t[:, 1, m, :], bb1.t[:, 0, :], bb1.t[:, 1, :], [pw1.b, bb1.b], [wtmp.b], T1, T2)
                for ri in range(2):
                    cp("act", Wb.t[:, :, tau, ri, :], wtmp.t[:, ri, :, :], [wtmp.b], [Wb.b])

            P.barrier()
        P.barrier()
        sA.close()
        with ExitStack() as s2:
            scope_holder[0] = s2
            a2t = sb0([128, 3, 16], F32, "a2"); P.dma(a2t.t[:], a2[:, :, :], [], [a2t.b])
            b2t = sb0([128, 2, 16, 128], F32, "b2"); P.dma(b2t.t[:], b2[:, :, :, :], [], [b2t.b])
            c2t = sb0([128, 2, 16, 32], F32, "c2"); P.dma(c2t.t[:], c2[:, :, :, :], [], [c2t.b])
            pw2, cf2 = cgen(a2t, 16, 8, "g2")
            ninc = sb0([128, 8], F32, "nin"); nqc = sb0([128, 2], F32, "nq"); noutc = sb0([128, 8], F32, "nout")
            P.dma(ninc.t[:], nin[:, :], [], [ninc.b]); P.dma(nqc.t[:], nq[:, :], [], [nqc.b]); P.dma(noutc.t[:], nout[:, :], [], [noutc.b])
            ts("dve", noutc.t[:, 0:4], noutc.t[:, 0:4], 0.125, None, ALU.mult, None, [noutc.b], [noutc.b])
            ts("dve", noutc.t[:, 4:8], noutc.t[:, 4:8], 0.5, None, ALU.mult, None, [noutc.b], [noutc.b])
            stg = [sb0([128, 2048], F32, "stg") for _ in range(2)]
            si = [0]

            def load_w(dst_ap, src_ap, ncol, gain_ap, e):
                s_ = stg[si[0] % 2]; si[0] += 1
                P.dma(s_.t[:, 0:ncol], src_ap, [], [s_.b])
                if gain_ap is None:
                    cp(e, dst_ap, s_.t[:, 0:ncol], [s_.b], [dstb[0]])
                elif e == "act":
                    act(dst_ap, s_.t[:, 0:ncol], AF.Copy, [s_.b, gainb[0]], [dstb[0]], scale=gain_ap)
                else:
                    ts(e, dst_ap, s_.t[:, 0:ncol], gain_ap, None, ALU.mult, None, [s_.b, gainb[0]], [dstb[0]])

            wst = [sb0([128, 8, 160], F32, "wst") for _ in range(2)]
            wbf = [sb0([128, 8, 160], BF16, "wbf") for _ in range(2)]
            for pi_, c0_ in enumerate(PIECE_COL0 + [1280]):
                w_ = 160 if pi_ == 14 else 128
                a_ = wst[pi_ % 2]; b_ = wbf[pi_ % 2]
                P.dma(a_.t[:, :, 0:w_], w_in[:, :, c0_:c0_ + w_], [], [a_.b])
                for d_ in range(8):
                    act(b_.t[:, d_, 0:w_], a_.t[:, d_, 0:w_], AF.Copy, [a_.b, ninc.b], [b_.b], scale=ninc.t[:, d_:d_ + 1])
                if pi_ < 14:
                    P.dma(WinD[:, pi_, :].rearrange("p (a b) -> p a b", a=8), b_.t[:, :, 0:128], [b_.b], [WinD_b])
                else:
                    P.dma(WinT[:, :].rearrange("p (a b) -> p a b", a=8), b_.t[:, :, 0:160], [b_.b], [WinT_b])
            dstb = [Wuq.b]; gainb = [nqc.b]
            for kt in range(2):
                load_w(Wuq.t[:, kt, :], w_uq[:, kt, :], 1024, nqc.t[:, kt:kt + 1], "act")
            dstb = [Wukv.b]
            load_w(Wukv.t[:, :], w_ukv[:, :], 1024, None, "act")
            dstb = [Wglu.b]
            load_w(Wglu.t[:, :, :].rearrange("p a b -> p (a b)"), w_glu[:, :, :].rearrange("p a b -> p (a b)"), 2048, None, "act")
            dstb = [Wout.b]; gainb = [noutc.b]
            for kt in range(8):
                load_w(Wout.t[:, kt, :], w_out[:, kt, :], 1024, noutc.t[:, kt:kt + 1], "act")
            cp("dve", A8.t[:, 0, :], pw2.t[:, 0, 8, :], [pw2.b], [A8.b])
            cp("dve", A8.t[:, 1, :], pw2.t[:, 1, 8, :], [pw2.b], [A8.b])
            U1 = sb0([128, 2048], F32, "U1"); U2 = sb0([128, 2048], F32, "U2")
            VdF = sb0([128, 16, 2, 9, 32], F32, "VdF")
            for m in range(9):
                cmul(VdF.t[:, :, 0, m, :], VdF.t[:, :, 1, m, :],
                     c2t.t[:, 0, :, :], c2t.t[:, 1, :, :],
                     pw2.t[:, 0, m, :].unsqueeze(2).to_broadcast([128, 16, 32]), pw2.t[:, 1, m, :].unsqueeze(2).to_broadcast([128, 16, 32]),
                     [c2t.b, pw2.b], [VdF.b], U1, U2, neg_im=True)
            for ri in range(2):
                for pr in range(16):
                    cp("act" if pr % 2 else "pool", Vd.t[:, pr, ri, :, :], VdF.t[:, pr, ri, 1:9, :], [VdF.b], [Vd.b])
            bb2 = sb0([128, 2, 16, 128], F32, "bb2")
            cmul(bb2.t[:, 0, :, :], bb2.t[:, 1, :, :],
                 cf2.t[:, 0, :].unsqueeze(2).to_broadcast([128, 16, 128]), cf2.t[:, 1, :].unsqueeze(2).to_broadcast([128, 16, 128]),
                 b2t.t[:, 0, :, :], b2t.t[:, 1, :, :], [cf2.b, b2t.b], [bb2.b], U1, U2)
            for ct in range(4):
                for lg in range(2):
                    kp = nextF()
                    for p4 in range(4):
                        pr = ct * 4 + p4
                        for ri in range(2):
                            mm(kp.t[:, 128 * p4:128 * p4 + 128], bb2.t[:, ri, pr, :], VdF.t[:, pr, ri, 4 * lg:4 * lg + 4, :],
                               ri == 0, ri == 1, [bb2.b, VdF.b], [kp.b])
                    kv4 = kp.t[:, :].rearrange("p (a l c) -> p a l c", a=4, l=4)
                    if lg == 0:
                        stt(Kbd.t[:, ct, 0, :].rearrange("p (a c) -> p a c", a=4), ident_f.t[:].rearrange("p (a c) -> p a c", a=4),
                            gc.t[:, 8 + ct:9 + ct], kv4[:, :, 0, :], ALU.mult, ALU.add, [ident_f.b, gc.b, kp.b], [Kbd.b])
                        for l_ in range(1, 4):
                            cp("dve", Kbd.t[:, ct, l_, :].rearrange("p (a c) -> p a c", a=4), kv4[:, :, l_, :], [kp.b], [Kbd.b])
                    else:
                        for l_ in range(4):
                            cp("dve", Kbd.t[:, ct, 4 + l_, :].rearrange("p (a c) -> p a c", a=4), kv4[:, :, l_, :], [kp.b], [Kbd.b])
            P.barrier()
        P.barrier()
        with ExitStack() as s3:
            scope_holder[0] = s3
            T1 = sb0([128, 1024], F32, "T1"); T2 = sb0([128, 1024], F32, "T2")
            cF = sb0([128, 2048], F32, "cF"); sF = sb0([128, 2048], F32, "sF")
            inv = sb0([128, 1], F32, "inv"); wv = sb0([128, 4], F32, "wv"); hp2 = sb0([128, 1], F32, "hp2")
            P.op("pool", lambda g: g.memset(hp2.t[:], PI / 2), [], [hp2.b])
            act(inv.t[:], gc.t[:, 7:8], AF.Exp, [gc.b], [inv.b], scale=float(-np.log(10000.0) / 16))
            act(wv.t[:, 1:2], inv.t[:], AF.Sin, [inv.b], [wv.b])
            act(wv.t[:, 0:1], inv.t[:], AF.Sin, [inv.b, hp2.b], [wv.b], bias=hp2.t[:, 0:1])
            P.op("pool", lambda g: g.memset(cF.t[:, 0:1], 1.0), [], [cF.b])
            P.op("pool", lambda g: g.memset(sF.t[:, 0:1], 0.0), [], [sF.b])
            for k in range(11):
                n = 1 << k
                ts("dve", T1.t[:, 0:n], sF.t[:, 0:n], wv.t[:, 1:2], None, ALU.mult, None, [sF.b, wv.b], [T1.b])
                ts("dve", T2.t[:, 0:n], cF.t[:, 0:n], wv.t[:, 1:2], None, ALU.mult, None, [cF.b, wv.b], [T2.b])
                stt(cF.t[:, n:2 * n], cF.t[:, 0:n], wv.t[:, 0:1], T1.t[:, 0:n], ALU.mult, ALU.subtract, [cF.b, wv.b, T1.b], [cF.b])
                stt(sF.t[:, n:2 * n], sF.t[:, 0:n], wv.t[:, 0:1], T2.t[:, 0:n], ALU.mult, ALU.add, [sF.b, wv.b, T2.b], [sF.b])
                tt("dve", wv.t[:, 2:3], wv.t[:, 0:1], wv.t[:, 0:1], ALU.mult, [wv.b], [wv.b])
                tt("dve", wv.t[:, 3:4], wv.t[:, 1:2], wv.t[:, 1:2], ALU.mult, [wv.b], [wv.b])
                tt("dve", wv.t[:, 1:2], wv.t[:, 0:1], wv.t[:, 1:2], ALU.mult, [wv.b], [wv.b])
                ts("dve", wv.t[:, 1:2], wv.t[:, 1:2], 2.0, None, ALU.mult, None, [wv.b], [wv.b])
                tt("dve", wv.t[:, 0:1], wv.t[:, 2:3], wv.t[:, 3:4], ALU.subtract, [wv.b], [wv.b])
            ts("dve", sF.t[:], sF.t[:], gc.t[:, 6:7], None, ALU.mult, None, [sF.b, gc.b], [sF.b])
            cp("act", cosF.t[:], cF.t[:], [cF.b], [cosF.b])
            cp("act", sinF.t[:], sF.t[:], [sF.b], [sinF.b])
            for src, dst in ((cF, cosT), (sF, sinT)):
                for t_ in range(17):
                    pt = nextF()
                    if t_ < 16:
                        in_ap = src.t[:, 128 * t_:128 * t_ + 128]
                        rb_ = src.b
                    else:
                        cp("dve", T1.t[:, 0:128].rearrange("p (a b) -> p a b", a=4), src.t[:, 1024:1056].unsqueeze(1).to_broadcast([128, 4, 32]), [src.b], [T1.b])
                        in_ap = T1.t[:, 0:128]
                        rb_ = T1.b
                    P.op("pe", lambda g: g.transpose(out=pt.t[:, 0:128], in_=in_ap, identity=ident_f.t[:]), [rb_, ident_f.b], [pt.b])
                    cp("dve", dst.t[:, t_, :], pt.t[:, 0:32], [pt.b], [dst.b])
            P.barrier()
        P.barrier()

        KTn = sb([128, 4, SEQ + 0], BF16, "KTn"); KTr = sb([128, SEQ], BF16, "KTr")
        Vc = sb([128, 16, 8, 65], BF16, "Vc")
        KTn_b = [Buf("KTn%d" % j_) for j_ in range(16)]; KTr_b = [Buf("KTr%d" % j_) for j_ in range(16)]; Vc_b = [Buf("Vc%d" % j_) for j_ in range(16)]
        P.op("pool", lambda g: g.memset(Vc.t[:, :, :, 64:65], 1.0), [], Vc_b)
        Hst = sb([128, 2, 16], F32, "Hst")
        xin = [sb([128, D], F32, "xin") for _ in range(2)]
        xslot = [0]
        ectr = [0]
        wctr = [0]

        def tile_pass(N, xsrc, yout, ckv_out, kr_out, pos0, tabidx0, sample, seq_first, seq_last, seqidx):
            NS = N // 128
            NC = N // L
            stage[0] = 'F'
            tidx[0] += 1
            xt = []
            for s_ in range(NS):
                x_ = xin[xslot[0] % 2]; xslot[0] += 1
                P.dma(x_.t[:], xsrc[128 * s_:128 * s_ + 128, :], [], [x_.b])
                xt.append(x_)
            ss = sb_t("ss", [128, 4], F32); junk = sb_t("xn", [128, D], BF16)
            for s_ in range(NS):
                act(junk.t[:], xt[s_].t[:], AF.Square, [xt[s_].b], [junk.b, ss.b], accum_out=ss.t[:, s_:s_ + 1])
            rs = sb_t("rs", [128, 4], F32)
            rsqrt_pow(rs.t[:, 0:NS], ss.t[:, 0:NS], [ss.b], [rs.b], scale=1.0 / D)
            hT = sb_t("hT", [128, 8, NT], BF16)
            xn = sb_t("xn", [128, D], BF16)
            for s_ in range(NS):
                ts("dve", xn.t[:], xt[s_].t[:], rs.t[:, s_:s_ + 1], None, ALU.mult, None, [xt[s_].b, rs.b], [xn.b])
                pb = nextB()
                with P.grp("pe"):
                    for d_ in range(8):
                        tr(pb.t[:, 128 * d_:128 * d_ + 128], xn.t[:, 128 * d_:128 * d_ + 128], ident_b.t[:], [xn.b, ident_b.b], [pb.b], inc=(d_ == 7))
                cp("act", hT.t[:, :, 128 * s_:128 * s_ + 128], pb.t[:, :].rearrange("p (a b) -> p a b", a=8), [pb.b], [hT.b])

            def proj_fm(col0, M):
                ps = nextF()
                wctr[0] += 1
                wp = sb_t("wp%d" % (wctr[0] % 3), [128, 8, 128], BF16)
                P.dma(wp.t[:], WinD[:, PIECE_COL0.index(col0), :].rearrange("p (a b) -> p a b", a=8), [WinD_b], [wp.b], nbytes=262144)
                with P.grp("pe"):
                    for d_ in range(8):
                        mm(ps.t[0:M, 0:N], wp.t[:, d_, 0:M], hT.t[:, d_, 0:N], d_ == 0, d_ == 7, [wp.b, hT.b], [ps.b])
                return ps

            uT = sb_t("uT", [128, 4, NT], BF16)
            ubd = sb_t("ubd", [128, 4, 4, NT], BF16)
            sg = sb_t("sg", [128, 4, NT], BF16)
            sgm = sb_t("sgm", [128, 4, NT], BF16)
            cq = sb_t("cq", [128, 2, NT], BF16)
            for i in range(4):
                ps = proj_fm(128 * i, 128)
                cp("dve", uT.t[:, i, 0:N], ps.t[:, 0:N], [ps.b], [uT.b])
                for k_ in range(4):
                    ts("dve", ubd.t[:, i, k_, 0:N], ps.t[:, 0:N], gc.t[:, 18 + k_:19 + k_], None, ALU.mult, None, [ps.b, gc.b], [ubd.b])
            for i in range(4):
                ps = proj_fm(512 + 128 * i, 128)
                th_ = sb_t("jk", [128, 160], F32)
                act(th_.t[:, 0:N], ps.t[:, 0:N], AF.Tanh, [ps.b], [th_.b], scale=0.5)
                stt(sg.t[:, i, 0:N], th_.t[:, 0:N], 1.0, ps.t[:, 0:N], ALU.add, ALU.mult, [th_.b, ps.b], [sg.b])
            for i in range(4):
                ps = proj_fm(1440 + 128 * i, 128)
                th_ = sb_t("jk", [128, 160], F32)
                act(th_.t[:, 0:N], ps.t[:, 0:N], AF.Tanh, [ps.b], [th_.b], scale=0.5)
                stt(sgm.t[:, i, 0:N], th_.t[:, 0:N], 1.0, ps.t[:, 0:N], ALU.add, ALU.mult, [th_.b, ps.b], [sgm.b])
            for i in range(2):
                ps = proj_fm(1024 + 128 * i, 128)
                cp("dve", cq.t[:, i, 0:N], ps.t[:, 0:N], [ps.b], [cq.b])
            ckvT = sb_t("ckvT", [128, NT], BF16)
            krT = sb_t("krT", [128, NT], BF16)
            for s_ in range(NS):
                ps = nextF()
                wt_ = sb_t("wtm", [128, 8, 160], BF16)
                if s_ == 0:
                    P.dma(wt_.t[:], WinT[:, :].rearrange("p (a b) -> p a b", a=8), [WinT_b], [wt_.b], nbytes=327680)
                with P.grp("pe"):
                    for d_ in range(8):
                        mm(ps.t[:, 0:160], hT.t[:, d_, 128 * s_:128 * s_ + 128], wt_.t[:, d_, :], d_ == 0, d_ == 7, [wt_.b, hT.b], [ps.b])
                st2 = sb_t("st2", [128, 4], F32); jk = sb_t("jk", [128, 160], F32)
                act(jk.t[:, 0:128], ps.t[:, 0:128], AF.Square, [ps.b], [jk.b, st2.b], accum_out=st2.t[:, 0:1])
                act(jk.t[:, 128:160], ps.t[:, 128:160], AF.Square, [ps.b], [jk.b, st2.b], accum_out=st2.t[:, 1:2])
                rsqrt_pow(st2.t[:, 2:3], st2.t[:, 0:1], [st2.b], [st2.b], scale=1.0 / 128)
                rsqrt_pow(st2.t[:, 3:4], st2.t[:, 1:2], [st2.b], [st2.b], scale=1.0 / 32)
                okv = sb_t("okv", [128, 128], F32); okr = sb_t("okr", [128, 32], F32); kn_ = sb_t("kn_", [128, 32], F32)
                stt(okv.t[:], ps.t[:, 0:128], st2.t[:, 2:3], gr.t[:, 0:128], ALU.mult, ALU.mult, [ps.b, st2.b, gr.b], [okv.b])
                stt(kn_.t[:], ps.t[:, 128:160], st2.t[:, 3:4], gr.t[:, 128:160], ALU.mult, ALU.mult, [ps.b, st2.b, gr.b], [kn_.b])
                ti = tabidx0 + s_
                r1 = sb_t("r1", [128, 32], F32); r2 = sb_t("r2", [128, 32], F32)
                tt("dve", r1.t[:], kn_.t[:], cosT.t[:, ti, :], ALU.mult, [kn_.b, cosT.b], [r1.b])
                tt("dve", r2.t[:, 0:16], kn_.t[:, 16:32], sinT.t[:, ti, 0:16], ALU.mult, [kn_.b, sinT.b], [r2.b])
                tt("dve", r2.t[:, 16:32], kn_.t[:, 0:16], sinT.t[:, ti, 16:32], ALU.mult, [kn_.b, sinT.b], [r2.b])
                tt("dve", okr.t[:], r1.t[:], r2.t[:], ALU.add, [r1.b, r2.b], [okr.b])
                P.dma(ckv_out[128 * s_:128 * s_ + 128, :], okv.t[:], [okv.b], [], is_output=True)
                P.dma(kr_out[128 * s_:128 * s_ + 128, :], okr.t[:], [okr.b], [], is_output=True)
                tb = sb_t("tb", [128, 256], BF16)
                cp("dve", tb.t[:, 0:128], okv.t[:], [okv.b], [tb.b])
                cp("dve", tb.t[:, 128:256].rearrange("p (a b) -> p a b", a=4), okr.t[:, :].unsqueeze(1).to_broadcast([128, 4, 32]), [okr.b], [tb.b])
                pb = nextB()
                with P.grp("pe"):
                    tr(pb.t[:, 0:128], tb.t[:, 0:128], ident_b.t[:], [tb.b, ident_b.b], [pb.b], inc=False)
                    tr(pb.t[:, 128:256], tb.t[:, 128:256], ident_b.t[:], [tb.b, ident_b.b], [pb.b])
                cp("act", ckvT.t[:, 128 * s_:128 * s_ + 128], pb.t[:, 0:128], [pb.b], [ckvT.b])
                cp("act", krT.t[:, 128 * s_:128 * s_ + 128], pb.t[:, 128:256], [pb.b], [krT.b])

            if flags.get('upto', 9) < 2:
                return
            stage[0] = 'B'
            Xs = sb_t("Xs", [128, 2, 16, NT // L], F32)
            Hs = sb_t("Hs", [128, 2, 16, NT // L + 4], F32)
            Hb = sb_t("Hb", [128, 2, 16, NT // L], BF16)
            for ri in range(2):
                xps = [nextF(), nextF()]
                for ct in range(4):
                    ps = xps[ct // 2]
                    c0_ = 4 * NC * (ct % 2)
                    with P.grp("pe"):
                        for tau in range(8):
                            mm(ps.t[:, c0_:c0_ + 4 * NC].rearrange("q (a k) -> q a k", a=4), Wb.t[:, ct, tau, ri, :], ubd.t[:, ct, :, tau:N:L],
                               tau == 0, tau == 7, [Wb.b, ubd.b], [ps.b])
                for hf in range(2):
                    cp("dve" if hf else "act", Xs.t[:, ri, 8 * hf:8 * hf + 8, 0:NC], xps[hf].t[:, 0:8 * NC].rearrange("p (a b) -> p a b", a=8), [xps[hf].b], [Xs.b])
            if flags.get('s3', 9) < 2:
                return
            SCAN_E = flags.get('scan_engine', 'pool')
            M1 = sb_t("M1", [128, 2, 16], F32); M2 = sb_t("M2", [128, 2, 16], F32)
            A8r = A8.t[:, 0, :].unsqueeze(1).to_broadcast([128, 2, 16]); A8i = A8.t[:, 1, :].unsqueeze(1).to_broadcast([128, 2, 16])
            segs = [(0, NC)] if not sample else [(4 * s_, 4) for s_ in range(4)]
            for sgi, (k0, nk) in enumerate(segs):
                base = k0 + sgi
                if sample:
                    h0t = sb_t("h0t", [128, 4, 2, 16], F32)
                    if sgi == 0:
                        P.dma(h0t.t[:, :, 0, :], h0r[:, :, :], [], [h0t.b])
                        P.dma(h0t.t[:, :, 1, :], h0i[:, :, :], [], [h0t.b])
                    cp(SCAN_E, Hs.t[:, :, :, base], h0t.t[:, sgi, :, :], [h0t.b], [Hs.b])
                elif seq_first:
                    mset(Hs.t[:, :, :, base], 0.0, [Hs.b], e=SCAN_E)
                else:
                    cp(SCAN_E, Hs.t[:, :, :, base], Hst.t[:, :, :], [Hst.b], [Hs.b])
                for j in range(nk):
                    hp_ = Hs.t[:, :, :, base + j]; hn_ = Hs.t[:, :, :, base + j + 1]
                    tt(SCAN_E, M1.t[:], hp_, A8r, ALU.mult, [Hs.b, A8.b], [M1.b])
                    tt(SCAN_E, M2.t[:], hp_, A8i, ALU.mult, [Hs.b, A8.b], [M2.b])
                    tt(SCAN_E, M1.t[:], M1.t[:], Xs.t[:, :, :, k0 + j], ALU.add, [M1.b, Xs.b], [M1.b])
                    tt(SCAN_E, Hs.t[:, 0, :, base + j + 1], M1.t[:, 0, :], M2.t[:, 1, :], ALU.subtract, [M1.b, M2.b], [Hs.b])
                    tt(SCAN_E, Hs.t[:, 1, :, base + j + 1], M1.t[:, 1, :], M2.t[:, 0, :], ALU.add, [M1.b, M2.b], [Hs.b])
                cp("dve", Hb.t[:, :, :, k0:k0 + nk], Hs.t[:, :, :, base:base + nk], [Hs.b], [Hb.b])
                if sample:
                    hso = sb_t("hso", [128, 4, 2, 16], F32)
                    cp(SCAN_E, hso.t[:, sgi, :, :], Hs.t[:, :, :, base + nk], [Hs.b], [hso.b])
                    if sgi == 3:
                        P.dma(ost_s[:, :, :, :], hso.t[:], [hso.b], [], is_output=True)
                else:
                    cp(SCAN_E, Hst.t[:, :, :], Hs.t[:, :, :, base + nk], [Hs.b], [Hst.b])
                    if seq_last:
                        hpo = sb_t("hpo", [128, 2, 16], F32)
                        cp(SCAN_E, hpo.t[:], Hst.t[:], [Hst.b], [hpo.b])
                        P.dma(ost_p[:, seqidx, :, :], hpo.t[:], [hpo.b], [], is_output=True)
            if flags.get('s3', 9) < 3:
                return
            yg = sb_t("yg", [128, 4, NT], BF16)
            for ct in range(4):
                yp_ = nextF()
                for tau in range(8):
                    o_ = yp_.t[:, NC * tau:NC * tau + NC]
                    with P.grp("pe"):
                        for lag in range(tau + 1):
                            mm(o_, Kbd.t[:, ct, lag, :], uT.t[:, ct, tau - lag:N:L], lag == 0, False, [Kbd.b, uT.b], [yp_.b], inc=False)
                        for p4 in range(4):
                            pr = 4 * ct + p4
                            for ri in range(2):
                                last = (p4 == 3 and ri == 1)
                                kw = {"tile_position": (0, 96)} if p4 == 3 else {}
                                mm(yp_.t[32 * p4:32 * p4 + 32, NC * tau:NC * tau + NC], Vd.t[:, pr, ri, tau, :], Hb.t[:, ri, pr, 0:NC], False, last,
                                   [Vd.b, Hb.b], [yp_.b], inc=last, **kw)
                g1 = sb_t("rs_ssm", [128, NT], F32); g2 = sb_t("sig", [128, NT], BF16)
                act(g1.t[:, 0:N], yp_.t[:, 0:N], AF.Square, [yp_.b], [g1.b])
                ts("dve", g1.t[:, 0:N], g1.t[:, 0:N], 0.044715, 1.0, ALU.mult, ALU.add, [g1.b], [g1.b])
                tt("dve", g1.t[:, 0:N], g1.t[:, 0:N], yp_.t[:, 0:N], ALU.mult, [g1.b, yp_.b], [g1.b])
                act(g2.t[:, 0:N], g1.t[:, 0:N], AF.Tanh, [g1.b], [g2.b], scale=0.7978845608028654)
                stt(yg.t[:, ct, 0:N].rearrange("p (k t) -> p t k", t=L), g2.t[:, 0:N].rearrange("p (t k) -> p t k", t=L), 1.0,
                    yp_.t[:, 0:N].rearrange("p (t k) -> p t k", t=L), ALU.add, ALU.mult, [g2.b, yp_.b], [yg.b])
            if flags.get('s3', 9) < 5:
                return
            ys_ = sb_t("ys_", [128, 4, NT], BF16)
            sq = sb_t("sq", [128, 4, NT], BF16)
            for co in range(4):
                ps = nextF()
                with P.grp("pe"):
                    for ci_ in range(4):
                        mm(ps.t[:, 0:N], Wglu.t[:, ci_, 128 * co:128 * co + 128], yg.t[:, ci_, 0:N], ci_ == 0, ci_ == 3, [Wglu.b, yg.b], [ps.b])
                sig = sb_t("sig", [128, NT], BF16)
                act(sig.t[:, 0:N], ps.t[:, 0:N], AF.Tanh, [ps.b, hbglu.b], [sig.b], bias=hbglu.t[:, co:co + 1], scale=0.25)
                stt(ys_.t[:, co, 0:N], sig.t[:, 0:N], 1.0, yg.t[:, co, 0:N], ALU.add, ALU.mult, [sig.b, yg.b], [ys_.b])
                act(sq.t[:, co, 0:N], ys_.t[:, co, 0:N], AF.Square, [ys_.b], [sq.b])
            def bc_rstd(sqt, ntile, onesT, name, scale=1.0):
                ps = nextF()
                with P.grp("pe"):
                    for i in range(ntile):
                        mm(ps.t[:, 0:N], onesT.t[:], sqt.t[:, i, 0:N], i == 0, i == ntile - 1, [onesT.b, sqt.b], [ps.b])
                r_ = sb_t(name, [128, NT], F32)
                rsqrt_pow(r_.t[:, 0:N], ps.t[:, 0:N], [ps.b], [r_.b], scale=scale)
                return r_
            rs_ssm = bc_rstd(sq, 4, ones512, "rs_ssm", scale=1.0 / 16)
            mix = sb_t("mix", [128, 8, NT], BF16)
            for ct in range(4):
                tt("dve", ys_.t[:, ct, 0:N], ys_.t[:, ct, 0:N], sg.t[:, ct, 0:N], ALU.mult, [ys_.b, sg.b], [ys_.b])
                tt("dve", mix.t[:, ct, 0:N], ys_.t[:, ct, 0:N], rs_ssm.t[:, 0:N], ALU.mult, [ys_.b, rs_ssm.b], [mix.b])

            if flags.get('upto', 9) < 3:
                return
            stage[0] = 'S4'
            sqq = sb_t("sq4", [128, 4, NT], BF16)
            for i in range(2):
                act(sqq.t[:, i, 0:N], cq.t[:, i, 0:N], AF.Square, [cq.b], [sqq.b])
            rq = bc_rstd(sqq, 2, ones256, "rq")
            rq2 = sb_t("rq2", [128, NT], F32)
            tt("dve", rq2.t[:, 0:N], rq.t[:, 0:N], rq.t[:, 0:N], ALU.mult, [rq.b], [rq2.b])
            QTn = sb_t("QTn", [128, 4, 2 * NT], BF16); QTr = sb_t("QTr", [128, 4, 2 * NT], BF16)

            def headnorm(ps, blk, rq_, rq2_, name):
                s2 = sb_t("hn_s2", [128, NT], BF16)
                act(s2.t[:, 0:N], ps.t[:, 0:N], AF.Square, [ps.b], [s2.b])
                p2 = nextF()
                mm(p2.t[:, 0:N], blk.t[:], s2.t[:, 0:N], True, True, [blk.b, s2.b], [p2.b])
                t_ = sb_t("hn_t", [128, NT], F32)
                if rq_ is not None:
                    tt("dve", t_.t[:, 0:N], p2.t[:, 0:N], rq2_.t[:, 0:N], ALU.mult, [p2.b, rq2_.b], [t_.b])
                    rsqrt_pow(t_.t[:, 0:N], t_.t[:, 0:N], [t_.b], [t_.b])
                else:
                    rsqrt_pow(t_.t[:, 0:N], p2.t[:, 0:N], [p2.b], [t_.b])
                if rq_ is not None:
                    tt("dve", t_.t[:, 0:N], t_.t[:, 0:N], rq_.t[:, 0:N], ALU.mult, [t_.b, rq_.b], [t_.b])
                return t_

            def qproj(col0):
                ps = nextF()
                with P.grp("pe"):
                    for kt in range(2):
                        mm(ps.t[:, 0:N], Wuq.t[:, kt, col0:col0 + 128], cq.t[:, kt, 0:N], kt == 0, kt == 1, [Wuq.b, cq.b], [ps.b])
                return ps

            for i in range(4):
                ps = qproj(128 * i)
                t_ = headnorm(ps, blk64, rq, rq2, "qn")
                stt(QTn.t[:, i, 0:N], ps.t[:, 0:N], gc.t[:, 16:17], t_.t[:, 0:N], ALU.mult, ALU.mult, [ps.b, gc.b, t_.b], [QTn.b])
                stt(QTn.t[:, i, NT:NT + N], ps.t[:, 0:N], gc.t[:, 17:18], t_.t[:, 0:N], ALU.mult, ALU.mult, [ps.b, gc.b, t_.b], [QTn.b])
            if sample:
                cos_q = cosF.t[:, 1024:1056].unsqueeze(1).to_broadcast([128, 4, 32]); sin_q = sinF.t[:, 1024:1056].unsqueeze(1).to_broadcast([128, 4, 32])
                vq = lambda ap: ap.rearrange("p (a b) -> p a b", a=4)
            else:
                cos_q = cosF.t[:, pos0:pos0 + N]; sin_q = sinF.t[:, pos0:pos0 + N]
                vq = lambda ap: ap
            for i in range(2):
                ps = qproj(512 + 128 * i)
                t_ = headnorm(ps, blk32, rq, rq2, "qr")
                qa = sb_t("qa", [128, NT], F32); qb = sb_t("qb", [128, NT], F32)
                stt(qa.t[:, 0:N], ps.t[:, 0:N], gc.t[:, 4:5], t_.t[:, 0:N], ALU.mult, ALU.mult, [ps.b, gc.b, t_.b], [qa.b])
                ps2 = qproj(768 + 128 * i)
                stt(qb.t[:, 0:N], ps2.t[:, 0:N], gc.t[:, 5:6], t_.t[:, 0:N], ALU.mult, ALU.mult, [ps2.b, gc.b, t_.b], [qb.b])
                tt("dve", vq(qa.t[:, 0:N]), vq(qa.t[:, 0:N]), cos_q, ALU.mult, [qa.b, cosF.b], [qa.b])
                tt("dve", vq(qb.t[:, 0:N]), vq(qb.t[:, 0:N]), sin_q, ALU.mult, [qb.b, sinF.b], [qb.b])
                tt("dve", qa.t[:, 0:N], qa.t[:, 0:N], qb.t[:, 0:N], ALU.add, [qa.b, qb.b], [qa.b])
                for k_ in range(4):
                    h_ = 4 * i + k_
                    ts("dve", QTr.t[:, h_ // 2, (h_ % 2) * NT:(h_ % 2) * NT + N], qa.t[:, 0:N], gc.t[:, 18 + k_:19 + k_], None, ALU.mult, None,
                       [qa.b, gc.b], [QTr.b])

            if sample:
                KTn_new = sb_t("KTnn", [128, 4, 128], BF16); KTr_new = krT
                Vn = sb_t("Vn", [32, 4, 8, 65], BF16)
                kdst = lambda i: KTn_new.t[:, i, 0:N]; kdb = KTn_new.b
            else:
                kdst = lambda i: KTn.t[:, i, pos0:pos0 + N]; kdb = KTn_b[pos0 // 128]
                cp("act", KTr.t[:, pos0:pos0 + N], krT.t[:, 0:N], [krT.b], [KTr_b[pos0 // 128]])
            for i in range(4):
                ps = nextF()
                mm(ps.t[:, 0:N], Wukv.t[:, 128 * i:128 * i + 128], ckvT.t[:, 0:N], True, True, [Wukv.b, ckvT.b], [ps.b])
                t_ = headnorm(ps, blk64, None, None, "kn")
                stt(kdst(i), ps.t[:, 0:N], gc.t[:, 12 + i:13 + i], t_.t[:, 0:N], ALU.mult, ALU.mult, [ps.b, gc.b, t_.b], [kdb])
            if sample:
                mset(Vn.t[:, :, :, 64:65], 1.0, [Vn.b])
                for s_ in range(4):
                    ps = nextF()
                    mm(ps.t[0:32, 0:512], ckvT.t[:, 32 * s_:32 * s_ + 32], Wukv.t[:, 512:1024], True, True, [ckvT.b, Wukv.b], [ps.b])
                    cp("act", Vn.t[:, s_, :, 0:64], ps.t[0:32, 0:512].rearrange("p (h v) -> p h v", h=8), [ps.b], [Vn.b])
            else:
                for s_ in range(NS):
                    ps = nextF()
                    mm(ps.t[:, 0:512], ckvT.t[:, 128 * s_:128 * s_ + 128], Wukv.t[:, 512:1024], True, True, [ckvT.b, Wukv.b], [ps.b])
                    cp("act", Vc.t[:, pos0 // 128 + s_, :, 0:64], ps.t[:, 0:512].rearrange("p (h v) -> p h v", h=8), [ps.b], [Vc_b[pos0 // 128 + s_]])

            attn = sb_t("attn", [128, 4, NT], BF16)
            sqa = sb_t("sq4", [128, 4, NT], BF16)

            def qview(Qt, p, c0, n):
                return Qt.t[:, p, :].rearrange("q (c n) -> q c n", c=2)[:, :, c0:c0 + n]

            def scores_pair(ps3, p, kn_ap, kr_ap, c0, n, R, Wb_):
                with P.grp("pe"):
                    mm(ps3, kn_ap, qview(QTn, p, c0, n), True, False, R + [QTn.b], [Wb_])
                    mm(ps3, kr_ap, qview(QTr, p, c0, n), False, True, R + [QTr.b], [Wb_])

            def finish_pair(ops, p, c0, n, stride):
                w_ = stride + n
                osb = sb_t("osb", [65, 2 * NT], F32)
                cp("act", osb.t[:, 0:w_], ops.t[0:65, 0:w_], [ops.b], [osb.b])
                dps = nextF()
                mm(dps.t[0:64, 0:w_], sel_f.t[0:65, :], osb.t[0:65, 0:w_], True, True, [sel_f.b, osb.b], [dps.b])
                rd = sb_t("rd", [64, 2 * NT], F32)
                recip(rd.t[:, 0:w_], dps.t[0:64, 0:w_], [dps.b], [rd.b])
                for c_ in range(2):
                    tt("dve", attn.t[64 * c_:64 * c_ + 64, p, c0:c0 + n], osb.t[0:64, c_ * stride:c_ * stride + n], rd.t[:, c_ * stride:c_ * stride + n],
                       ALU.mult, [osb.b, rd.b], [attn.b])

            if not sample:
                qb0 = pos0 // 128
                for p in range(4):
                    ops = nextO()
                    nj = qb0 + NS
                    for j in range(nj):
                        lo = max(j, qb0) - qb0
                        nq_ = N - 128 * lo
                        sp_ = nextF()
                        sp3 = sp_.t[:, 0:2 * nq_].rearrange("q (c n) -> q c n", c=2)
                        scores_pair(sp3, p, KTn.t[:, p, 128 * j:128 * j + 128], KTr.t[:, 128 * j:128 * j + 128], 128 * lo, nq_, [KTn_b[j], KTr_b[j]], sp_.b)
                        ectr[0] += 1
                        E = sb_t("E%d" % (ectr[0] % 3), [128, 2 * NT], BF16)
                        E3 = E.t[:, 0:2 * nq_].rearrange("q (c n) -> q c n", c=2)
                        act(E3, sp3, AF.Exp, [sp_.b], [E.b])
                        if j >= qb0:
                            mset(E.t[64:128, 0:2 * nq_].rearrange("q (c n) -> q c n", c=2)[:, :, 0:64], 0.0, [E.b], e="dve")
                        for c_ in range(2):
                            mm(ops.t[0:65, c_ * NT + 128 * lo:c_ * NT + N], Vc.t[:, j, 2 * p + c_, :], E.t[:, c_ * nq_:(c_ + 1) * nq_], j == 0 and c_ == 0,
                               j == nj - 1, [Vc_b[j], E.b], [ops.b], inc=True)
                    finish_pair(ops, p, 0, N, NT)
            else:
                for s_ in range(4):
                    prep_cache(s_)
                    for p in range(4):
                        ops = nextO()
                        for j in range(9):
                            sp_ = nextF()
                            ectr[0] += 1
                            E = sb_t("E%d" % (ectr[0] % 3), [128, 2 * NT], BF16)
                            if j < 8:
                                sp3 = sp_.t[:, 0:64].rearrange("q (c n) -> q c n", c=2)
                                jj = 8 * (s_ % 2) + j
                                scores_pair(sp3, p, KTc[s_].t[:, p, 128 * jj:128 * jj + 128], KRc[s_].t[:, 128 * jj:128 * jj + 128], 32 * s_, 32, [KTn_b[jj], KTr_b[jj]], sp_.b)
                                act(E.t[:, 0:64], sp_.t[:, 0:64], AF.Exp, [sp_.b], [E.b])
                                for c_ in range(2):
                                    mm(ops.t[0:65, 32 * c_:32 * c_ + 32], Vcc[s_].t[:, jj, 2 * p + c_, :], E.t[:, 32 * c_:32 * c_ + 32], j == 0 and c_ == 0, False, [Vc_b[jj], E.b], [ops.b], inc=True)
                            else:
                                sp3 = sp_.t[0:32, 0:64].rearrange("q (c n) -> q c n", c=2)
                                scores_pair(sp3, p, KTn_new.t[:, p, 32 * s_:32 * s_ + 32], KTr_new.t[:, 32 * s_:32 * s_ + 32], 32 * s_, 32, [KTn_new.b, KTr_new.b], sp_.b)
                                act(E.t[0:32, 0:64], sp_.t[0:32, 0:64], AF.Exp, [sp_.b], [E.b])
                                for c_ in range(2):
                                    mm(ops.t[0:65, 32 * c_:32 * c_ + 32], Vn.t[0:32, s_, 2 * p + c_, :], E.t[0:32, 32 * c_:32 * c_ + 32], False, True, [Vn.b, E.b], [ops.b], inc=True)
                        finish_pair(ops, p, 32 * s_, 32, 32)
            for i in range(4):
                act(sqa.t[:, i, 0:N], attn.t[:, i, 0:N], AF.Square, [attn.b], [sqa.b])
            rs_mla = bc_rstd(sqa, 4, ones512, "rs_mla")
            for i in range(4):
                tt("dve", attn.t[:, i, 0:N], attn.t[:, i, 0:N], sgm.t[:, i, 0:N], ALU.mult, [attn.b, sgm.b], [attn.b])
                tt("dve", mix.t[:, 4 + i, 0:N], attn.t[:, i, 0:N], rs_mla.t[:, 0:N], ALU.mult, [attn.b, rs_mla.b], [mix.b])

            if flags.get('upto', 9) < 4:
                return
            stage[0] = 'B'
            for s_ in range(NS):
                for half in range(2):
                    ps = nextF()
                    with P.grp("pe"):
                        for kt in range(8):
                            mm(ps.t[:, 0:512], mix.t[:, kt, 128 * s_:128 * s_ + 128], Wout.t[:, kt, 512 * half:512 * half + 512], kt == 0, kt == 7, [mix.b, Wout.b], [ps.b])
                    yo = sb_t("yo", [128, 512], F32)
                    tt("dve", yo.t[:], ps.t[:, 0:512], xt[s_].t[:, 512 * half:512 * half + 512], ALU.add, [ps.b, xt[s_].b], [yo.b])
                    P.dma(yout[128 * s_:128 * s_ + 128, 512 * half:512 * half + 512], yo.t[:], [yo.b], [], is_output=True)

        pool_tiles = {}

        tidx = [0]
        DOUBLE = set(flags.get("double", ("hT", "uT", "sg", "sgm", "cq", "ckvT", "krT", "st2", "okv", "okr", "kn_", "r1", "r2", "tb", "ss", "rs",
                                          "Xs", "Hs", "Hb", "yg", "ys_", "sig", "rs_ssm", "rq", "rq2", "hn_s2", "hn_t",
                                          "attn", "rs_mla", "M1", "M2")))

        def sb_t(name, shape, dt):
            key = name + ("_%d" % (tidx[0] % 2) if name in DOUBLE else "")
            if key not in pool_tiles:
                pool_tiles[key] = sb(shape, dt, key)
            return pool_tiles[key]

        KTc, KRc, Vcc = [], [], []
        if flags.get("sample", True):
            ktc = KTn; krc = KTr; vcc = Vc
            for s_ in range(4):
                KTc.append(ktc); KRc.append(krc); Vcc.append(vcc)

            def prep_cache(s_):
                o8 = 8 * (s_ % 2)
                cT = sb_t("cT", [128, 1024], BF16)
                for j in range(8):
                    cl = sb_t("cl", [128, 160], F32)
                    P.dma(cl.t[:, 0:128], cckv[s_, 128 * j:128 * j + 128, :], [], [cl.b])
                    P.dma(cl.t[:, 128:160], ckr[s_, 128 * j:128 * j + 128, :], [], [cl.b])
                    tb = sb_t("tb", [128, 256], BF16)
                    cp("dve", tb.t[:, 0:128], cl.t[:, 0:128], [cl.b], [tb.b])
                    cp("dve", tb.t[:, 128:256].rearrange("p (a b) -> p a b", a=4), cl.t[:, 128:160].unsqueeze(1).to_broadcast([128, 4, 32]), [cl.b], [tb.b])
                    pb = nextB()
                    with P.grp("pe"):
                        tr(pb.t[:, 0:128], tb.t[:, 0:128], ident_b.t[:], [tb.b, ident_b.b], [pb.b], inc=False)
                        tr(pb.t[:, 128:256], tb.t[:, 128:256], ident_b.t[:], [tb.b, ident_b.b], [pb.b])
                    cp("act", cT.t[:, 128 * j:128 * j + 128], pb.t[:, 0:128], [pb.b], [cT.b])
                    cp("act", krc.t[:, 128 * (o8 + j):128 * (o8 + j) + 128], pb.t[:, 128:256], [pb.b], [KTr_b[o8 + j]])
                for j in range(8):
                    ps = nextF()
                    mm(ps.t[:, 0:512], cT.t[:, 128 * j:128 * j + 128], Wukv.t[:, 512:1024], True, True, [cT.b, Wukv.b], [ps.b])
                    cp("act", vcc.t[:, o8 + j, :, 0:64], ps.t[:, 0:512].rearrange("p (h v) -> p h v", h=8), [ps.b], [Vc_b[o8 + j]])
                for i in range(4):
                    for hf in range(2):
                        ps = nextF()
                        mm(ps.t[:, 0:512], Wukv.t[:, 128 * i:128 * i + 128], cT.t[:, 512 * hf:512 * hf + 512], True, True, [Wukv.b, cT.b], [ps.b])
                        s2 = sb_t("sq4", [128, 4, NT], BF16)
                        s2v = s2.t[:, :, :].rearrange("p a b -> p (a b)")
                        act(s2v, ps.t[:, 0:512], AF.Square, [ps.b], [s2.b])
                        p2 = nextF()
                        mm(p2.t[:, 0:512], blk64.t[:], s2v, True, True, [blk64.b, s2.b], [p2.b])
                        t_ = sb_t("kt_", [128, 512], F32)
                        rsqrt_pow(t_.t[:], p2.t[:, 0:512], [p2.b], [t_.b])
                        stt(ktc.t[:, i, 128 * o8 + 512 * hf:128 * o8 + 512 * hf + 512], ps.t[:, 0:512], gc.t[:, 12 + i:13 + i], t_.t[:], ALU.mult, ALU.mult, [ps.b, gc.b, t_.b],
                            KTn_b[o8 + 4 * hf:o8 + 4 * hf + 4])

        if flags.get('sched', True):
            P.start_recording()
        if flags.get("sample", True) and flags.get('upto', 9) >= 1:
            tile_pass(128, xs, ys, ockv_s, okr_s, 1024, 16, True, True, True, 0)
        nseq = flags.get("nseq", 2) if flags.get('upto', 9) >= 1 else 0
        ntile = flags.get("ntile", SEQ // NT)
        for sq_ in range(nseq):
            for it in range(ntile):
                r0 = sq_ * SEQ + it * NT
                tile_pass(NT, xp[r0:r0 + NT, :], yp[r0:r0 + NT, :], ockv_p[r0:r0 + NT, :], okr_p[r0:r0 + NT, :],
                          it * NT, (it * NT) // 128, False, it == 0, it == ntile - 1, sq_)
        if flags.get('sched', True):
            P.sched_slack = flags.get('slack', 0.25)
            P.sched_tabpen = flags.get('tabpen', 1.3)
            P.sched_xlat = flags.get('xlat', 0.4)
            P.schedule_and_emit(flags.get('window', 1300))
        P.finish()
    return nc


def _host_weights(inp):
    f = lambda a: np.ascontiguousarray(np.asarray(a, dtype=np.float32))
    W = {}
    W["w_in"] = f(inp["w_in"][0].reshape(8, 128, 1952).transpose(1, 0, 2))
    W["nin"] = f(inp["norm_in"][0].reshape(8, 128).T)
    wuq = np.asarray(inp["w_uq"][0]).reshape(256, 8, 96)
    sw = np.concatenate([np.arange(16, 32), np.arange(0, 16)])
    wq = np.concatenate([wuq[:, :, :64].reshape(256, 512), wuq[:, :, 64:].reshape(256, 256), wuq[:, :, 64:][:, :, sw].reshape(256, 256)], axis=1)
    W["w_uq"] = f(wq.reshape(2, 128, 1024).transpose(1, 0, 2))
    W["nq"] = f(inp["q_lora_norm"][0].reshape(2, 128).T)
    wkv = np.asarray(inp["w_ukv"][0]).reshape(128, 8, 128)
    W["w_ukv"] = f(np.concatenate([wkv[:, :, :64].reshape(128, 512), wkv[:, :, 64:].reshape(128, 512)], axis=1))
    W["w_glu"] = f(inp["w_glu"][0].reshape(4, 128, 512).transpose(1, 0, 2))
    W["b_glu"] = f(inp["b_glu"][0].reshape(4, 128).T)
    W["w_out"] = f(inp["w_out"][0].reshape(8, 128, 1024).transpose(1, 0, 2))
    W["nout"] = f(np.concatenate([inp["out_norm_ssm"][0], inp["out_norm_mla"][0]]).reshape(8, 128).T)
    gcols = np.zeros((128, 24), np.float32)
    qn = np.asarray(inp["q_nope_norm"][0]); qr = np.asarray(inp["q_rope_norm"][0]); kn = np.asarray(inp["k_nope_norm"][0])
    for i in range(4):
        gcols[:, i] = np.tile(qn, 2)
        gcols[:, 12 + i] = np.tile(kn, 2)
        gcols[:, 8 + i] = np.asarray(inp["ssm_d"][0])[128 * i:128 * i + 128]
    gcols[:, 4] = np.tile(qr, 4)
    pidx = np.arange(128)
    gcols[:, 16] = np.tile(qn, 2) * (pidx < 64)
    gcols[:, 17] = np.tile(qn, 2) * (pidx >= 64)
    for k_ in range(4):
        gcols[:, 18 + k_] = (pidx // 32 == k_)
    gcols[:, 5] = np.tile(qr[sw], 4)
    gcols[:, 6] = np.tile(np.concatenate([-np.ones(16), np.ones(16)]), 4)
    gcols[:, 7] = np.tile(np.concatenate([np.arange(16), np.arange(16)]), 4)
    W["gcols"] = gcols
    W["grows"] = f(np.tile(np.concatenate([inp["kv_lora_norm"][0], inp["k_rope_norm"][0]])[None, :], (128, 1)))
    cmx = np.zeros((128, 5, 128), np.float32)
    cmx[:, 0] = np.eye(128)
    p = np.arange(128)
    cmx[:, 1] = (p[:, None] // 64 == p[None, :] // 64) / 64.0
    cmx[:, 2] = (p[:, None] // 32 == p[None, :] // 32) / 32.0
    cmx[:, 3] = 1.0 / 256
    cmx[:, 4] = 1.0 / 512
    W["cm"] = cmx
    sel = np.zeros((128, 64), np.float32); sel[64, :] = 1.0
    W["sel"] = sel
    are = np.asarray(inp["ssm_a_re"][0]); aim = np.asarray(inp["ssm_a_im"][0]); ldt = np.asarray(inp["ssm_log_dt"][0])
    bre = np.asarray(inp["ssm_b_re"][0]); bim = np.asarray(inp["ssm_b_im"][0])
    cre = np.asarray(inp["ssm_c_re"][0]); cim = np.asarray(inp["ssm_c_im"][0])
    def l2(a):
        return a.reshape(16, 2, 64).transpose(1, 2, 0).reshape(128, 16)
    W["a2"] = f(np.stack([l2(are), l2(aim), l2(np.tile(ldt[:, None], (1, 64)))], axis=1))
    def l1(a):
        v = a.reshape(4, 4, 2, 64)
        v = v.transpose(1, 0, 2, 3)
        v = np.broadcast_to(v[:, None, None], (4, 2, 16, 4, 2, 64))
        return v.reshape(128, 512)
    W["a1"] = f(np.stack([l1(are), l1(aim), l1(np.tile(ldt[:, None], (1, 64)))], axis=1))
    def lbt(b):
        v = b.reshape(4, 4, 2, 64, 16)
        out = np.zeros((4, 2, 16, 4, 2, 64), np.float32)
        for g2 in range(2):
            out[:, g2, :, :, g2, :] = v[:, :, g2].transpose(1, 3, 0, 2)
        return out.reshape(128, 512)
    W["bt1"] = f(np.stack([lbt(bre), lbt(bim)], axis=1))
    def lb2(b):
        v = b.reshape(16, 2, 64, 16)
        out = np.zeros((2, 64, 16, 4, 2, 16), np.float32)
        for pr in range(16):
            for g2 in range(2):
                out[g2, :, pr, pr % 4, g2, :] = v[pr, g2]
        return out.reshape(128, 16, 128)
    W["b2"] = f(np.stack([lb2(bre), lb2(bim)], axis=1))
    def lc2(c_):
        v = c_.reshape(16, 2, 16, 64)
        out = np.zeros((2, 64, 16, 2, 16), np.float32)
        for g2 in range(2):
            out[g2, :, :, g2, :] = v[:, g2].transpose(2, 0, 1)
        return out.reshape(128, 16, 32)
    W["c2"] = f(np.stack([lc2(cre), lc2(cim)], axis=1))
    return W


FLAGS = {}


def kernel(**inp):
    inp = {k: np.asarray(v) for k, v in inp.items()}
    W = _host_weights(inp)
    nc = build_program(FLAGS)
    in_maps = []
    for c in range(NCORES):
        m = dict(W)
        m["xp"] = np.ascontiguousarray(inp["x_prompt"][2 * c:2 * c + 2].reshape(2 * SEQ, D))
        m["xs"] = np.ascontiguousarray(inp["x_sample"][4 * c:4 * c + 4].reshape(128, D))
        m["cckv"] = np.ascontiguousarray(inp["cache_ckv"][0, 4 * c:4 * c + 4])
        m["ckr"] = np.ascontiguousarray(inp["cache_krope"][0, 4 * c:4 * c + 4])
        for nm, key in (("h0r", "state_ssm_re"), ("h0i", "state_ssm_im")):
            v = inp[key][0, 4 * c:4 * c + 4].reshape(4, 16, 2, 64)
            m[nm] = np.ascontiguousarray(v.transpose(2, 3, 0, 1).reshape(128, 4, 16))
        in_maps.append(m)
    res = run_bass_kernel_spmd(nc, in_maps, core_ids=list(range(NCORES)))
    R = res.results
    yp = np.concatenate([R[c]["yp"].reshape(2, SEQ, D) for c in range(NCORES)], axis=0)
    ysm = np.concatenate([R[c]["ys"].reshape(4, 32, D) for c in range(NCORES)], axis=0)
    ckv_p = np.concatenate([R[c]["ockv_p"].reshape(2, SEQ, 128) for c in range(NCORES)], axis=0)[None]
    kr_p = np.concatenate([R[c]["okr_p"].reshape(2, SEQ, 32) for c in range(NCORES)], axis=0)[None]
    ckv_s = np.concatenate([R[c]["ockv_s"].reshape(4, 32, 128) for c in range(NCORES)], axis=0)[None]
    kr_s = np.concatenate([R[c]["okr_s"].reshape(4, 32, 32) for c in range(NCORES)], axis=0)[None]

    def unst(a, nseq):
        v = a.reshape(2, 64, nseq, 2, 16).transpose(2, 3, 4, 0, 1)
        v = v.reshape(nseq, 2, 32, 64)
        return v[:, 0], v[:, 1]
    sp_ = [unst(R[c]["ost_p"], 2) for c in range(NCORES)]
    ss_ = [unst(R[c]["ost_s"], 4) for c in range(NCORES)]
    re_p = np.concatenate([a for a, _ in sp_], axis=0)[None]; im_p = np.concatenate([b for _, b in sp_], axis=0)[None]
    re_s = np.concatenate([a for a, _ in ss_], axis=0)[None]; im_s = np.concatenate([b for _, b in ss_], axis=0)[None]
    f = lambda a: np.ascontiguousarray(a, dtype=np.float32)
    return (f(yp), f(ysm), f(ckv_p), f(kr_p), f(re_p), f(im_p), f(ckv_s), f(kr_s), f(re_s), f(im_s))
```
